# Optimizing a Trainium2 kernel written in Bass

```python
import math
import jax
import jax.numpy as jnp
from jax import lax
import numpy as np

D_MODEL = 1024
BATCH = 1
SEQ = 16384
DEPTH = 4

N_BRANCHES = 4
A_HEADS = 4
A_HEAD_DIM = 64
A_WIDTH = A_HEADS * A_HEAD_DIM
A_CHUNK = 64
B_PAIRS = ((128, 1), (512, 4), (2048, 16))
B_GROUPS = len(B_PAIRS)
B_HEADS = 4
B_HEAD_DIM = 64
B_WIDTH = B_HEADS * B_HEAD_DIM
B_QKV_WIDTH = B_GROUPS * B_WIDTH
C_HEADS = 4
C_HEAD_DIM = 64
C_WIDTH = C_HEADS * C_HEAD_DIM
C_CONV = 5
C_CHUNK = 64
D_Q_HEADS = 8
D_KV_HEADS = 2
D_HEAD_DIM = 64
D_Q_WIDTH = D_Q_HEADS * D_HEAD_DIM
D_KV_WIDTH = D_KV_HEADS * D_HEAD_DIM
D_WINDOW = 128
D_BLOCK = 128
ROPE_THETA = 10000.0
D_FF = 2816
N_EXPERTS = 8
TOP_K = 2
D_FF_EXPERT = 1408
MOE_BLOCK = 128
N_DENSE = (DEPTH + 1) // 2
N_MOE = DEPTH // 2
DEEPNORM_ALPHA = (2 * DEPTH) ** 0.25
DEEPNORM_BETA = (8 * DEPTH) ** -0.25
LN_EPS = 1e-5

IN_SEGMENTS = (
    ('gates', N_BRANCHES * D_MODEL),
    ('a_q', A_WIDTH), ('a_k', A_WIDTH), ('a_v', A_WIDTH), ('a_o', A_WIDTH),
    ('a_if', 2 * 2 * A_HEADS),
    ('b_q', B_QKV_WIDTH), ('b_k', B_QKV_WIDTH), ('b_v', B_QKV_WIDTH),
    ('c_qkv', 3 * C_WIDTH), ('c_gate', C_WIDTH), ('c_beta', 2 * C_HEADS), ('c_decay', 2 * C_HEADS),
    ('d_q', D_Q_WIDTH), ('d_k', D_KV_WIDTH), ('d_v', D_KV_WIDTH),
)
N_IN = sum(size for _, size in IN_SEGMENTS)

kernel_name = 'hybrid_gated_mlstm_dilated_deltanet_swa_moe_encoder'

F32 = jnp.float32


def split_columns(proj):
    parts = {}
    off = 0
    for name, size in IN_SEGMENTS:
        parts[name] = proj[..., off:off + size]
        off += size
    return parts


def layer_norm(x, w, b):
    xf = x.astype(F32)
    mu = xf.mean(-1, keepdims=True)
    var = jnp.square(xf - mu).mean(-1, keepdims=True)
    return ((xf - mu) * lax.rsqrt(var + LN_EPS) * w.astype(F32) + b.astype(F32)).astype(x.dtype)


def rope_tables(seq_len, dim):
    inv_freq = ROPE_THETA ** (-jnp.arange(0, dim, 2, dtype=F32) / dim)
    ang = jnp.arange(seq_len, dtype=F32)[:, None] * inv_freq[None, :]
    return jnp.cos(ang), jnp.sin(ang)


def apply_rope(t, cos, sin):
    t = t.astype(F32)
    half = t.shape[-1] // 2
    t1, t2 = t[..., :half], t[..., half:]
    c = cos[None, :, None, :]
    s = sin[None, :, None, :]
    return jnp.concatenate([t1 * c - t2 * s, t2 * c + t1 * s], axis=-1)


def banded_attention(q, k, v, half_window, block, sink=None):
    Bn, T, Hq, Dh = q.shape
    Hkv = k.shape[2]
    G = Hq // Hkv
    nb = -(-T // block)
    Tp = nb * block
    pad = Tp - T
    qb = jnp.pad(q.astype(F32), ((0, 0), (0, pad), (0, 0), (0, 0))).reshape(Bn, nb, block, Hkv, G, Dh)
    kp = jnp.pad(k.astype(F32), ((0, 0), (block, pad + block), (0, 0), (0, 0))).reshape(Bn, nb + 2, block, Hkv, Dh)
    vp = jnp.pad(v.astype(F32), ((0, 0), (block, pad + block), (0, 0), (0, 0))).reshape(Bn, nb + 2, block, Hkv, Dh)
    kw = jnp.concatenate([kp[:, :-2], kp[:, 1:-1], kp[:, 2:]], axis=2)
    vw = jnp.concatenate([vp[:, :-2], vp[:, 1:-1], vp[:, 2:]], axis=2)
    qpos = jnp.arange(Tp).reshape(nb, block)
    kpos = (jnp.arange(nb)[:, None] - 1) * block + jnp.arange(3 * block)[None, :]
    valid = ((jnp.abs(qpos[:, :, None] - kpos[:, None, :]) <= half_window)
             & (kpos[:, None, :] >= 0) & (kpos[:, None, :] < T))
    s = jnp.einsum('bnqhgd,bnkhd->bnhgqk', qb, kw) * (Dh ** -0.5)
    s = jnp.where(valid[None, :, None, None], s, -jnp.inf)
    m = s.max(-1)
    if sink is not None:
        sink_b = sink.astype(F32).reshape(Hkv, G)[None, None, :, :, None]
        m = jnp.maximum(m, sink_b)
    p = jnp.exp(s - m[..., None])
    l = p.sum(-1)
    if sink is not None:
        l = l + jnp.exp(sink_b - m)
    acc = jnp.einsum('bnhgqk,bnkhd->bnqhgd', p, vw).reshape(Bn, Tp, Hq, Dh)[:, :T]
    m = m.transpose(0, 1, 4, 2, 3).reshape(Bn, Tp, Hq)[:, :T]
    l = l.transpose(0, 1, 4, 2, 3).reshape(Bn, Tp, Hq)[:, :T]
    return acc, m, l


def dilated_attention(q, k, v, window, dilation):
    Bn, S = q.shape[0], q.shape[1]
    half = window // (2 * dilation)
    sub = S // dilation

    def to_strided(t):
        rest = t.shape[2:]
        return t.reshape((Bn, sub, dilation) + rest).swapaxes(1, 2).reshape((Bn * dilation, sub) + rest)

    def from_strided(t):
        rest = t.shape[2:]
        return t.reshape((Bn, dilation, sub) + rest).swapaxes(1, 2).reshape((Bn, S) + rest)

    acc, m, l = banded_attention(to_strided(q), to_strided(k), to_strided(v), half, half)
    return from_strided(acc), from_strided(m), from_strided(l)


def mlstm_chunkwise(q, k, v, i_pre, log_f):
    Bn, H, S, Dh = q.shape
    L = A_CHUNK
    N = S // L
    q = q.reshape(Bn, H, N, L, Dh)
    k = k.reshape(Bn, H, N, L, Dh)
    v = v.reshape(Bn, H, N, L, Dh)
    ig = i_pre.reshape(Bn, H, N, L)
    b = jnp.cumsum(log_f.reshape(Bn, H, N, L), axis=-1)
    causal = jnp.tril(jnp.ones((L, L), bool))
    d_log = jnp.where(causal, b[..., :, None] - b[..., None, :] + ig[..., None, :], -jnp.inf)
    m_intra = d_log.max(-1)
    w_qk = jnp.exp(d_log - m_intra[..., None]) * jnp.einsum('bhntd,bhnsd->bhnts', q, k)
    intra_num = jnp.einsum('bhnts,bhnsd->bhntd', w_qk, v)
    intra_den = w_qk.sum(-1)
    b_last = b[..., -1]
    s_log = b_last[..., None] - b + ig
    m_chunk = s_log.max(-1)
    s_w = jnp.exp(s_log - m_chunk[..., None])
    c_chunk = jnp.einsum('bhnsd,bhnse->bhnde', k * s_w[..., None], v)
    n_chunk = jnp.einsum('bhns,bhnsd->bhnd', s_w, k)

    def step(carry, inp):
        c, n, m = carry
        bl, mc, cc, nc = inp
        m_new = jnp.maximum(bl + m, mc)
        dec = jnp.exp(bl + m - m_new)
        gain = jnp.exp(mc - m_new)
        c_new = dec[..., None, None] * c + gain[..., None, None] * cc
        n_new = dec[..., None] * n + gain[..., None] * nc
        return (c_new, n_new, m_new), (c, n, m)

    init = (jnp.zeros((Bn, H, Dh, Dh), F32), jnp.zeros((Bn, H, Dh), F32), jnp.zeros((Bn, H), F32))
    xs = (jnp.moveaxis(b_last, 2, 0), jnp.moveaxis(m_chunk, 2, 0),
          jnp.moveaxis(c_chunk, 2, 0), jnp.moveaxis(n_chunk, 2, 0))
    _, (c_prev, n_prev, m_prev) = lax.scan(step, init, xs)
    c_prev = jnp.moveaxis(c_prev, 0, 2)
    n_prev = jnp.moveaxis(n_prev, 0, 2)
    m_prev = jnp.moveaxis(m_prev, 0, 2)
    inter_log = b + m_prev[..., None]
    m_t = jnp.maximum(inter_log, m_intra)
    a_inter = jnp.exp(inter_log - m_t)
    a_intra = jnp.exp(m_intra - m_t)
    num = (a_inter[..., None] * jnp.einsum('bhntd,bhnde->bhnte', q, c_prev)
           + a_intra[..., None] * intra_num)
    den = a_inter * jnp.einsum('bhntd,bhnd->bhnt', q, n_prev) + a_intra * intra_den
    h = num / jnp.maximum(jnp.abs(den), jnp.exp(-m_t))[..., None]
    return h.reshape(Bn, H, S, Dh)


def gated_delta_chunkwise(q, k, v, beta, g):
    Bn, H, S, Dk = q.shape
    Dv = v.shape[-1]
    L = C_CHUNK
    N = S // L
    q = q.reshape(Bn, H, N, L, Dk)
    k = k.reshape(Bn, H, N, L, Dk)
    v = v.reshape(Bn, H, N, L, Dv)
    beta = beta.reshape(Bn, H, N, L)
    gc = jnp.cumsum(g.reshape(Bn, H, N, L), axis=-1)
    incl = jnp.tril(jnp.ones((L, L), bool))
    strict = jnp.tril(jnp.ones((L, L), bool), -1)
    decay = jnp.exp(jnp.where(incl, gc[..., :, None] - gc[..., None, :], -jnp.inf))
    k_beta = k * beta[..., None]
    a_strict = jnp.where(strict, jnp.einsum('bhnid,bhnjd->bhnij', k_beta, k) * decay, 0.0)
    rhs = jnp.concatenate([v * beta[..., None], k_beta * jnp.exp(gc)[..., None]], axis=-1)
    sol = lax.linalg.triangular_solve(a_strict, rhs, left_side=True, lower=True, unit_diagonal=True)
    u, w = sol[..., :Dv], sol[..., Dv:]
    attn = jnp.where(incl, jnp.einsum('bhnid,bhnjd->bhnij', q, k) * decay, 0.0)
    g_last = gc[..., -1]
    k_dec = k * jnp.exp(g_last[..., None] - gc)[..., None]

    def step(state, inp):
        u_n, w_n, kd_n, dl_n = inp
        v_new = u_n - jnp.einsum('bhld,bhde->bhle', w_n, state)
        new_state = dl_n[..., None, None] * state + jnp.einsum('bhld,bhle->bhde', kd_n, v_new)
        return new_state, (state, v_new)

    init = jnp.zeros((Bn, H, Dk, Dv), F32)
    xs = (jnp.moveaxis(u, 2, 0), jnp.moveaxis(w, 2, 0), jnp.moveaxis(k_dec, 2, 0),
          jnp.moveaxis(jnp.exp(g_last), 2, 0))
    _, (s_prev, v_new) = lax.scan(step, init, xs)
    s_prev = jnp.moveaxis(s_prev, 0, 2)
    v_new = jnp.moveaxis(v_new, 0, 2)
    o = (jnp.einsum('bhnld,bhnde->bhnle', q * jnp.exp(gc)[..., None], s_prev)
         + jnp.einsum('bhnij,bhnje->bhnie', attn, v_new))
    return o.reshape(Bn, H, S, Dv)


def flip_seq(t):
    return jnp.flip(t, axis=2)


def mlstm_branch(p, gate_bias, norm_w):
    Bn, S, _ = p['a_q'].shape
    to_heads = lambda t: t.astype(F32).reshape(Bn, S, A_HEADS, A_HEAD_DIM).transpose(0, 2, 1, 3)
    q = to_heads(p['a_q'])
    k = to_heads(p['a_k']) * (A_HEAD_DIM ** -0.5)
    v = to_heads(p['a_v'])
    gp = p['a_if'].astype(F32).reshape(Bn, S, 2, 2, A_HEADS) + gate_bias.astype(F32)
    gp = gp.transpose(2, 3, 0, 4, 1)
    h_fwd = mlstm_chunkwise(q, k, v, gp[0, 0], jax.nn.log_sigmoid(gp[0, 1]))
    h_bwd = flip_seq(mlstm_chunkwise(flip_seq(q), flip_seq(k), flip_seq(v),
                                     flip_seq(gp[1, 0]), flip_seq(jax.nn.log_sigmoid(gp[1, 1]))))
    h = (h_fwd + h_bwd).transpose(0, 2, 1, 3)
    mu = h.mean(-1, keepdims=True)
    var = jnp.square(h - mu).mean(-1, keepdims=True)
    h = ((h - mu) * lax.rsqrt(var + LN_EPS)).reshape(Bn, S, A_WIDTH) * norm_w.astype(F32)
    return jax.nn.sigmoid(p['a_o'].astype(F32)) * h


def dilated_branch(p, cos, sin):
    Bn, S, _ = p['b_q'].shape
    shp = (Bn, S, B_GROUPS, B_HEADS, B_HEAD_DIM)
    q = apply_rope(p['b_q'].reshape(Bn, S, B_GROUPS * B_HEADS, B_HEAD_DIM), cos, sin).reshape(shp)
    k = apply_rope(p['b_k'].reshape(Bn, S, B_GROUPS * B_HEADS, B_HEAD_DIM), cos, sin).reshape(shp)
    v = p['b_v'].astype(F32).reshape(shp)
    accs, ms, ls = [], [], []
    for grp, (window, dilation) in enumerate(B_PAIRS):
        acc, m, l = dilated_attention(q[:, :, grp], k[:, :, grp], v[:, :, grp], window, dilation)
        accs.append(acc)
        ms.append(m)
        ls.append(l)
    acc = jnp.stack(accs)
    m = jnp.stack(ms)
    l = jnp.stack(ls)
    wgt = jnp.exp(m - m.max(0, keepdims=True))
    out = (wgt[..., None] * acc).sum(0) / (wgt * l).sum(0)[..., None]
    return out.reshape(Bn, S, B_WIDTH)


def deltanet_branch(p, conv_w, a_log, dt_bias, norm_w):
    Bn, S, _ = p['c_qkv'].shape
    qkv = p['c_qkv']
    qkv = lax.conv_general_dilated(qkv, conv_w[:, None, :].astype(qkv.dtype), window_strides=(1,),
                                   padding=[(C_CONV // 2, C_CONV // 2)],
                                   dimension_numbers=('NWC', 'WIO', 'NWC'),
                                   feature_group_count=3 * C_WIDTH)
    qkv = jax.nn.silu(qkv.astype(F32))
    q, k, v = jnp.split(qkv, 3, axis=-1)
    to_heads = lambda t: t.reshape(Bn, S, C_HEADS, C_HEAD_DIM).transpose(0, 2, 1, 3)
    l2n = lambda t: t * lax.rsqrt(jnp.sum(t * t, -1, keepdims=True) + 1e-6)
    q = l2n(to_heads(q)) * (C_HEAD_DIM ** -0.5)
    k = l2n(to_heads(k))
    v = to_heads(v)
    beta = jax.nn.sigmoid(p['c_beta'].astype(F32).reshape(Bn, S, 2, C_HEADS))
    g = -jnp.exp(a_log.astype(F32)) * jax.nn.softplus(
        p['c_decay'].astype(F32).reshape(Bn, S, 2, C_HEADS) + dt_bias.astype(F32))
    beta = beta.transpose(2, 0, 3, 1)
    g = g.transpose(2, 0, 3, 1)
    o_fwd = gated_delta_chunkwise(q, k, v, beta[0], g[0])
    o_bwd = flip_seq(gated_delta_chunkwise(flip_seq(q), flip_seq(k), flip_seq(v),
                                           flip_seq(beta[1]), flip_seq(g[1])))
    o = (o_fwd + o_bwd).transpose(0, 2, 1, 3)
    o = o * lax.rsqrt(jnp.mean(o * o, -1, keepdims=True) + 1e-6) * norm_w.astype(F32)
    o = o * jax.nn.silu(p['c_gate'].astype(F32).reshape(Bn, S, C_HEADS, C_HEAD_DIM))
    return o.reshape(Bn, S, C_WIDTH)


def window_branch(p, sink, cos, sin):
    Bn, S, _ = p['d_q'].shape
    q = apply_rope(p['d_q'].reshape(Bn, S, D_Q_HEADS, D_HEAD_DIM), cos, sin)
    k = apply_rope(p['d_k'].reshape(Bn, S, D_KV_HEADS, D_HEAD_DIM), cos, sin)
    v = p['d_v'].astype(F32).reshape(Bn, S, D_KV_HEADS, D_HEAD_DIM)
    acc, m, l = banded_attention(q, k, v, D_WINDOW, D_BLOCK, sink=sink)
    return (acc / l[..., None]).reshape(Bn, S, D_Q_WIDTH)


def token_mixing(x, w_in, a_gate_bias, a_norm_w, c_conv_w, c_a_log, c_dt_bias, c_norm_w, d_sink,
                 w_branch_a, w_branch_b, w_branch_c, w_branch_d, w_out, cos, sin):
    Bn, S, _ = x.shape
    p = split_columns(jnp.matmul(x, w_in))
    y_a = mlstm_branch(p, a_gate_bias, a_norm_w).astype(x.dtype)
    y_b = dilated_branch(p, cos, sin).astype(x.dtype)
    y_c = deltanet_branch(p, c_conv_w, c_a_log, c_dt_bias, c_norm_w).astype(x.dtype)
    y_d = window_branch(p, d_sink, cos, sin).astype(x.dtype)
    gates = jax.nn.sigmoid(p['gates'].astype(F32).reshape(Bn, S, N_BRANCHES, D_MODEL))
    merged = (gates[:, :, 0] * jnp.matmul(y_a, w_branch_a).astype(F32)
              + gates[:, :, 1] * jnp.matmul(y_b, w_branch_b).astype(F32)
              + gates[:, :, 2] * jnp.matmul(y_c, w_branch_c).astype(F32)
              + gates[:, :, 3] * jnp.matmul(y_d, w_branch_d).astype(F32))
    return jnp.matmul(merged.astype(x.dtype), w_out).astype(x.dtype)


def swiglu(x, w1, w3, w2):
    return jnp.matmul(jax.nn.silu(jnp.matmul(x, w1)) * jnp.matmul(x, w3), w2).astype(x.dtype)


def moe_swiglu(x, router_w, w1, w3, w2):
    Bn, S, D = x.shape
    T = Bn * S
    A = T * TOP_K
    xt = x.reshape(T, D)
    logits = jnp.matmul(xt, router_w).astype(F32)
    top_logit, top_idx = lax.top_k(logits, TOP_K)
    gates = jax.nn.softmax(top_logit, axis=-1)
    flat_e = top_idx.reshape(A).astype(jnp.int32)
    order = jnp.argsort(flat_e).astype(jnp.int32)
    sorted_e = flat_e[order]
    counts = jnp.bincount(flat_e, length=N_EXPERTS).astype(jnp.int32)
    padded = (counts + MOE_BLOCK - 1) // MOE_BLOCK * MOE_BLOCK
    ends = jnp.cumsum(padded)
    starts_p = ends - padded
    starts = jnp.cumsum(counts) - counts
    dest = starts_p[sorted_e] + jnp.arange(A, dtype=jnp.int32) - starts[sorted_e]
    n_blocks = -(-A // MOE_BLOCK) + N_EXPERTS
    slot_token = jnp.zeros(n_blocks * MOE_BLOCK, jnp.int32).at[dest].set(order // TOP_K)
    block_start = jnp.arange(n_blocks, dtype=jnp.int32) * MOE_BLOCK
    block_expert = jnp.minimum(jnp.searchsorted(ends, block_start, side='right'), N_EXPERTS - 1)
    xb = xt[slot_token].reshape(n_blocks, MOE_BLOCK, D)

    def expert_block(args):
        xi, e = args
        return jnp.matmul(jax.nn.silu(jnp.matmul(xi, w1[e])) * jnp.matmul(xi, w3[e]), w2[e])

    yb = lax.map(expert_block, (xb, block_expert)).reshape(n_blocks * MOE_BLOCK, D)
    slot_of = jnp.zeros(A, jnp.int32).at[order].set(dest).reshape(T, TOP_K)
    y = jnp.einsum('tk,tkd->td', gates.astype(yb.dtype), yb[slot_of])
    return y.reshape(Bn, S, D).astype(x.dtype)


def setup_inputs(seed: int = 0) -> dict:
    key = jax.random.key(seed)
    ks = jax.random.split(key, 32)
    L = DEPTH
    nrm = lambda k, shape, scale: jax.random.normal(k, shape, F32) * scale
    x = nrm(ks[0], (BATCH, SEQ, D_MODEL), 1.0)
    w_in = nrm(ks[1], (L, D_MODEL, N_IN), D_MODEL ** -0.5)
    i_bias = nrm(ks[2], (L, 2, A_HEADS), 0.1)
    f_bias = 3.0 + 3.0 * jax.random.uniform(ks[3], (L, 2, A_HEADS), F32)
    a_gate_bias = jnp.stack([i_bias, f_bias], axis=2)
    a_norm_w = 1.0 + nrm(ks[4], (L, A_WIDTH), 0.02)
    c_conv_w = nrm(ks[5], (L, C_CONV, 3 * C_WIDTH), C_CONV ** -0.5)
    c_a_log = jnp.log(jax.random.uniform(ks[6], (L, 2, C_HEADS), F32, minval=1.0, maxval=16.0))
    dt = jnp.exp(jax.random.uniform(ks[7], (L, 2, C_HEADS), F32,
                                    minval=math.log(1e-3), maxval=math.log(1e-1)))
    c_dt_bias = dt + jnp.log(-jnp.expm1(-dt))
    c_norm_w = 1.0 + nrm(ks[8], (L, C_HEAD_DIM), 0.02)
    d_sink = nrm(ks[9], (L, D_Q_HEADS), 0.5)
    w_branch_a = nrm(ks[10], (L, A_WIDTH, D_MODEL), DEEPNORM_BETA * A_WIDTH ** -0.5)
    w_branch_b = nrm(ks[11], (L, B_WIDTH, D_MODEL), DEEPNORM_BETA * B_WIDTH ** -0.5)
    w_branch_c = nrm(ks[12], (L, C_WIDTH, D_MODEL), DEEPNORM_BETA * C_WIDTH ** -0.5)
    w_branch_d = nrm(ks[13], (L, D_Q_WIDTH, D_MODEL), DEEPNORM_BETA * D_Q_WIDTH ** -0.5)
    w_out = nrm(ks[14], (L, D_MODEL, D_MODEL), DEEPNORM_BETA * D_MODEL ** -0.5)
    ln1_w = 1.0 + nrm(ks[15], (L, D_MODEL), 0.02)
    ln1_b = nrm(ks[16], (L, D_MODEL), 0.02)
    ln2_w = 1.0 + nrm(ks[17], (L, D_MODEL), 0.02)
    ln2_b = nrm(ks[18], (L, D_MODEL), 0.02)
    ffn_w1 = nrm(ks[19], (N_DENSE, D_MODEL, D_FF), DEEPNORM_BETA * D_MODEL ** -0.5)
    ffn_w3 = nrm(ks[20], (N_DENSE, D_MODEL, D_FF), DEEPNORM_BETA * D_MODEL ** -0.5)
    ffn_w2 = nrm(ks[21], (N_DENSE, D_FF, D_MODEL), DEEPNORM_BETA * D_FF ** -0.5)
    moe_router = nrm(ks[22], (N_MOE, D_MODEL, N_EXPERTS), D_MODEL ** -0.5)
    moe_w1 = nrm(ks[23], (N_MOE, N_EXPERTS, D_MODEL, D_FF_EXPERT), DEEPNORM_BETA * D_MODEL ** -0.5)
    moe_w3 = nrm(ks[24], (N_MOE, N_EXPERTS, D_MODEL, D_FF_EXPERT), DEEPNORM_BETA * D_MODEL ** -0.5)
    moe_w2 = nrm(ks[25], (N_MOE, N_EXPERTS, D_FF_EXPERT, D_MODEL), DEEPNORM_BETA * D_FF_EXPERT ** -0.5)
    return {'x': x, 'w_in': w_in, 'a_gate_bias': a_gate_bias, 'a_norm_w': a_norm_w,
            'c_conv_w': c_conv_w, 'c_a_log': c_a_log, 'c_dt_bias': c_dt_bias, 'c_norm_w': c_norm_w,
            'd_sink': d_sink, 'w_branch_a': w_branch_a, 'w_branch_b': w_branch_b,
            'w_branch_c': w_branch_c, 'w_branch_d': w_branch_d, 'w_out': w_out,
            'ln1_w': ln1_w, 'ln1_b': ln1_b, 'ln2_w': ln2_w, 'ln2_b': ln2_b,
            'ffn_w1': ffn_w1, 'ffn_w3': ffn_w3, 'ffn_w2': ffn_w2,
            'moe_router': moe_router, 'moe_w1': moe_w1, 'moe_w3': moe_w3, 'moe_w2': moe_w2}


def reference(x, w_in, a_gate_bias, a_norm_w, c_conv_w, c_a_log, c_dt_bias, c_norm_w, d_sink,
              w_branch_a, w_branch_b, w_branch_c, w_branch_d, w_out, ln1_w, ln1_b, ln2_w, ln2_b,
              ffn_w1, ffn_w3, ffn_w2, moe_router, moe_w1, moe_w3, moe_w2):
    cos, sin = rope_tables(x.shape[1], B_HEAD_DIM)
    for layer in range(DEPTH):
        mix = token_mixing(x, w_in[layer], a_gate_bias[layer], a_norm_w[layer], c_conv_w[layer],
                           c_a_log[layer], c_dt_bias[layer], c_norm_w[layer], d_sink[layer],
                           w_branch_a[layer], w_branch_b[layer], w_branch_c[layer], w_branch_d[layer],
                           w_out[layer], cos, sin)
        x = layer_norm(DEEPNORM_ALPHA * x + mix, ln1_w[layer], ln1_b[layer])
        j = layer // 2
        if layer % 2 == 0:
            f = swiglu(x, ffn_w1[j], ffn_w3[j], ffn_w2[j])
        else:
            f = moe_swiglu(x, moe_router[j], moe_w1[j], moe_w3[j], moe_w2[j])
        x = layer_norm(DEEPNORM_ALPHA * x + f, ln2_w[layer], ln2_b[layer])
    return x
```

```python
import contextlib
import numpy as np
import ml_dtypes
import concourse.bass as bass
import concourse.mybir as mybir
from concourse.bass_utils import run_bass_kernel_spmd

F32 = mybir.dt.float32
BF16 = mybir.dt.bfloat16
AF = mybir.ActivationFunctionType
ALU = mybir.AluOpType
AX = mybir.AxisListType
NPBF = ml_dtypes.bfloat16

NCORES = 8
SEQ = 16384
D = 1024
TC = SEQ // NCORES
DEPTH = 4
ALPHA = (2 * DEPTH) ** 0.25
LN_EPS = 1e-5


class Prog:
    ENGS = ('tensor', 'vector', 'scalar', 'gpsimd', 'sync')

    def __init__(self, nc):
        self.nc = nc
        self.stack = contextlib.ExitStack()
        self.recs = {e: [] for e in self.ENGS}
        self.cnt = {}
        self.lastw = {}
        self.rd = {}
        self.seen = {e: {} for e in self.ENGS}
        self.final = []
        self.nsb = 0
        self.pskeys = set()
        self.strict = False

    def sb(self, name, shape, dt):
        return self.stack.enter_context(self.nc.sbuf_tensor("sb_" + name, list(shape), dt))

    def ps(self, name, shape=None, dt=F32):
        self.pskeys.add(name)
        return self.stack.enter_context(self.nc.psum_tensor("pp_" + name, [128, 512], F32))

    def _deps(self, eng, reads, writes):
        deps = {}

        def add(tok):
            if tok is None:
                return
            k, v = tok
            if k == eng and eng == 'tensor':
                return
            if deps.get(k, 0) < v:
                deps[k] = v

        for key in reads:
            add(self.lastw.get(key))
            if key in self.pskeys:
                for t in self.rd.get(key, {}).items():
                    add(t)
        for key in writes:
            add(self.lastw.get(key))
            for t in self.rd.get(key, {}).items():
                add(t)
        out = []
        for k, v in deps.items():
            if self.seen[eng].get(k, 0) < v:
                self.seen[eng][k] = v
                out.append((k, v))
        return out

    def _commit(self, tok, reads, writes):
        for key in reads:
            d = self.rd.setdefault(key, {})
            if d.get(tok[0], 0) < tok[1]:
                d[tok[0]] = tok[1]
        for key in writes:
            self.lastw[key] = tok
            self.rd[key] = {}

    def op(self, eng, fn, reads=(), writes=()):
        waits = self._deps(eng, reads, writes)
        self.cnt[eng] = self.cnt.get(eng, 0) + 1
        tok = (eng, self.cnt[eng])
        self.recs[eng].append(dict(fn=fn, waits=waits, tok=tok, dma=False))
        self._commit(tok, reads, writes)
        return tok

    def dma(self, eng, fn, sem, reads=(), writes=(), final=False):
        waits = self._deps(eng, reads, writes)
        sk = 'dma:' + sem
        self.cnt[sk] = self.cnt.get(sk, 0) + 16
        tok = (sk, self.cnt[sk])
        self.recs[eng].append(dict(fn=fn, waits=waits, tok=tok, dma=True))
        self._commit(tok, reads, writes)
        if final:
            self.final.append((eng, tok))
        return tok

    def mm(self, out, lhsT, rhs, start, stop, reads, writes):
        return self.op('tensor', lambda e: e.matmul(out, lhsT=lhsT, rhs=rhs, start=start, stop=stop),
                       reads=reads, writes=writes)

    def emit(self):
        nc = self.nc
        fin = {}
        for eng, tok in self.final:
            d = fin.setdefault(eng, {})
            if d.get(tok[0], 0) < tok[1]:
                d[tok[0]] = tok[1]
        for eng, d in fin.items():
            self.recs[eng].append(dict(fn=None, waits=list(d.items()), tok=None, dma=False))
        waited = {}
        for e in self.ENGS:
            for r in self.recs[e]:
                for (k, v) in r['waits']:
                    waited.setdefault(k, set()).add(v)
        rank = {}
        for k, vals in waited.items():
            if k in self.ENGS:
                rank[k] = {v: i + 1 for i, v in enumerate(sorted(vals))}
        semkeys = sorted(set(list(waited.keys()) + [k for k in self.cnt if k.startswith('dma:')]))
        sems = {}
        for i, k in enumerate(semkeys):
            sems[k] = self.stack.enter_context(nc.semaphore("s%d_%s" % (i, k.replace(':', '_'))))
        recs = self.recs
        with nc.Block() as block:
            for e in self.ENGS:
                if not recs[e]:
                    continue

                def body(engobj, e=e):
                    wset = waited.get(e, ())
                    for r in recs[e]:
                        for (k, v) in r['waits']:
                            val = rank[k][v] if k in rank else v
                            engobj.wait_ge(sems[k], val)
                        if r['fn'] is None:
                            continue
                        ins = r['fn'](engobj)
                        if r['dma']:
                            ins.then_inc(sems[r['tok'][0]], 16)
                        elif r['tok'][1] in wset:
                            ins.then_inc(sems[e], 1)

                getattr(block, e)(body)
        self.stack.close()


def _new_nc():
    return bass.Bass("TRN2", target_bir_lowering=False)


W_IN_OFF = {}
_off = 0
for _n, _s in (('gates', 4096), ('a_q', 256), ('a_k', 256), ('a_v', 256), ('a_o', 256), ('a_if', 16),
               ('b_q', 768), ('b_k', 768), ('b_v', 768), ('c_qkv', 768), ('c_gate', 256), ('c_beta', 8),
               ('c_decay', 8), ('d_q', 512), ('d_k', 128), ('d_v', 128)):
    W_IN_OFF[_n] = (_off, _s)
    _off += _s
N_IN = _off


def _cols(name):
    o, s = W_IN_OFF[name]
    return np.arange(o, o + s)


def l1_columns():
    plain = np.concatenate([_cols('a_q'), _cols('a_k'), _cols('a_v'), _cols('b_v'), _cols('d_v'),
                            -np.ones(128, np.int64)])
    rope = np.concatenate([_cols('b_q'), _cols('b_k'), _cols('d_q'), _cols('d_k')])
    r = np.arange(rope.size)
    swap = rope[(r // 64) * 64 + ((r % 64) + 32) % 64]
    inter = np.stack([rope.reshape(-1, 128), swap.reshape(-1, 128)], axis=1).reshape(-1)
    conv = _cols('c_qkv')
    gate = -np.ones(128, np.int64)
    oi, _ = W_IN_OFF['a_if']
    for d in range(2):
        for h in range(4):
            gate[0 + d * 4 + h] = oi + d * 8 + h
            gate[32 + d * 4 + h] = oi + d * 8 + 4 + h
            gate[64 + d * 4 + h] = W_IN_OFF['c_beta'][0] + d * 4 + h
            gate[96 + d * 4 + h] = W_IN_OFF['c_decay'][0] + d * 4 + h
    return np.concatenate([plain, inter, conv, gate])


L1_NPLAIN, L1_NROPE, L1_NCONV = 1664, 2176, 768
L1_NC = (L1_NPLAIN + 128) + 2 * L1_NROPE + L1_NCONV + 128
L1_WG = 512


def build_l1():
    nc = _new_nc()
    T = TC
    NT = T // 512
    xT = nc.dram_tensor("xT", [D, T], F32, kind="ExternalInput").ap()
    xhT = nc.dram_tensor("xhT", [D, 4], F32, kind="ExternalInput").ap()
    w1 = nc.dram_tensor("w1", [D, L1_NC], F32, kind="ExternalInput").ap()
    cosd = nc.dram_tensor("cos", [128, T], F32, kind="ExternalInput").ap()
    sind = nc.dram_tensor("sin", [128, T], F32, kind="ExternalInput").ap()
    convw = nc.dram_tensor("convw", [768, 5], F32, kind="ExternalInput").ap()
    gpar = nc.dram_tensor("gpar", [128, 2], F32, kind="ExternalInput").ap()
    bones = nc.dram_tensor("bones", [128, 128], F32, kind="ExternalInput").ap()
    oA = nc.dram_tensor("oA", [L1_NPLAIN, T], BF16, kind="ExternalOutput").ap()
    oB = nc.dram_tensor("oB", [L1_NROPE, T], BF16, kind="ExternalOutput").ap()
    oC = nc.dram_tensor("oC", [L1_NCONV, T], BF16, kind="ExternalOutput").ap()
    oG = nc.dram_tensor("oG", [128, T], F32, kind="ExternalOutput").ap()

    P = Prog(nc)
    xf = [P.sb("xf%d" % i, [128, T], F32) for i in range(2)]
    xb = P.sb("xb", [128, 8, T], BF16)
    xhf = P.sb("xhf", [128, 8, 4], F32)
    xhb = P.sb("xhb", [128, 8, 4], BF16)
    cos_t = P.sb("cos_t", [128, T], F32)
    sin_t = P.sb("sin_t", [128, T], F32)
    cw = P.sb("cw", [128, 6, 5], F32)
    gp = P.sb("gp", [128, 2], F32)
    gneg = P.sb("gneg", [128, 2], F32)
    bo = P.sb("bo", [128, 128], F32)
    wst = [P.sb("wst%d" % i, [128, 8, L1_WG], F32) for i in range(2)]
    wbf = [P.sb("wbf%d" % i, [128, 8, L1_WG], BF16) for i in range(2)]
    ot = [P.sb("ot%d" % i, [128, 512], BF16) for i in range(3)]
    otg = P.sb("otg", [128, T], F32)
    r1 = [P.sb("r1_%d" % i, [128, 512], F32) for i in range(2)]
    r2 = [P.sb("r2_%d" % i, [128, 512], F32) for i in range(2)]
    cbuf = P.sb("cbuf", [128, T + 4], F32)
    cacc = P.sb("cacc", [128, T], F32)
    csq = P.sb("csq", [128, T], F32)
    crs = P.sb("crs", [128, 512], F32)
    cout = P.sb("cout", [128, T], BF16)
    gt1 = P.sb("gt1", [128, 512], F32)
    eps6 = P.sb("eps6", [128, 1], F32)
    P.op('vector', lambda e: e.memset(eps6[:], 1e-6), writes=['eps6'])
    pss = [P.ps("ps%d" % i, [128, 512]) for i in range(6)]
    psh = P.ps("psh", [128, 4])
    psn = P.ps("psn", [128, 512])

    P.dma('sync', lambda e: e.dma_start(out=xhf[:], in_=xhT.rearrange("(k p) t -> p k t", p=128)), 'xhf', writes=['xhf'])
    P.dma('sync', lambda e: e.dma_start(out=cos_t[:], in_=cosd), 'cos', writes=['cos'])
    P.dma('sync', lambda e: e.dma_start(out=sin_t[:], in_=sind), 'sin', writes=['sin'])
    P.dma('sync', lambda e: e.dma_start(out=cw[:], in_=convw.rearrange("(c p) j -> p c j", p=128)), 'cw', writes=['cw'])
    P.dma('sync', lambda e: e.dma_start(out=gp[:], in_=gpar), 'gp', writes=['gp'])
    P.dma('sync', lambda e: e.dma_start(out=bo[:], in_=bones), 'bo', writes=['bo'])
    xbk = ['xb%d' % k for k in range(8)]
    for k in range(8):
        eng = 'vector' if k % 2 == 0 else 'gpsimd'
        P.dma('sync', lambda e, k=k: e.dma_start(out=xf[k % 2][:], in_=xT[k * 128:(k + 1) * 128, :]), 'xf%d' % (k % 2), writes=['xf%d' % (k % 2)])
        P.op(eng, lambda e, k=k: e.tensor_copy(out=xb[:, k, :], in_=xf[k % 2][:]), reads=['xf%d' % (k % 2)], writes=[xbk[k]])
    P.op('vector', lambda e: e.tensor_copy(out=xhb[:], in_=xhf[:]), reads=['xhf'], writes=['xhb'])
    P.op('vector', lambda e: e.tensor_scalar(out=gneg[:, 0:1], in0=gp[:, 0:1], scalar1=-1.0, scalar2=None, op0=ALU.mult),
         reads=['gp'], writes=['gneg'])
    P.op('scalar', lambda e: e.activation(out=gneg[:, 1:2], in_=gp[:, 1:2], func=AF.Exp), reads=['gp', 'gneg'], writes=['gneg'])
    P.op('vector', lambda e: e.tensor_scalar(out=gneg[:, 1:2], in0=gneg[:, 1:2], scalar1=-1.0, scalar2=None, op0=ALU.mult),
         reads=['gneg'], writes=['gneg'])

    ngroups = (L1_NC + L1_WG - 1) // L1_WG
    state = dict(oti=0, psi=0, r=0)
    loaded = set()

    def load_group(g):
        if g in loaded or g >= ngroups:
            return
        loaded.add(g)
        s = g % 2
        c0 = g * L1_WG
        cw_ = min(L1_WG, L1_NC - c0)
        P.dma('sync', lambda e: e.dma_start(out=wst[s][:, :, 0:cw_],
                                            in_=w1[:, c0:c0 + cw_].rearrange("(k p) c -> p k c", p=128)),
              'wst%d' % s, writes=['wst%d' % s])
        for k in range(8):
            eng = ('vector', 'gpsimd')[k % 2]
            P.op(eng, lambda e, k=k: e.tensor_copy(out=wbf[s][:, k, 0:cw_], in_=wst[s][:, k, 0:cw_]),
                 reads=['wst%d' % s], writes=['wbf%d_%d' % (s, k)])

    def wkeys(s):
        return ['wbf%d_%d' % (s, k) for k in range(8)]

    def ensure(col):
        g = col // L1_WG
        load_group(g)
        load_group(g + 1)
        return g % 2, col % L1_WG

    def proj(ps, psk, col, t0, n, halo=False):
        s, cl = ensure(col)
        for k in range(8):
            rhs = xhb[:, k, :] if halo else xb[:, k, t0:t0 + n]
            P.mm(ps[:, 0:n], wbf[s][:, k, cl:cl + 128], rhs, k == 0, k == 7,
                 reads=wkeys(s) + (['xhb'] if halo else xbk), writes=[psk])

    def next_ps():
        i = state['psi']
        state['psi'] = (i + 1) % 6
        return pss[i], 'ps%d' % i

    def next_ot():
        i = state['oti']
        state['oti'] = (i + 1) % 3
        return ot[i], 'ot%d' % i

    n_plain, n_rope, n_conv = L1_NPLAIN // 128, L1_NROPE // 128, L1_NCONV // 128
    for c in range(n_plain):
        scale = 0.125 if 2 <= c < 4 else 1.0
        for tt in range(NT):
            ps, psk = next_ps()
            proj(ps, psk, c * 128, tt * 512, 512)
            o, ok = next_ot()
            P.op('scalar', lambda e, o=o, ps=ps, scale=scale: e.activation(out=o[:], in_=ps[:], func=AF.Copy, scale=scale),
                 reads=[psk], writes=[ok])
            P.dma('gpsimd', lambda e, o=o, c=c, tt=tt: e.dma_start(out=oA[c * 128:(c + 1) * 128, tt * 512:(tt + 1) * 512], in_=o[:]),
                  ok, reads=[ok], final=True)
    base = L1_NPLAIN + 128
    for c in range(n_rope):
        for tt in range(NT):
            psa, pska = next_ps()
            proj(psa, pska, base + c * 256, tt * 512, 512)
            psb, pskb = next_ps()
            proj(psb, pskb, base + c * 256 + 128, tt * 512, 512)
            i = state['r']
            state['r'] = (i + 1) % 2
            P.op('vector', lambda e, i=i, psa=psa, tt=tt: e.tensor_tensor(out=r1[i][:], in0=psa[:], in1=cos_t[:, tt * 512:(tt + 1) * 512], op=ALU.mult),
                 reads=[pska, 'cos'], writes=['r1_%d' % i])
            P.op('vector', lambda e, i=i, psb=psb, tt=tt: e.tensor_tensor(out=r2[i][:], in0=psb[:], in1=sin_t[:, tt * 512:(tt + 1) * 512], op=ALU.mult),
                 reads=[pskb, 'sin'], writes=['r2_%d' % i])
            o, ok = next_ot()
            P.op('gpsimd', lambda e, i=i, o=o: e.tensor_tensor(out=o[:], in0=r1[i][:], in1=r2[i][:], op=ALU.add),
                 reads=['r1_%d' % i, 'r2_%d' % i], writes=[ok])
            P.dma('gpsimd', lambda e, o=o, c=c, tt=tt: e.dma_start(out=oB[c * 128:(c + 1) * 128, tt * 512:(tt + 1) * 512], in_=o[:]),
                  ok, reads=[ok], final=True)
    base = L1_NPLAIN + 128 + 2 * L1_NROPE
    for c in range(n_conv):
        col = base + c * 128
        proj(psh, 'psh', col, 0, 4, halo=True)
        P.op('scalar', lambda e: e.copy(out=cbuf[:, 0:2], in_=psh[:, 0:2]), reads=['psh'], writes=['cbuf'])
        P.op('scalar', lambda e: e.copy(out=cbuf[:, T + 2:T + 4], in_=psh[:, 2:4]), reads=['psh'], writes=['cbuf'])
        for tt in range(NT):
            ps, psk = next_ps()
            proj(ps, psk, col, tt * 512, 512)
            P.op('scalar', lambda e, ps=ps, tt=tt: e.copy(out=cbuf[:, 2 + tt * 512:2 + (tt + 1) * 512], in_=ps[:]),
                 reads=[psk], writes=['cbuf'])
        P.op('vector', lambda e, c=c: e.tensor_scalar(out=cacc[:], in0=cbuf[:, 0:T], scalar1=cw[:, c, 0:1], scalar2=None, op0=ALU.mult),
             reads=['cbuf', 'cw'], writes=['cacc'])
        for j in range(1, 5):
            P.op('vector', lambda e, c=c, j=j: e.scalar_tensor_tensor(out=cacc[:], in0=cbuf[:, j:j + T], scalar=cw[:, c, j:j + 1], in1=cacc[:],
                                                                      op0=ALU.mult, op1=ALU.add),
                 reads=['cbuf', 'cw', 'cacc'], writes=['cacc'])
        if c >= 4:
            P.op('scalar', lambda e: e.activation(out=cout[:], in_=cacc[:], func=AF.Silu), reads=['cacc'], writes=['cout'])
        else:
            P.op('scalar', lambda e: e.activation(out=cacc[:], in_=cacc[:], func=AF.Silu), reads=['cacc'], writes=['cacc'])
            P.op('gpsimd', lambda e: e.tensor_tensor(out=csq[:], in0=cacc[:], in1=cacc[:], op=ALU.mult), reads=['cacc'], writes=['csq'])
            sc = 0.125 if c < 2 else 1.0
            for tt in range(NT):
                P.mm(psn[:], bo[:], csq[:, tt * 512:(tt + 1) * 512], True, True, reads=['bo', 'csq'], writes=['psn'])
                P.op('scalar', lambda e: e.activation(out=crs[:], in_=psn[:], func=AF.Sqrt, bias=eps6[:, 0:1], scale=1.0),
                     reads=['psn', 'eps6'], writes=['crs'])
                P.op('vector', lambda e: e.reciprocal(out=crs[:], in_=crs[:]), reads=['crs'], writes=['crs'])
                P.op('vector', lambda e, tt=tt, sc=sc: e.scalar_tensor_tensor(out=cout[:, tt * 512:(tt + 1) * 512], in0=cacc[:, tt * 512:(tt + 1) * 512],
                                                                             scalar=sc, in1=crs[:], op0=ALU.mult, op1=ALU.mult),
                     reads=['cacc', 'crs'], writes=['cout'])
        P.dma('gpsimd', lambda e, c=c: e.dma_start(out=oC[c * 128:(c + 1) * 128, :], in_=cout[:]), 'cout', reads=['cout'], final=True)
    col = base + L1_NCONV
    for tt in range(NT):
        ps, psk = next_ps()
        proj(ps, psk, col, tt * 512, 512)
        sl = slice(tt * 512, (tt + 1) * 512)
        P.op('scalar', lambda e, ps=ps, sl=sl: e.activation(out=otg[0:32, sl], in_=ps[0:32, :], func=AF.Identity, bias=gp[0:32, 0:1], scale=1.0),
             reads=[psk, 'gp'], writes=['otg'])
        P.op('scalar', lambda e, ps=ps: e.activation(out=gt1[32:64, :], in_=ps[32:64, :], func=AF.Exp, bias=gneg[32:64, 0:1], scale=-1.0),
             reads=[psk, 'gneg'], writes=['gt1'])
        P.op('scalar', lambda e: e.activation(out=gt1[32:64, :], in_=gt1[32:64, :], func=AF.Ln, bias=1.0, scale=1.0),
             reads=['gt1'], writes=['gt1'])
        P.op('vector', lambda e, sl=sl: e.tensor_scalar(out=otg[32:64, sl], in0=gt1[32:64, :], scalar1=-1.0, scalar2=None, op0=ALU.mult),
             reads=['gt1'], writes=['otg'])
        P.op('scalar', lambda e, ps=ps, sl=sl: e.activation(out=otg[64:96, sl], in_=ps[64:96, :], func=AF.Sigmoid),
             reads=[psk], writes=['otg'])
        P.op('scalar', lambda e, ps=ps: e.activation(out=gt1[96:128, :], in_=ps[96:128, :], func=AF.Exp, bias=gp[96:128, 0:1], scale=1.0),
             reads=[psk, 'gp'], writes=['gt1'])
        P.op('scalar', lambda e: e.activation(out=gt1[96:128, :], in_=gt1[96:128, :], func=AF.Ln, bias=1.0, scale=1.0),
             reads=['gt1'], writes=['gt1'])
        P.op('vector', lambda e, sl=sl: e.tensor_scalar(out=otg[96:128, sl], in0=gt1[96:128, :], scalar1=gneg[96:128, 1:2], scalar2=None, op0=ALU.mult),
             reads=['gt1', 'gneg'], writes=['otg'])
    P.dma('gpsimd', lambda e: e.dma_start(out=oG[:, :], in_=otg[:]), 'otg', reads=['otg'], final=True)
    P.emit()
    return nc


def rope_tables_np():
    inv = (10000.0 ** (-np.arange(0, 64, 2, dtype=np.float32) / np.float32(64))).astype(np.float32)
    ang = np.arange(SEQ, dtype=np.float32)[:, None] * inv[None, :]
    return np.cos(ang).astype(np.float32), np.sin(ang).astype(np.float32)


def l1_host_inputs(xT_full, layer, inp, consts):
    w1 = consts['w1'][layer]
    maps = []
    xpad = np.zeros((D, SEQ + 4), np.float32)
    xpad[:, 2:SEQ + 2] = xT_full
    for c in range(NCORES):
        t0 = c * TC
        xh = np.concatenate([xpad[:, t0:t0 + 2], xpad[:, t0 + TC + 2:t0 + TC + 4]], axis=1)
        maps.append(dict(xT=np.ascontiguousarray(xT_full[:, t0:t0 + TC]), xhT=np.ascontiguousarray(xh), w1=w1,
                         cos=consts['cos128'][:, t0:t0 + TC], sin=consts['sin128'][:, t0:t0 + TC],
                         convw=consts['convw'][layer], gpar=consts['gpar'][layer], bones=consts['bones']))
    return maps


def make_consts(inp):
    c = {}
    cols = l1_columns()
    w_in = inp['w_in']
    w1 = []
    for l in range(DEPTH):
        w = np.zeros((D, L1_NC), np.float32)
        m = cols >= 0
        w[:, m] = w_in[l][:, cols[m]]
        w1.append(w)
    c['w1'] = w1
    cos, sin = rope_tables_np()
    r = np.arange(128)
    c['cos128'] = np.ascontiguousarray(cos[:, r % 32].T)
    sgn = np.where((r % 64) < 32, -1.0, 1.0).astype(np.float32)
    c['sin128'] = np.ascontiguousarray((sin[:, r % 32] * sgn[None, :]).T)
    c['convw'] = [np.ascontiguousarray(inp['c_conv_w'][l].T) for l in range(DEPTH)]
    gpar = []
    for l in range(DEPTH):
        g = np.zeros((128, 2), np.float32)
        for d in range(2):
            for h in range(4):
                g[d * 4 + h, 0] = inp['a_gate_bias'][l, d, 0, h]
                g[32 + d * 4 + h, 0] = inp['a_gate_bias'][l, d, 1, h]
                g[96 + d * 4 + h, 0] = inp['c_dt_bias'][l, d, h]
                g[96 + d * 4 + h, 1] = inp['c_a_log'][l, d, h]
        gpar.append(g)
    c['gpar'] = gpar
    i = np.arange(128)
    c['bones'] = (i[:, None] // 64 == i[None, :] // 64).astype(np.float32)
    return c


JQ = 1024
NB_JOBS = 6
ND_JOBS = 2


def build_l2a():
    nc = _new_nc()
    BqT = nc.dram_tensor("BqT", [NB_JOBS, 64, 4, JQ], BF16, kind="ExternalInput").ap()
    BkT = nc.dram_tensor("BkT", [NB_JOBS, 64, 4, JQ + 128], BF16, kind="ExternalInput").ap()
    Bv = nc.dram_tensor("Bv", [NB_JOBS, 4, JQ + 128, 65], BF16, kind="ExternalInput").ap()
    DqT = nc.dram_tensor("DqT", [ND_JOBS, 64, 2, 8, 4, 128], BF16, kind="ExternalInput").ap()
    DkT = nc.dram_tensor("DkT", [ND_JOBS, 64, 2, JQ + 256], BF16, kind="ExternalInput").ap()
    Dv = nc.dram_tensor("Dv", [ND_JOBS, 2, JQ + 256, 65], BF16, kind="ExternalInput").ap()
    sinkd = nc.dram_tensor("sink", [128, 8], F32, kind="ExternalInput").ap()
    masks = nc.dram_tensor("masks", [128, 3, 512], BF16, kind="ExternalInput").ap()
    yB = nc.dram_tensor("yB", [NB_JOBS, JQ, 260], F32, kind="ExternalOutput").ap()
    yD = nc.dram_tensor("yD", [ND_JOBS, JQ, 520], F32, kind="ExternalOutput").ap()

    P = Prog(nc)
    mk = P.sb("mk", [128, 3, 512], BF16)
    esink = P.sb("esink", [128, 8], F32)
    qt = [P.sb("qt%d" % i, [64, 4, JQ], BF16) for i in range(2)]
    kt = [P.sb("kt%d" % i, [64, 4, JQ + 128], BF16) for i in range(2)]
    vt = [P.sb("vt%d" % i, [128, 4, 9, 65], BF16) for i in range(2)]
    dq = [P.sb("dq%d" % i, [64, 2, 8, 4, 128], BF16) for i in range(2)]
    dk = [P.sb("dk%d" % i, [64, 2, JQ + 256], BF16) for i in range(2)]
    dv = [P.sb("dv%d" % i, [128, 2, 10, 65], BF16) for i in range(2)]
    pt = [P.sb("pt%d" % i, [128, 512], BF16) for i in range(4)]
    ob = [P.sb("ob%d" % i, [128, 260], F32) for i in range(2)]
    od = [P.sb("od%d" % i, [128, 260], F32) for i in range(2)]
    den = P.sb("den", [128, 4], F32)
    pss = [P.ps("pss%d" % i, [128, 512]) for i in range(4)]
    pso = [P.ps("pso%d" % i) for i in range(2)]

    P.dma('sync', lambda e: e.dma_start(out=mk[:], in_=masks), 'mk', writes=['mk'])
    P.dma('sync', lambda e: e.dma_start(out=esink[:], in_=sinkd), 'esink', writes=['esink'])
    P.op('scalar', lambda e: e.activation(out=esink[:], in_=esink[:], func=AF.Exp), reads=['esink'], writes=['esink'])
    st = dict(ps=0, pt=0, po=0, ob=0, od=0)

    def nxt(name, n):
        i = st[name]
        st[name] = (i + 1) % n
        return i

    for j in range(NB_JOBS):
        s = j % 2
        P.dma('sync', lambda e, j=j, s=s: e.dma_start(out=qt[s][:], in_=BqT[j]), 'qt%d' % s, writes=['qt%d' % s])
        P.dma('sync', lambda e, j=j, s=s: e.dma_start(out=kt[s][:], in_=BkT[j]), 'kt%d' % s, writes=['kt%d' % s])
        for h in range(4):
            P.dma('sync', lambda e, j=j, s=s, h=h: e.dma_start(out=vt[s][:, h, :, :], in_=Bv[j, h].rearrange("(n p) c -> p n c", p=128)),
                  'vt%d_%d' % (s, h), writes=['vt%d_%d' % (s, h)])
        for i in range(JQ // 128):
            po = nxt('po', 2)
            for hp in range(2):
                pi = nxt('ps', 4)
                ps = pss[pi]
                for hh in range(2):
                    h = hp * 2 + hh
                    for tl in range(2):
                        P.mm(ps[:, (hh * 2 + tl) * 128:(hh * 2 + tl + 1) * 128], kt[s][:, h, (i + tl) * 128:(i + tl + 1) * 128],
                             qt[s][:, h, i * 128:(i + 1) * 128], True, True, reads=['kt%d' % s, 'qt%d' % s], writes=['pss%d' % pi])
                ti = nxt('pt', 4)
                P.op('scalar', lambda e, ti=ti, ps=ps: e.activation(out=pt[ti][:], in_=ps[:], func=AF.Exp, scale=0.125),
                     reads=['pss%d' % pi], writes=['pt%d' % ti])
                P.op('vector', lambda e, ti=ti: e.tensor_tensor(out=pt[ti][:], in0=pt[ti][:], in1=mk[:, 0, :], op=ALU.mult),
                     reads=['pt%d' % ti, 'mk'], writes=['pt%d' % ti])
                for hh in range(2):
                    h = hp * 2 + hh
                    for tl in range(2):
                        P.mm(pso[po][:, h * 65:(h + 1) * 65], pt[ti][:, (hh * 2 + tl) * 128:(hh * 2 + tl + 1) * 128], vt[s][:, h, i + tl, :],
                             tl == 0, tl == 1, reads=['pt%d' % ti, 'vt%d_%d' % (s, h)], writes=['pso%d' % po])
            oi = nxt('ob', 2)
            P.op('scalar', lambda e, oi=oi, po=po: e.copy(out=ob[oi][:], in_=pso[po][:, 0:260]),
                 reads=['pso%d' % po], writes=['ob%d' % oi])
            P.dma('gpsimd', lambda e, oi=oi, j=j, i=i: e.dma_start(out=yB[j, i * 128:(i + 1) * 128, :], in_=ob[oi][:]),
                  'ob%d' % oi, reads=['ob%d' % oi], final=True)
    for j in range(ND_JOBS):
        s = j % 2
        P.dma('sync', lambda e, j=j, s=s: e.dma_start(out=dq[s][:], in_=DqT[j]), 'dq%d' % s, writes=['dq%d' % s])
        P.dma('sync', lambda e, j=j, s=s: e.dma_start(out=dk[s][:], in_=DkT[j]), 'dk%d' % s, writes=['dk%d' % s])
        for g in range(2):
            P.dma('sync', lambda e, j=j, s=s, g=g: e.dma_start(out=dv[s][:, g, :, :], in_=Dv[j, g].rearrange("(n p) c -> p n c", p=128)),
                  'dv%d_%d' % (s, g), writes=['dv%d_%d' % (s, g)])
        for i in range(JQ // 128):
            for g in range(2):
                po = nxt('po', 2)
                tis = []
                for tl in range(3):
                    pi = nxt('ps', 4)
                    ps = pss[pi]
                    P.mm(ps[:], dk[s][:, g, (i + tl) * 128:(i + tl + 1) * 128], dq[s][:, g, i, :, :].rearrange("p h q -> p (h q)"),
                         True, True, reads=['dk%d' % s, 'dq%d' % s], writes=['pss%d' % pi])
                    ti = nxt('pt', 4)
                    tis.append(ti)
                    P.op('scalar', lambda e, ti=ti, ps=ps: e.activation(out=pt[ti][:], in_=ps[:], func=AF.Exp, scale=0.125),
                         reads=['pss%d' % pi], writes=['pt%d' % ti])
                    if tl != 1:
                        mi = 1 if tl == 0 else 2
                        P.op('vector', lambda e, ti=ti, mi=mi: e.tensor_tensor(out=pt[ti][:], in0=pt[ti][:], in1=mk[:, mi, :], op=ALU.mult),
                             reads=['pt%d' % ti, 'mk'], writes=['pt%d' % ti])
                for h in range(4):
                    for tl in range(3):
                        P.mm(pso[po][:, h * 65:(h + 1) * 65], pt[tis[tl]][:, h * 128:(h + 1) * 128], dv[s][:, g, i + tl, :],
                             tl == 0, tl == 2, reads=['pt%d' % tis[tl], 'dv%d_%d' % (s, g)], writes=['pso%d' % po])
                oi = nxt('od', 2)
                P.op('scalar', lambda e, oi=oi, po=po: e.copy(out=od[oi][:], in_=pso[po][:, 0:260]),
                     reads=['pso%d' % po], writes=['od%d' % oi])
                P.dma('gpsimd', lambda e, oi=oi, j=j, i=i, g=g: e.dma_start(out=yD[j, i * 128:(i + 1) * 128, g * 260:(g + 1) * 260], in_=od[oi][:]),
                      'od%d' % oi, reads=['od%d' % oi], final=True)
    P.emit()
    return nc


B_DIL = (1, 4, 16)


def l2a_host_inputs(oA, oB, inp, layer):
    bq = oB[0:768]
    bk = oB[768:1536]
    dqa = oB[1536:2048]
    dka = oB[2048:2176]
    bv = oA[768:1536]
    dva = oA[1536:1664]
    maps = [dict() for _ in range(NCORES)]
    BqT = np.zeros((NCORES, NB_JOBS, 64, 4, JQ), NPBF)
    BkT = np.zeros((NCORES, NB_JOBS, 64, 4, JQ + 128), NPBF)
    Bvv = np.zeros((NCORES, NB_JOBS, 4, JQ + 128, 65), NPBF)
    for g, r in enumerate(B_DIL):
        n = SEQ // r
        def strided(a):
            a = a.reshape(4, 64, n, r).transpose(0, 1, 3, 2)
            out = np.zeros((4, 64, r, n + 128), NPBF)
            out[..., 64:64 + n] = a
            return out
        q = strided(bq[g * 256:(g + 1) * 256])
        k = strided(bk[g * 256:(g + 1) * 256])
        v = strided(bv[g * 256:(g + 1) * 256])
        valid = np.zeros((n + 128,), NPBF)
        valid[64:64 + n] = 1
        ppr = n // JQ
        for jid in range(16):
            rho, pj = jid // ppr, jid % ppr
            c, jl = jid // 2, g * 2 + jid % 2
            BqT[c, jl] = q[:, :, rho, 64 + pj * JQ:64 + (pj + 1) * JQ].transpose(1, 0, 2)
            BkT[c, jl] = k[:, :, rho, pj * JQ:pj * JQ + JQ + 128].transpose(1, 0, 2)
            Bvv[c, jl, :, :, 0:64] = v[:, :, rho, pj * JQ:pj * JQ + JQ + 128].transpose(0, 2, 1)
            Bvv[c, jl, :, :, 64] = valid[pj * JQ:pj * JQ + JQ + 128][None, :]
    DqT = np.zeros((NCORES, ND_JOBS, 64, 2, 8, 4, 128), NPBF)
    DkT = np.zeros((NCORES, ND_JOBS, 64, 2, JQ + 256), NPBF)
    Dvv = np.zeros((NCORES, ND_JOBS, 2, JQ + 256, 65), NPBF)
    kpad = np.zeros((2, 64, SEQ + 256), NPBF)
    kpad[:, :, 128:128 + SEQ] = dka.reshape(2, 64, SEQ)
    vpad = np.zeros((2, 64, SEQ + 256), NPBF)
    vpad[:, :, 128:128 + SEQ] = dva.reshape(2, 64, SEQ)
    valid = np.zeros((SEQ + 256,), NPBF)
    valid[128:128 + SEQ] = 1
    dq4 = dqa.reshape(2, 4, 64, SEQ)
    for jid in range(16):
        c, jl = jid // 2, jid % 2
        t0 = jid * JQ
        DqT[c, jl] = dq4[:, :, :, t0:t0 + JQ].reshape(2, 4, 64, 8, 128).transpose(2, 0, 3, 1, 4)
        DkT[c, jl] = kpad[:, :, t0:t0 + JQ + 256].transpose(1, 0, 2)
        Dvv[c, jl, :, :, 0:64] = vpad[:, :, t0:t0 + JQ + 256].transpose(0, 2, 1)
        Dvv[c, jl, :, :, 64] = valid[t0:t0 + JQ + 256][None, :]
    kk = np.arange(128)
    ge = (kk[:, None] >= kk[None, :]).astype(np.float32)
    le = (kk[:, None] <= kk[None, :]).astype(np.float32)
    masks = np.stack([np.concatenate([ge, le, ge, le], 1), np.concatenate([ge] * 4, 1), np.concatenate([le] * 4, 1)], 1).astype(NPBF)
    sink = np.ascontiguousarray(np.broadcast_to(inp['d_sink'][layer][None, :], (128, 8))).astype(np.float32)
    for c in range(NCORES):
        maps[c] = dict(BqT=BqT[c], BkT=BkT[c], Bv=Bvv[c], DqT=DqT[c], DkT=DkT[c], Dv=Dvv[c], sink=sink, masks=masks)
    return maps


def l2a_host_outputs(res):
    accB = np.zeros((3, SEQ, 4, 65), np.float32)
    yD = np.zeros((SEQ, 520), np.float32)
    for g, r in enumerate(B_DIL):
        n = SEQ // r
        ppr = n // JQ
        tmp = np.zeros((r, n, 260), np.float32)
        for jid in range(16):
            rho, pj = jid // ppr, jid % ppr
            c, jl = jid // 2, g * 2 + jid % 2
            tmp[rho, pj * JQ:(pj + 1) * JQ] = res[c]['yB'][jl]
        accB[g] = tmp.transpose(1, 0, 2).reshape(SEQ, 4, 65)
    for jid in range(16):
        c, jl = jid // 2, jid % 2
        yD[jid * JQ:(jid + 1) * JQ] = res[c]['yD'][jl]
    return accB, yD.reshape(SEQ, 8, 65)


NCH = SEQ // 128


def build_l2m():
    nc = _new_nc()
    qTd = nc.dram_tensor("qT", [64, SEQ], BF16, kind="ExternalInput").ap()
    kTd = nc.dram_tensor("kT", [64, SEQ], BF16, kind="ExternalInput").ap()
    ktd = nc.dram_tensor("ktok", [SEQ, 64], BF16, kind="ExternalInput").ap()
    vtd = nc.dram_tensor("vtok", [SEQ, 64], BF16, kind="ExternalInput").ap()
    igd = nc.dram_tensor("ig", [128, NCH], F32, kind="ExternalInput").ap()
    lfd = nc.dram_tensor("lf", [128, NCH], F32, kind="ExternalInput").ap()
    cst = nc.dram_tensor("cst", [128, 3, 128], F32, kind="ExternalInput").ap()
    yM = nc.dram_tensor("yM", [SEQ, 64], F32, kind="ExternalOutput").ap()

    P = Prog(nc)
    qT = P.sb("qT", [64, SEQ], BF16)
    kT = P.sb("kT", [64, SEQ], BF16)
    kt = P.sb("kt", [128, NCH, 64], BF16)
    vt = P.sb("vt", [128, NCH, 64], BF16)
    ig = P.sb("ig", [128, NCH], F32)
    lf = P.sb("lf", [128, NCH], F32)
    cs = P.sb("cs", [128, 3, 128], F32)
    gS = P.sb("gS", [128, NCH], F32)
    gS2 = P.sb("gS2", [128, NCH], F32)
    eb = P.sb("eb", [128, NCH], F32)
    dec = P.sb("dec", [128, NCH], F32)
    dn = P.sb("dn", [128, NCH], F32)
    Va = P.sb("Va", [128, NCH, 65], BF16)
    Va2 = P.sb("Va2", [128, NCH, 65], BF16)
    raw = P.sb("raw", [128, NCH, 65], F32)
    sm = [P.sb("sm%d" % i, [128, 128], BF16) for i in range(2)]
    C = P.sb("C", [64, 65], F32)
    Cb = P.sb("Cb", [64, 65], BF16)
    psA = P.ps("psA")[:, 0:NCH]
    psB = P.ps("psB")[:, 0:NCH]
    psS = [P.ps("psS%d" % i)[:, 0:128] for i in range(2)]
    psO = [P.ps("psO%d" % i)[:, 0:65] for i in range(2)]
    psC = P.ps("psC")[0:64, 0:65]

    for (t, d, k) in ((qT, qTd, 'qT'), (kT, kTd, 'kT'), (ig, igd, 'ig'), (lf, lfd, 'lf'), (cs, cst, 'cs')):
        P.dma('sync', lambda e, t=t, d=d: e.dma_start(out=t[:], in_=d), k, writes=[k])
    P.dma('sync', lambda e: e.dma_start(out=kt[:], in_=ktd.rearrange("(n p) d -> p n d", p=128)), 'kt', writes=['kt'])
    P.dma('sync', lambda e: e.dma_start(out=vt[:], in_=vtd.rearrange("(n p) d -> p n d", p=128)), 'vt', writes=['vt'])
    P.mm(psA[:], cs[:, 0, :], lf[:], True, True, reads=['cs', 'lf'], writes=['psA'])
    P.mm(psB[:], cs[:, 1, :], lf[:], True, True, reads=['cs', 'lf'], writes=['psB'])
    P.op('vector', lambda e: e.tensor_tensor(out=gS[:], in0=ig[:], in1=psA[:], op=ALU.subtract), reads=['ig', 'psA'], writes=['gS'])
    P.op('scalar', lambda e: e.activation(out=gS[:], in_=gS[:], func=AF.Exp), reads=['gS'], writes=['gS'])
    P.op('scalar', lambda e: e.activation(out=eb[:], in_=psA[:], func=AF.Exp), reads=['psA'], writes=['eb'])
    P.op('scalar', lambda e: e.activation(out=dec[:], in_=psB[:], func=AF.Exp), reads=['psB'], writes=['dec'])
    P.op('vector', lambda e: e.tensor_tensor(out=gS2[:], in0=gS[:], in1=dec[:], op=ALU.mult), reads=['gS', 'dec'], writes=['gS2'])
    P.op('vector', lambda e: e.tensor_copy(out=Va[:, :, 64], in_=gS[:]), reads=['gS'], writes=['Va'])
    P.op('vector', lambda e: e.tensor_copy(out=Va2[:, :, 64], in_=gS2[:]), reads=['gS2'], writes=['Va2'])
    for n in range(NCH):
        P.op('vector', lambda e, n=n: e.tensor_scalar(out=Va[:, n, 0:64], in0=vt[:, n, :], scalar1=gS[:, n:n + 1], scalar2=None, op0=ALU.mult),
             reads=['vt', 'gS', 'Va'], writes=['Va'])
        P.op('gpsimd', lambda e, n=n: e.tensor_scalar(out=Va2[:, n, 0:64], in0=vt[:, n, :], scalar1=gS2[:, n:n + 1], scalar2=None, op0=ALU.mult),
             reads=['vt', 'gS2', 'Va2'], writes=['Va2'])
    P.op('vector', lambda e: e.memset(C[:], 0.0), writes=['C'])
    for n in range(NCH):
        s = n % 2
        sl = slice(n * 128, (n + 1) * 128)
        P.mm(psS[s][:], kT[:, sl], qT[:, sl], True, True, reads=['kT', 'qT'], writes=['psS%d' % s])
        P.op('vector', lambda e, s=s: e.tensor_tensor(out=sm[s][:], in0=psS[s][:], in1=cs[:, 2, :], op=ALU.mult),
             reads=['psS%d' % s, 'cs'], writes=['sm%d' % s])
        P.mm(psO[s][:], sm[s][:], Va[:, n, :], True, n == 0, reads=['sm%d' % s, 'Va'], writes=['psO%d' % s])
        if n > 0:
            P.mm(psO[s][:], qT[:, sl], Cb[:], False, True, reads=['qT', 'Cb'], writes=['psO%d' % s])
        P.op('scalar', lambda e, s=s, n=n: e.copy(out=raw[:, n, :], in_=psO[s][:]), reads=['psO%d' % s], writes=['raw'])
        if n < NCH - 1:
            P.mm(psC[:], kt[:, n, :], Va2[:, n, :], True, True, reads=['kt', 'Va2'], writes=['psC'])
            P.op('vector', lambda e, n=n: e.scalar_tensor_tensor(out=C[:], in0=C[:], scalar=dec[0:64, n:n + 1], in1=psC[:], op0=ALU.mult, op1=ALU.add),
                 reads=['C', 'dec', 'psC'], writes=['C'])
            P.op('scalar', lambda e: e.copy(out=Cb[:], in_=C[:]), reads=['C'], writes=['Cb'])
    P.op('vector', lambda e: e.tensor_tensor(out=dn[:], in0=raw[:, :, 64], in1=eb[:], op=ALU.mult), reads=['raw', 'eb'], writes=['dn'])
    P.op('scalar', lambda e: e.activation(out=dn[:], in_=dn[:], func=AF.Abs), reads=['dn'], writes=['dn'])
    P.op('vector', lambda e: e.tensor_scalar(out=dn[:], in0=dn[:], scalar1=1.0, scalar2=None, op0=ALU.max), reads=['dn'], writes=['dn'])
    P.op('vector', lambda e: e.reciprocal(out=dn[:], in_=dn[:]), reads=['dn'], writes=['dn'])
    P.op('vector', lambda e: e.tensor_tensor(out=dn[:], in0=dn[:], in1=eb[:], op=ALU.mult), reads=['dn', 'eb'], writes=['dn'])
    for n in range(NCH):
        eng = ('vector', 'gpsimd')[n % 2]
        P.op(eng, lambda e, n=n: e.tensor_scalar(out=raw[:, n, 0:64], in0=raw[:, n, 0:64], scalar1=dn[:, n:n + 1], scalar2=None, op0=ALU.mult),
             reads=['raw', 'dn'], writes=['raw'])
    P.dma('sync', lambda e: e.dma_start(out=yM.rearrange("(n p) d -> p n d", p=128), in_=raw[:, :, 0:64]), 'raw', reads=['raw'], final=True)
    P.emit()
    return nc


def seq_consts():
    i = np.arange(128)
    triT = (i[:, None] <= i[None, :]).astype(np.float32)
    ones = np.ones((128, 128), np.float32)
    return np.ascontiguousarray(np.stack([triT, ones, triT], 1))


def l2m_host_inputs(oA, oG):
    cst = seq_consts()
    maps = []
    for u in range(NCORES):
        d, h = u // 4, u % 4
        fl = (lambda a: a[..., ::-1]) if d == 1 else (lambda a: a)
        qT = np.ascontiguousarray(fl(oA[h * 64:(h + 1) * 64]))
        kT = np.ascontiguousarray(fl(oA[256 + h * 64:256 + (h + 1) * 64]))
        vT = fl(oA[512 + h * 64:512 + (h + 1) * 64])
        ig = fl(oG[d * 4 + h])
        lf = fl(oG[32 + d * 4 + h])
        maps.append(dict(qT=qT, kT=kT, ktok=np.ascontiguousarray(kT.T), vtok=np.ascontiguousarray(vT.T),
                         ig=np.ascontiguousarray(ig.reshape(NCH, 128).T), lf=np.ascontiguousarray(lf.reshape(NCH, 128).T), cst=cst))
    return maps


def l2m_host_outputs(res):
    out = np.zeros((2, 256, SEQ), np.float32)
    for u in range(NCORES):
        d, h = u // 4, u % 4
        y = res[u]['yM']
        if d == 1:
            y = y[::-1]
        out[d, h * 64:(h + 1) * 64] = y.T
    return out


def build_l2d():
    nc = _new_nc()
    qTd = nc.dram_tensor("qT", [64, SEQ], BF16, kind="ExternalInput").ap()
    kTd = nc.dram_tensor("kT", [64, SEQ], BF16, kind="ExternalInput").ap()
    ktd = nc.dram_tensor("ktok", [SEQ, 64], BF16, kind="ExternalInput").ap()
    vtd = nc.dram_tensor("vtok", [SEQ, 64], BF16, kind="ExternalInput").ap()
    btd = nc.dram_tensor("beta", [128, NCH], F32, kind="ExternalInput").ap()
    gd = nc.dram_tensor("g", [128, NCH], F32, kind="ExternalInput").ap()
    cst = nc.dram_tensor("cst", [128, 5, 128], F32, kind="ExternalInput").ap()
    yC = nc.dram_tensor("yC", [SEQ, 64], F32, kind="ExternalOutput").ap()

    P = Prog(nc)
    R = 4
    LA = 2
    qT = P.sb("qT", [64, SEQ], BF16)
    kT = P.sb("kT", [64, SEQ], BF16)
    kt = P.sb("kt", [128, NCH, 64], BF16)
    vt = P.sb("vt", [128, NCH, 64], BF16)
    kdec = P.sb("kdec", [128, NCH, 64], BF16)
    oout = P.sb("oout", [128, NCH, 64], F32)
    beta = P.sb("beta", [128, NCH], F32)
    g = P.sb("g", [128, NCH], F32)
    cs = P.sb("cs", [128, 5, 128], F32)
    gc = P.sb("gc", [128, NCH], F32)
    ngc = P.sb("ngc", [128, NCH], F32)
    egc = P.sb("egc", [128, NCH], F32)
    negc = P.sb("negc", [128, NCH], F32)
    egl = P.sb("egl", [128, NCH], F32)
    kfac = P.sb("kfac", [128, NCH], F32)
    Gb = [P.sb("Gb%d" % i, [128, 128], F32) for i in range(2)]
    decL = [P.sb("decL%d" % i, [128, 128], F32) for i in range(2)]
    decT = [P.sb("decT%d" % i, [128, 128], F32) for i in range(2)]
    Ak = [P.sb("Ak%d" % i, [128, 128], F32) for i in range(2)]
    Bk = [P.sb("Bk%d" % i, [128, 128], F32) for i in range(2)]
    Yk = [P.sb("Yk%d" % i, [128, 128], F32) for i in range(2)]
    TT = [P.sb("TT%d" % i, [128, 128], F32) for i in range(R)]
    aT = [P.sb("aT%d" % i, [128, 128], BF16) for i in range(R)]
    Z = P.sb("Z", [128, 64], F32)
    vn = P.sb("vn", [128, 64], BF16)
    o2s = P.sb("o2s", [128, 64], F32)
    S = P.sb("S", [64, 64], F32)
    Sb = P.sb("Sb", [64, 64], BF16)
    psG_ = P.ps("psG")
    psLT_ = P.ps("psLT")
    psKQ_ = P.ps("psKQ")
    psP_ = [P.ps("psP%d" % i) for i in range(2)]
    psY = P.ps("psY")[:, 0:128]
    psRA = P.ps("psRA")
    psRB = P.ps("psRB")
    H0, H1 = slice(0, 128), slice(128, 256)
    for (t, d, k) in ((qT, qTd, 'qT'), (kT, kTd, 'kT'), (beta, btd, 'beta'), (g, gd, 'g'), (cs, cst, 'cs')):
        P.dma('sync', lambda e, t=t, d=d: e.dma_start(out=t[:], in_=d), k, writes=[k])
    P.dma('sync', lambda e: e.dma_start(out=kt[:], in_=ktd.rearrange("(n p) d -> p n d", p=128)), 'kt', writes=['kt'])
    P.dma('sync', lambda e: e.dma_start(out=vt[:], in_=vtd.rearrange("(n p) d -> p n d", p=128)), 'vt', writes=['vt'])
    TRI, ONES, IDN, M1, M2 = (cs[:, i, :] for i in range(5))
    P.mm(psG_[:, H0], TRI, g[:], True, True, reads=['cs', 'g'], writes=['psG'])
    P.mm(psG_[:, H1], ONES, g[:], True, True, reads=['cs', 'g'], writes=['psG'])
    P.op('vector', lambda e: e.tensor_copy(out=gc[:], in_=psG_[:, H0]), reads=['psG'], writes=['gc'])
    P.op('vector', lambda e: e.tensor_scalar(out=ngc[:], in0=psG_[:, H0], scalar1=-1.0, scalar2=None, op0=ALU.mult), reads=['psG'], writes=['ngc'])
    P.op('scalar', lambda e: e.activation(out=egc[:], in_=psG_[:, H0], func=AF.Exp), reads=['psG'], writes=['egc'])
    P.op('vector', lambda e: e.tensor_scalar(out=negc[:], in0=egc[:], scalar1=-1.0, scalar2=None, op0=ALU.mult), reads=['egc'], writes=['negc'])
    P.op('scalar', lambda e: e.activation(out=egl[:], in_=psG_[:, H1], func=AF.Exp), reads=['psG'], writes=['egl'])
    P.op('vector', lambda e: e.tensor_tensor(out=kfac[:], in0=psG_[:, H1], in1=gc[:], op=ALU.subtract), reads=['psG', 'gc'], writes=['kfac'])
    P.op('scalar', lambda e: e.activation(out=kfac[:], in_=kfac[:], func=AF.Exp), reads=['kfac'], writes=['kfac'])
    for n in range(NCH):
        P.op('gpsimd', lambda e, n=n: e.tensor_scalar(out=kdec[:, n, :], in0=kt[:, n, :], scalar1=kfac[:, n:n + 1], scalar2=None, op0=ALU.mult),
             reads=['kt', 'kfac', 'kdec'], writes=['kdec'])
    P.op('vector', lambda e: e.memset(S[:], 0.0), writes=['S'])

    def tphase(n):
        s2 = n % 2
        r = n % R
        sl = slice(n * 128, (n + 1) * 128)
        P.op('gpsimd', lambda e: e.tensor_scalar(out=Gb[s2][:], in0=ONES, scalar1=g[:, n:n + 1], scalar2=None, op0=ALU.mult),
             reads=['cs', 'g'], writes=['Gb%d' % s2])
        P.mm(psLT_[:, H0], Gb[s2][:], TRI, True, False, reads=['Gb%d' % s2, 'cs'], writes=['psLT'])
        P.mm(psLT_[:, H0], IDN, M1, False, True, reads=['cs'], writes=['psLT'])
        P.mm(psLT_[:, H1], Gb[s2][:], TRI, True, False, reads=['Gb%d' % s2, 'cs'], writes=['psLT'])
        P.mm(psLT_[:, H1], IDN, M2, False, True, reads=['cs'], writes=['psLT'])
        P.op('scalar', lambda e: e.activation(out=decL[s2][:], in_=psLT_[:, H0], func=AF.Exp, bias=gc[:, n:n + 1], scale=-1.0),
             reads=['psLT', 'gc'], writes=['decL%d' % s2])
        P.op('scalar', lambda e: e.activation(out=decT[s2][:], in_=psLT_[:, H1], func=AF.Exp, bias=ngc[:, n:n + 1], scale=1.0),
             reads=['psLT', 'ngc'], writes=['decT%d' % s2])
        P.mm(psKQ_[:, H0], kT[:, sl], kT[:, sl], True, True, reads=['kT'], writes=['psKQ'])
        P.mm(psKQ_[:, H1], kT[:, sl], qT[:, sl], True, True, reads=['kT', 'qT'], writes=['psKQ'])
        P.op('vector', lambda e: e.scalar_tensor_tensor(out=Ak[0][:], in0=psKQ_[:, H0], scalar=beta[:, n:n + 1], in1=decL[s2][:], op0=ALU.mult, op1=ALU.mult),
             reads=['psKQ', 'beta', 'decL%d' % s2], writes=['Ak0'])
        P.op('vector', lambda e: e.tensor_tensor(out=aT[r][:], in0=psKQ_[:, H1], in1=decT[s2][:], op=ALU.mult),
             reads=['psKQ', 'decT%d' % s2], writes=['aT%d' % r])
        P.mm(psP_[0][:, H1], Ak[0][:], IDN, True, True, reads=['Ak0', 'cs'], writes=['psP0'])
        P.op('scalar', lambda e: e.copy(out=Bk[0][:], in_=psP_[0][:, H1]), reads=['psP0'], writes=['Bk0'])
        P.op('vector', lambda e: e.tensor_tensor(out=Yk[0][:], in0=IDN, in1=psP_[0][:, H1], op=ALU.subtract), reads=['psP0', 'cs'], writes=['Yk0'])
        for k in range(1, 7):
            a, b = (k - 1) % 2, k % 2
            pp = psP_[k % 2]
            P.mm(pp[:, H0], Bk[a][:], Ak[a][:], True, True, reads=['Bk%d' % a, 'Ak%d' % a], writes=['psP%d' % (k % 2)])
            P.op('scalar', lambda e, b=b, pp=pp: e.copy(out=Ak[b][:], in_=pp[:, H0]), reads=['psP%d' % (k % 2)], writes=['Ak%d' % b])
            if k < 6:
                P.mm(pp[:, H1], Ak[a][:], Bk[a][:], True, True, reads=['Bk%d' % a, 'Ak%d' % a], writes=['psP%d' % (k % 2)])
                P.op('vector', lambda e, b=b, pp=pp: e.tensor_copy(out=Bk[b][:], in_=pp[:, H1]), reads=['psP%d' % (k % 2)], writes=['Bk%d' % b])
            P.mm(psY[:], Ak[b][:], Yk[a][:], True, True, reads=['Ak%d' % b, 'Yk%d' % a], writes=['psY'])
            P.op('vector', lambda e, a=a, b=b: e.tensor_tensor(out=Yk[b][:], in0=Yk[a][:], in1=psY[:], op=ALU.add),
                 reads=['psY', 'Yk%d' % a], writes=['Yk%d' % b])
        P.op('gpsimd', lambda e: e.tensor_scalar(out=TT[r][:], in0=Yk[0][:], scalar1=beta[:, n:n + 1], scalar2=None, op0=ALU.mult),
             reads=['Yk0', 'beta'], writes=['TT%d' % r])

    def recur(n):
        r = n % R
        sl = slice(n * 128, (n + 1) * 128)
        if n > 0:
            P.mm(psRA[:, 0:64], kT[:, sl], Sb[:], True, True, reads=['kT', 'Sb'], writes=['psRA'])
            P.mm(psRA[:, 64:128], qT[:, sl], Sb[:], True, True, reads=['qT', 'Sb'], writes=['psRA'])
            P.op('vector', lambda e: e.scalar_tensor_tensor(out=Z[:], in0=psRA[:, 0:64], scalar=negc[:, n:n + 1], in1=vt[:, n, :], op0=ALU.mult, op1=ALU.add),
                 reads=['psRA', 'negc', 'vt'], writes=['Z'])
            P.op('vector', lambda e: e.tensor_scalar(out=o2s[:], in0=psRA[:, 64:128], scalar1=egc[:, n:n + 1], scalar2=None, op0=ALU.mult),
                 reads=['psRA', 'egc'], writes=['o2s'])
        else:
            P.op('vector', lambda e: e.tensor_copy(out=Z[:], in_=vt[:, n, :]), reads=['vt'], writes=['Z'])
            P.op('vector', lambda e: e.memset(o2s[:], 0.0), writes=['o2s'])
        P.mm(psRB[:, 0:64], TT[r][:], Z[:], True, True, reads=['TT%d' % r, 'Z'], writes=['psRB'])
        P.op('scalar', lambda e: e.copy(out=vn[:], in_=psRB[:, 0:64]), reads=['psRB'], writes=['vn'])
        if n < NCH - 1:
            P.mm(psRB[0:64, 128:192], kdec[:, n, :], vn[:], True, True, reads=['kdec', 'vn'], writes=['psRB'])
        P.mm(psRB[:, 64:128], aT[r][:], vn[:], True, True, reads=['aT%d' % r, 'vn'], writes=['psRB'])
        if n < NCH - 1:
            P.op('vector', lambda e: e.scalar_tensor_tensor(out=S[:], in0=S[:], scalar=egl[0:64, n:n + 1], in1=psRB[0:64, 128:192], op0=ALU.mult, op1=ALU.add),
                 reads=['S', 'egl', 'psRB'], writes=['S'])
            P.op('scalar', lambda e: e.copy(out=Sb[:], in_=S[:]), reads=['S'], writes=['Sb'])
        P.op('vector', lambda e: e.tensor_tensor(out=oout[:, n, :], in0=o2s[:], in1=psRB[:, 64:128], op=ALU.add),
             reads=['psRB', 'o2s'], writes=['oout'])

    for n in range(NCH + LA):
        if n < NCH:
            tphase(n)
        if n >= LA:
            recur(n - LA)
    P.dma('sync', lambda e: e.dma_start(out=yC.rearrange("(n p) d -> p n d", p=128), in_=oout[:]), 'oout', reads=['oout'], final=True)
    P.emit()
    return nc


def l2d_consts():
    i = np.arange(128)
    triT = (i[:, None] <= i[None, :]).astype(np.float32)
    ones = np.ones((128, 128), np.float32)
    ident = np.eye(128, dtype=np.float32)
    BIG = 1.0e5
    m1 = np.where(i[None, :] < i[:, None], 0.0, BIG).astype(np.float32)
    m2 = np.where(i[None, :] >= i[:, None], 0.0, -BIG).astype(np.float32)
    return np.ascontiguousarray(np.stack([triT, ones, ident, m1, m2], 1))


def l2d_host_inputs(oC, oG):
    cst = l2d_consts()
    maps = []
    for u in range(NCORES):
        d, h = u // 4, u % 4
        fl = (lambda a: a[..., ::-1]) if d == 1 else (lambda a: a)
        qT = np.ascontiguousarray(fl(oC[h * 64:(h + 1) * 64]))
        kT = np.ascontiguousarray(fl(oC[256 + h * 64:256 + (h + 1) * 64]))
        vT = fl(oC[512 + h * 64:512 + (h + 1) * 64])
        bt = fl(oG[64 + d * 4 + h])
        gg = fl(oG[96 + d * 4 + h])
        maps.append(dict(qT=qT, kT=kT, ktok=np.ascontiguousarray(kT.T), vtok=np.ascontiguousarray(vT.T),
                         beta=np.ascontiguousarray(bt.reshape(NCH, 128).T), g=np.ascontiguousarray(gg.reshape(NCH, 128).T), cst=cst))
    return maps


def l2d_host_outputs(res):
    out = np.zeros((2, 256, SEQ), np.float32)
    for u in range(NCORES):
        d, h = u // 4, u % 4
        y = res[u]['yC']
        if d == 1:
            y = y[::-1]
        out[d, h * 64:(h + 1) * 64] = y.T
    return out


CAST_TF = 4096
CAST_NT = 16


def build_l0():
    nc = _new_nc()
    src = nc.dram_tensor("src", [128, CAST_NT * CAST_TF], F32, kind="ExternalInput").ap()
    dst = nc.dram_tensor("dst", [128, CAST_NT * CAST_TF], BF16, kind="ExternalOutput").ap()
    P = Prog(nc)
    a = [P.sb("ca%d" % i, [128, CAST_TF], F32) for i in range(3)]
    b = [P.sb("cb%d" % i, [128, CAST_TF], BF16) for i in range(3)]
    for t in range(CAST_NT):
        s = t % 3
        sl = slice(t * CAST_TF, (t + 1) * CAST_TF)
        P.dma('sync', lambda e, s=s, sl=sl: e.dma_start(out=a[s][:], in_=src[:, sl]), 'ca%d' % s, writes=['ca%d' % s])
        eng = ('vector', 'gpsimd', 'scalar')[t % 3]
        if eng == 'scalar':
            P.op(eng, lambda e, s=s: e.copy(out=b[s][:], in_=a[s][:]), reads=['ca%d' % s], writes=['cb%d' % s])
        else:
            P.op(eng, lambda e, s=s: e.tensor_copy(out=b[s][:], in_=a[s][:]), reads=['ca%d' % s], writes=['cb%d' % s])
        P.dma('gpsimd', lambda e, s=s, sl=sl: e.dma_start(out=dst[:, sl], in_=b[s][:]), 'cb%d' % s, reads=['cb%d' % s], final=True)
    P.emit()
    return nc


def cast_weights_on_device(arrs, nc0, launch):
    names = list(arrs.keys())
    flat = np.concatenate([np.ascontiguousarray(arrs[n]).reshape(-1) for n in names])
    per_launch = NCORES * 128 * CAST_NT * CAST_TF
    nl = (flat.size + per_launch - 1) // per_launch
    buf = np.zeros(nl * per_launch, np.float32)
    buf[:flat.size] = flat
    buf = buf.reshape(nl, NCORES, 128, CAST_NT * CAST_TF)
    out = np.zeros(buf.shape, NPBF)
    for i in range(nl):
        res = launch(nc0, [dict(src=buf[i, c]) for c in range(NCORES)])
        for c in range(NCORES):
            out[i, c] = res[c]['dst']
    out = out.reshape(-1)
    ret = {}
    off = 0
    for n in names:
        sz = arrs[n].size
        ret[n] = out[off:off + sz].reshape(arrs[n].shape)
        off += sz
    return ret


NPASS = TC // 512
PC_ANW, PC_CNW, PC_SINK, PC_LN1W, PC_LN1B, PC_LN2W, PC_LN2B = 0, 2, 3, 4, 12, 20, 28
CS_BONES, CS_ONES, CS_IDENT, CS_E4, CS_E8, CS_EE = 0, 128, 256, 384, 640, 1152
CS_N = 1152 + 1024


def build_l3(moe, dbg=False):
    nc = _new_nc()
    T = TC
    E = 8 if moe else 1
    J = 11 if moe else 22
    xT = nc.dram_tensor("xT", [D, T], F32, kind="ExternalInput").ap()
    hA = nc.dram_tensor("hA", [2, 256, T], F32, kind="ExternalInput").ap()
    oCd = nc.dram_tensor("oCd", [2, 256, T], F32, kind="ExternalInput").ap()
    accB = nc.dram_tensor("accB", [3, 256, T], F32, kind="ExternalInput").ap()
    lB = nc.dram_tensor("lB", [3, 4, T], F32, kind="ExternalInput").ap()
    accD = nc.dram_tensor("accD", [512, T], F32, kind="ExternalInput").ap()
    lD = nc.dram_tensor("lD", [8, T], F32, kind="ExternalInput").ap()
    wg = nc.dram_tensor("wg", [9, 128, 8 * 512], BF16, kind="ExternalInput").ap()
    wbr = nc.dram_tensor("wbr", [128, 10 * 1024], BF16, kind="ExternalInput").ap()
    wout = nc.dram_tensor("wout", [128, 8 * 1024], BF16, kind="ExternalInput").ap()
    pcol = nc.dram_tensor("pcol", [128, 40], F32, kind="ExternalInput").ap()
    cst3 = nc.dram_tensor("cst3", [128, CS_N], F32, kind="ExternalInput").ap()
    w1d = nc.dram_tensor("w1p", [E, J, 128, 8 * 128], BF16, kind="ExternalInput").ap()
    w3d = nc.dram_tensor("w3p", [E, J, 128, 8 * 128], BF16, kind="ExternalInput").ap()
    w2d = nc.dram_tensor("w2p", [E, 8, 128, J * 128], BF16, kind="ExternalInput").ap()
    if moe:
        rtd = nc.dram_tensor("router", [128, 8 * 8], F32, kind="ExternalInput").ap()
    xo = nc.dram_tensor("xo", [D, T], F32, kind="ExternalOutput").ap()
    if dbg:
        dbg_g = nc.dram_tensor("dbg_g", [8, T], F32, kind="ExternalOutput").ap()
        dbg_x = nc.dram_tensor("dbg_x", [D, T], F32, kind="ExternalOutput").ap()
        dbg_l = nc.dram_tensor("dbg_l", [NPASS, 128, 64], F32, kind="ExternalOutput").ap()

    P = Prog(nc)
    xr = P.sb("xr", [128, 8, 512], F32)
    xb = P.sb("xb", [128, 8, 512], BF16)
    yb = P.sb("yb", [128, 10, 512], BF16)
    mg = P.sb("mg", [128, 8, 512], BF16)
    hT = P.sb("hT", [128, 22, 512], BF16)
    wbr_t = P.sb("wbr_t", [128, 10, 1024], BF16)
    wout_t = P.sb("wout_t", [128, 8, 1024], BF16)
    wgr = [P.sb("wgr%d" % i, [128, 8, 512], BF16) for i in range(2)]
    w1r = [P.sb("w1r%d" % i, [128, 8, 128], BF16) for i in range(4)]
    w3r = [P.sb("w3r%d" % i, [128, 8, 128], BF16) for i in range(4)]
    w2r = [P.sb("w2r%d" % i, [128, J, 128], BF16) for i in range(3)]
    pc = P.sb("pc", [128, 40], F32)
    cs = P.sb("cs", [128, CS_N], F32)
    esink = P.sb("esink", [8, 1], F32)
    eps5 = P.sb("eps5", [128, 1], F32)
    eps6 = P.sb("eps6", [128, 1], F32)
    class _TT(dict):
        ALIAS = {'ts0': 'td0', 'ts1': 'td1', 'tsq0': 'tm0', 'tsq1': 'tm1', 'tn0': 'tacc0', 'tn1': 'tacc1'}
    tt = _TT({n: P.sb(n, [128, 512], F32) for n in ('ta', 'tb', 'tc', 'td0', 'td1', 'tm0', 'tm1', 'tacc0', 'tacc1', 'tmean', 'tmsq', 'trstd')})
    l4 = [P.sb("l4_%d" % i, [4, 512], F32) for i in range(3)]
    l8 = P.sb("l8", [8, 512], F32)
    if moe:
        rt = P.sb("rt", [128, 8, 8], F32)
        lg = P.sb("lg", [128, 4, 8], F32)
        lg2 = P.sb("lg2", [128, 8], F32)
        gsel = P.sb("gsel", [128, 8], F32)
        m1 = P.sb("m1", [128, 4], F32)
        gts = P.sb("gts", [128, 4, 8], F32)
        gT = P.sb("gT", [8, 512], F32)
        gb = P.sb("gb", [128, 8, 512], F32)
    banks = [P.ps("bk%d" % i) for i in range(8)]
    st = dict(b=0, w1=0, w3=0, w2=0, wg=0, td=0, tm=0, tacc=0, tsq=0, tn=0, ts=0)

    def nb():
        i = st['b']
        st['b'] = (i + 1) % 8
        return banks[i], 'bk%d' % i

    def rot(name, n):
        i = st[name]
        st[name] = (i + 1) % n
        return i

    BONES = cs[:, CS_BONES:CS_BONES + 128]
    ONES = cs[:, CS_ONES:CS_ONES + 128]
    IDN = cs[:, CS_IDENT:CS_IDENT + 128]

    P.dma('sync', lambda e: e.dma_start(out=pc[:], in_=pcol), 'pc', writes=['pc'])
    P.dma('sync', lambda e: e.dma_start(out=cs[:], in_=cst3), 'cs', writes=['cs'])
    P.dma('sync', lambda e: e.dma_start(out=wbr_t[:].rearrange("p k c -> p (k c)"), in_=wbr), 'wbr', writes=['wbr'])
    P.dma('sync', lambda e: e.dma_start(out=wout_t[:].rearrange("p k c -> p (k c)"), in_=wout), 'wout', writes=['wout'])
    if moe:
        P.dma('sync', lambda e: e.dma_start(out=rt[:].rearrange("p k c -> p (k c)"), in_=rtd), 'rt', writes=['rt'])
    P.op('vector', lambda e: e.memset(eps5[:], 1e-5), writes=['eps5'])
    P.op('vector', lambda e: e.memset(eps6[:], 1e-6), writes=['eps6'])
    P.op('scalar', lambda e: e.activation(out=esink[:], in_=pc[0:8, PC_SINK:PC_SINK + 1], func=AF.Exp), reads=['pc'], writes=['esink'])

    def proj(wt, wkey, col, rhs_t, rhs_key):
        ps, pk = nb()
        for k in range(8):
            P.mm(ps[:, 0:512], wt[:, k, col:col + 128], rhs_t[:, k, :], k == 0, k == 7, reads=[wkey, rhs_key], writes=[pk])
        return ps, pk

    def load_wg(gi):
        s = rot('wg', 2)
        P.dma('sync', lambda e: e.dma_start(out=wgr[s][:].rearrange("p k c -> p (k c)"), in_=wg[gi]), 'wgr%d' % s, writes=['wgr%d' % s])
        return wgr[s], 'wgr%d' % s

    def rsqrt_from_psum(ps, pk, scale, epst, ekey, out_t, okey):
        P.op('scalar', lambda e: e.activation(out=out_t[:], in_=ps[:, 0:512], func=AF.Sqrt, bias=epst[:, 0:1], scale=scale),
             reads=[pk, ekey], writes=[okey])
        P.op('vector', lambda e: e.reciprocal(out=out_t[:], in_=out_t[:]), reads=[okey], writes=[okey])

    def layer_norm(wc, bc, make_bf16, out_dram_t0=None):
        s1, k1 = nb()
        for k in range(8):
            P.mm(s1[:, 0:512], ONES, xr[:, k, :], k == 0, k == 7, reads=['cs', 'xr%d' % k], writes=[k1])
        s2, k2 = nb()
        for k in range(8):
            i = rot('tsq', 2)
            tq = tt['tm%d' % i]
            P.op('scalar', lambda e, tq=tq, k=k: e.activation(out=tq[:], in_=xr[:, k, :], func=AF.Square), reads=['xr%d' % k], writes=['tm%d' % i])
            P.mm(s2[:, 0:512], ONES, tq[:], k == 0, k == 7, reads=['cs', 'tm%d' % i], writes=[k2])
        tmean, tmsq, trstd = tt['tmean'], tt['tmsq'], tt['trstd']
        P.op('vector', lambda e: e.tensor_scalar(out=tmean[:], in0=s1[:, 0:512], scalar1=1.0 / D, scalar2=None, op0=ALU.mult), reads=[k1], writes=['tmean'])
        P.op('gpsimd', lambda e: e.tensor_tensor(out=tmsq[:], in0=tmean[:], in1=tmean[:], op=ALU.mult), reads=['tmean'], writes=['tmsq'])
        P.op('vector', lambda e: e.scalar_tensor_tensor(out=trstd[:], in0=s2[:, 0:512], scalar=1.0 / D, in1=tmsq[:], op0=ALU.mult, op1=ALU.subtract),
             reads=[k2, 'tmsq'], writes=['trstd'])
        P.op('scalar', lambda e: e.activation(out=trstd[:], in_=trstd[:], func=AF.Sqrt, bias=eps5[:, 0:1], scale=1.0), reads=['trstd', 'eps5'], writes=['trstd'])
        P.op('vector', lambda e: e.reciprocal(out=trstd[:], in_=trstd[:]), reads=['trstd'], writes=['trstd'])
        for k in range(8):
            i = rot('tn', 2)
            tn = tt['tacc%d' % i]
            P.op('vector', lambda e, tn=tn, k=k: e.tensor_tensor(out=tn[:], in0=xr[:, k, :], in1=tmean[:], op=ALU.subtract),
                 reads=['xr%d' % k, 'tmean'], writes=['tacc%d' % i])
            P.op('gpsimd', lambda e, tn=tn: e.tensor_tensor(out=tn[:], in0=tn[:], in1=trstd[:], op=ALU.mult), reads=['tacc%d' % i, 'trstd'], writes=['tacc%d' % i])
            P.op('scalar', lambda e, tn=tn, k=k: e.activation(out=xr[:, k, :], in_=tn[:], func=AF.Identity, bias=pc[:, bc + k:bc + k + 1], scale=pc[:, wc + k:wc + k + 1]),
                 reads=['tacc%d' % i, 'pc'], writes=['xr%d' % k])
            if make_bf16:
                P.op('gpsimd', lambda e, k=k: e.tensor_copy(out=xb[:, k, :], in_=xr[:, k, :]), reads=['xr%d' % k], writes=['xb'])
            if out_dram_t0 is not None:
                t0 = out_dram_t0
                P.dma('gpsimd', lambda e, k=k, t0=t0: e.dma_start(out=xo[k * 128:(k + 1) * 128, t0:t0 + 512], in_=xr[:, k, :]),
                      'xr%d' % k, reads=['xr%d' % k], final=True)

    xrk = ['xr%d' % k for k in range(8)]
    for ps_i in range(NPASS):
        t0 = ps_i * 512
        tsl = slice(t0, t0 + 512)
        for k in range(8):
            P.dma('sync', lambda e, k=k, tsl=tsl: e.dma_start(out=xr[:, k, :], in_=xT[k * 128:(k + 1) * 128, tsl]), 'xr%d' % k, writes=['xr%d' % k])
            P.op(('vector', 'gpsimd')[k % 2], lambda e, k=k: e.tensor_copy(out=xb[:, k, :], in_=xr[:, k, :]), reads=['xr%d' % k], writes=['xb'])
        ta, tb, tc = tt['ta'], tt['tb'], tt['tc']
        wg8, wg8k = load_wg(8)
        for c in range(2):
            rs = slice(c * 128, (c + 1) * 128)
            P.dma('sync', lambda e, rs=rs, tsl=tsl: e.dma_start(out=ta[:], in_=hA[0, rs, tsl]), 'ta', writes=['ta'])
            P.dma('sync', lambda e, rs=rs, tsl=tsl: e.dma_start(out=tb[:], in_=hA[1, rs, tsl]), 'tb', writes=['tb'])
            P.op('vector', lambda e: e.tensor_tensor(out=ta[:], in0=ta[:], in1=tb[:], op=ALU.add), reads=['ta', 'tb'], writes=['ta'])
            ps, pk = nb()
            P.mm(ps[:, 0:512], BONES, ta[:], True, True, reads=['cs', 'ta'], writes=[pk])
            P.op('vector', lambda e, ps=ps: e.scalar_tensor_tensor(out=tb[:], in0=ps[:, 0:512], scalar=-1.0 / 64, in1=ta[:], op0=ALU.mult, op1=ALU.add),
                 reads=[pk, 'ta'], writes=['tb'])
            P.op('gpsimd', lambda e: e.tensor_tensor(out=tc[:], in0=tb[:], in1=tb[:], op=ALU.mult), reads=['tb'], writes=['tc'])
            ps2, pk2 = nb()
            P.mm(ps2[:, 0:512], BONES, tc[:], True, True, reads=['cs', 'tc'], writes=[pk2])
            rsqrt_from_psum(ps2, pk2, 1.0 / 64, eps5, 'eps5', tc, 'tc')
            P.op('gpsimd', lambda e: e.tensor_tensor(out=tb[:], in0=tb[:], in1=tc[:], op=ALU.mult), reads=['tb', 'tc'], writes=['tb'])
            psg, pkg = proj(wg8, wg8k, c * 128, xb, 'xb')
            i = rot('td', 2)
            td = tt['td%d' % i]
            P.op('scalar', lambda e, td=td, psg=psg: e.activation(out=td[:], in_=psg[:, 0:512], func=AF.Sigmoid), reads=[pkg], writes=['td%d' % i])
            P.op('vector', lambda e, td=td, c=c: e.scalar_tensor_tensor(out=yb[:, c, :], in0=tb[:], scalar=pc[:, PC_ANW + c:PC_ANW + c + 1], in1=td[:],
                                                                        op0=ALU.mult, op1=ALU.mult), reads=['tb', 'pc', 'td%d' % i], writes=['yb'])
        for c in range(2):
            rs = slice(c * 128, (c + 1) * 128)
            P.dma('sync', lambda e, rs=rs, tsl=tsl: e.dma_start(out=ta[:], in_=oCd[0, rs, tsl]), 'ta', writes=['ta'])
            P.dma('sync', lambda e, rs=rs, tsl=tsl: e.dma_start(out=tb[:], in_=oCd[1, rs, tsl]), 'tb', writes=['tb'])
            P.op('vector', lambda e: e.tensor_tensor(out=ta[:], in0=ta[:], in1=tb[:], op=ALU.add), reads=['ta', 'tb'], writes=['ta'])
            P.op('gpsimd', lambda e: e.tensor_tensor(out=tc[:], in0=ta[:], in1=ta[:], op=ALU.mult), reads=['ta'], writes=['tc'])
            ps2, pk2 = nb()
            P.mm(ps2[:, 0:512], BONES, tc[:], True, True, reads=['cs', 'tc'], writes=[pk2])
            rsqrt_from_psum(ps2, pk2, 1.0 / 64, eps6, 'eps6', tc, 'tc')
            P.op('gpsimd', lambda e: e.tensor_tensor(out=tb[:], in0=ta[:], in1=tc[:], op=ALU.mult), reads=['ta', 'tc', 'tb'], writes=['tb'])
            psg, pkg = proj(wg8, wg8k, 256 + c * 128, xb, 'xb')
            i = rot('td', 2)
            td = tt['td%d' % i]
            P.op('scalar', lambda e, td=td, psg=psg: e.activation(out=td[:], in_=psg[:, 0:512], func=AF.Silu), reads=[pkg], writes=['td%d' % i])
            P.op('vector', lambda e, td=td, c=c: e.scalar_tensor_tensor(out=yb[:, 4 + c, :], in0=tb[:], scalar=pc[:, PC_CNW:PC_CNW + 1], in1=td[:],
                                                                        op0=ALU.mult, op1=ALU.mult), reads=['tb', 'pc', 'td%d' % i], writes=['yb'])
        for gi in range(3):
            P.dma('sync', lambda e, gi=gi, tsl=tsl: e.dma_start(out=l4[gi][:], in_=lB[gi, :, tsl]), 'l4_%d' % gi, writes=['l4_%d' % gi])
        P.op('vector', lambda e: e.tensor_tensor(out=l4[0][:], in0=l4[0][:], in1=l4[1][:], op=ALU.add), reads=['l4_0', 'l4_1'], writes=['l4_0'])
        P.op('vector', lambda e: e.tensor_tensor(out=l4[0][:], in0=l4[0][:], in1=l4[2][:], op=ALU.add), reads=['l4_0', 'l4_2'], writes=['l4_0'])
        P.op('vector', lambda e: e.reciprocal(out=l4[0][:], in_=l4[0][:]), reads=['l4_0'], writes=['l4_0'])
        for c in range(2):
            rs = slice(c * 128, (c + 1) * 128)
            P.dma('sync', lambda e, rs=rs, tsl=tsl: e.dma_start(out=ta[:], in_=accB[0, rs, tsl]), 'ta', writes=['ta'])
            P.dma('sync', lambda e, rs=rs, tsl=tsl: e.dma_start(out=tb[:], in_=accB[1, rs, tsl]), 'tb', writes=['tb'])
            P.dma('sync', lambda e, rs=rs, tsl=tsl: e.dma_start(out=tc[:], in_=accB[2, rs, tsl]), 'tc', writes=['tc'])
            P.op('vector', lambda e: e.tensor_tensor(out=ta[:], in0=ta[:], in1=tb[:], op=ALU.add), reads=['ta', 'tb'], writes=['ta'])
            P.op('gpsimd', lambda e: e.tensor_tensor(out=ta[:], in0=ta[:], in1=tc[:], op=ALU.add), reads=['ta', 'tc'], writes=['ta'])
            ps, pk = nb()
            P.mm(ps[:, 0:512], cs[0:4, CS_E4 + c * 128:CS_E4 + (c + 1) * 128], l4[0][:], True, True, reads=['cs', 'l4_0'], writes=[pk])
            P.op('vector', lambda e, ps=ps, c=c: e.tensor_tensor(out=yb[:, 2 + c, :], in0=ta[:], in1=ps[:, 0:512], op=ALU.mult), reads=['ta', pk], writes=['yb'])
        P.dma('sync', lambda e, tsl=tsl: e.dma_start(out=l8[:], in_=lD[:, tsl]), 'l8', writes=['l8'])
        P.op('vector', lambda e: e.tensor_scalar(out=l8[:], in0=l8[:], scalar1=esink[:, 0:1], scalar2=None, op0=ALU.add), reads=['l8', 'esink'], writes=['l8'])
        P.op('vector', lambda e: e.reciprocal(out=l8[:], in_=l8[:]), reads=['l8'], writes=['l8'])
        for c in range(4):
            rs = slice(c * 128, (c + 1) * 128)
            P.dma('sync', lambda e, rs=rs, tsl=tsl: e.dma_start(out=ta[:], in_=accD[rs, tsl]), 'ta', writes=['ta'])
            ps, pk = nb()
            P.mm(ps[:, 0:512], cs[0:8, CS_E8 + c * 128:CS_E8 + (c + 1) * 128], l8[:], True, True, reads=['cs', 'l8'], writes=[pk])
            P.op('vector', lambda e, ps=ps, c=c: e.tensor_tensor(out=yb[:, 6 + c, :], in0=ta[:], in1=ps[:, 0:512], op=ALU.mult), reads=['ta', pk], writes=['yb'])
        BR = ((0, 1), (2, 3), (4, 5), (6, 7, 8, 9))
        for m in range(8):
            wgm, wgk = load_wg(m)
            ai = rot('tacc', 2)
            tacc = tt['tacc%d' % ai]
            for b in range(4):
                psg, pkg = proj(wgm, wgk, b * 128, xb, 'xb')
                i = rot('td', 2)
                td = tt['td%d' % i]
                P.op('scalar', lambda e, td=td, psg=psg: e.activation(out=td[:], in_=psg[:, 0:512], func=AF.Sigmoid), reads=[pkg], writes=['td%d' % i])
                psb, pkb = nb()
                for kk, kc in enumerate(BR[b]):
                    P.mm(psb[:, 0:512], wbr_t[:, kc, m * 128:(m + 1) * 128], yb[:, kc, :], kk == 0, kk == len(BR[b]) - 1, reads=['wbr', 'yb'], writes=[pkb])
                if b == 0:
                    P.op('vector', lambda e, td=td, psb=psb, tacc=tacc: e.tensor_tensor(out=tacc[:], in0=td[:], in1=psb[:, 0:512], op=ALU.mult),
                         reads=['td%d' % i, pkb], writes=['tacc%d' % ai])
                else:
                    j = rot('tm', 2)
                    tm = tt['tm%d' % j]
                    P.op('vector', lambda e, td=td, psb=psb, tm=tm: e.tensor_tensor(out=tm[:], in0=td[:], in1=psb[:, 0:512], op=ALU.mult),
                         reads=['td%d' % i, pkb], writes=['tm%d' % j])
                    if b < 3:
                        P.op('gpsimd', lambda e, tm=tm, tacc=tacc: e.tensor_tensor(out=tacc[:], in0=tacc[:], in1=tm[:], op=ALU.add),
                             reads=['tacc%d' % ai, 'tm%d' % j], writes=['tacc%d' % ai])
                    else:
                        P.op('gpsimd', lambda e, tm=tm, tacc=tacc, m=m: e.tensor_tensor(out=mg[:, m, :], in0=tacc[:], in1=tm[:], op=ALU.add),
                             reads=['tacc%d' % ai, 'tm%d' % j], writes=['mg'])
        for m in range(8):
            ps, pk = nb()
            for k in range(8):
                P.mm(ps[:, 0:512], wout_t[:, k, m * 128:(m + 1) * 128], mg[:, k, :], k == 0, k == 7, reads=['wout', 'mg'], writes=[pk])
            P.op('vector', lambda e, ps=ps, m=m: e.scalar_tensor_tensor(out=xr[:, m, :], in0=xr[:, m, :], scalar=ALPHA, in1=ps[:, 0:512], op0=ALU.mult, op1=ALU.add),
                 reads=['xr%d' % m, pk], writes=['xr%d' % m])
        layer_norm(PC_LN1W, PC_LN1B, True)
        if moe:
            P.strict = True
            for sbt in range(4):
                ps, pk = nb()
                for k in range(8):
                    P.mm(ps[:, 0:8], xr[:, k, sbt * 128:(sbt + 1) * 128], rt[:, k, :], k == 0, k == 7, reads=['xr%d' % k, 'rt'], writes=[pk])
                P.op('vector', lambda e, ps=ps, sbt=sbt: e.tensor_copy(out=lg[:, sbt, :], in_=ps[:, 0:8]), reads=[pk], writes=['lg'])
                P.op('vector', lambda e, sbt=sbt: e.tensor_reduce(out=m1[:, 0:1], in_=lg[:, sbt, :], axis=AX.X, op=ALU.max), reads=['lg'], writes=['m1'])
                P.op('vector', lambda e, sbt=sbt: e.tensor_scalar(out=gsel[:], in0=lg[:, sbt, :], scalar1=m1[:, 0:1], scalar2=None, op0=ALU.is_equal),
                     reads=['lg', 'm1'], writes=['gsel'])
                P.op('vector', lambda e, sbt=sbt: e.scalar_tensor_tensor(out=lg2[:], in0=gsel[:], scalar=-1.0e30, in1=lg[:, sbt, :], op0=ALU.mult, op1=ALU.add),
                     reads=['gsel', 'lg'], writes=['lg2'])
                P.op('vector', lambda e: e.tensor_reduce(out=m1[:, 1:2], in_=lg2[:], axis=AX.X, op=ALU.max), reads=['lg2'], writes=['m1'])
                P.op('vector', lambda e, sbt=sbt: e.tensor_scalar(out=gsel[:], in0=lg[:, sbt, :], scalar1=m1[:, 1:2], scalar2=None, op0=ALU.is_ge),
                     reads=['lg', 'm1'], writes=['gsel'])
                P.op('vector', lambda e: e.tensor_scalar(out=m1[:, 2:3], in0=m1[:, 0:1], scalar1=-1.0, scalar2=None, op0=ALU.mult), reads=['m1'], writes=['m1'])
                P.op('scalar', lambda e, sbt=sbt: e.activation(out=lg2[:], in_=lg[:, sbt, :], func=AF.Exp, bias=m1[:, 2:3], scale=1.0), reads=['lg', 'm1'], writes=['lg2'])
                P.op('vector', lambda e: e.tensor_tensor(out=lg2[:], in0=lg2[:], in1=gsel[:], op=ALU.mult), reads=['lg2', 'gsel'], writes=['lg2'])
                P.op('vector', lambda e: e.tensor_reduce(out=m1[:, 3:4], in_=lg2[:], axis=AX.X, op=ALU.add), reads=['lg2'], writes=['m1'])
                P.op('vector', lambda e: e.reciprocal(out=m1[:, 3:4], in_=m1[:, 3:4]), reads=['m1'], writes=['m1'])
                P.op('vector', lambda e, sbt=sbt: e.tensor_scalar(out=gts[:, sbt, :], in0=lg2[:], scalar1=m1[:, 3:4], scalar2=None, op0=ALU.mult),
                     reads=['lg2', 'm1'], writes=['gts'])
                ps2, pk2 = nb()
                P.mm(ps2[0:8, 0:128], gts[:, sbt, :], IDN, True, True, reads=['gts', 'cs'], writes=[pk2])
                P.op('vector', lambda e, ps2=ps2, sbt=sbt: e.tensor_copy(out=gT[:, sbt * 128:(sbt + 1) * 128], in_=ps2[0:8, 0:128]), reads=[pk2], writes=['gT'])
            P.strict = False
            if dbg:
                P.dma('gpsimd', lambda e, t0=t0: e.dma_start(out=dbg_g[:, t0:t0 + 512], in_=gT[:]), 'gT', reads=['gT'], final=True)
                P.dma('gpsimd', lambda e, ps_i=ps_i: e.dma_start(out=dbg_l[ps_i, :, 0:32], in_=lg[:].rearrange("p a b -> p (a b)")), 'lg', reads=['lg'], final=True)
                P.dma('gpsimd', lambda e, ps_i=ps_i: e.dma_start(out=dbg_l[ps_i, :, 32:64], in_=gts[:].rearrange("p a b -> p (a b)")), 'gts', reads=['gts'], final=True)
                for k in range(8):
                    P.dma('gpsimd', lambda e, t0=t0, k=k: e.dma_start(out=dbg_x[k * 128:(k + 1) * 128, t0:t0 + 512], in_=xr[:, k, :]), 'xr%d' % k, reads=['xr%d' % k], final=True)
            for ex in range(8):
                ps, pk = nb()
                P.mm(ps[:, 0:512], cs[0:8, CS_EE + ex * 128:CS_EE + (ex + 1) * 128], gT[:], True, True, reads=['cs', 'gT'], writes=[pk])
                P.op('scalar', lambda e, ps=ps, ex=ex: e.copy(out=gb[:, ex, :], in_=ps[:, 0:512]), reads=[pk], writes=['gb%d' % ex])
        for ex in range(E):
            hoff = (ex % 2) * 11 if moe else 0
            hkey = 'hT%d' % (ex % 2)
            for jc in range(J):
                i1 = rot('w1', 4)
                P.dma('sync', lambda e, i1=i1, ex=ex, jc=jc: e.dma_start(out=w1r[i1][:].rearrange("p k c -> p (k c)"), in_=w1d[ex, jc]), 'w1r%d' % i1, writes=['w1r%d' % i1])
                i3 = rot('w3', 4)
                P.dma('sync', lambda e, i3=i3, ex=ex, jc=jc: e.dma_start(out=w3r[i3][:].rearrange("p k c -> p (k c)"), in_=w3d[ex, jc]), 'w3r%d' % i3, writes=['w3r%d' % i3])
                p1, k1 = proj(w1r[i1], 'w1r%d' % i1, 0, xb, 'xb')
                p3, k3 = proj(w3r[i3], 'w3r%d' % i3, 0, xb, 'xb')
                si = rot('ts', 2)
                ts_ = tt['td%d' % si]
                P.op('scalar', lambda e, ts_=ts_, p1=p1: e.activation(out=ts_[:], in_=p1[:, 0:512], func=AF.Silu), reads=[k1], writes=['td%d' % si])
                if moe:
                    P.op('vector', lambda e, ts_=ts_, p3=p3: e.tensor_tensor(out=ts_[:], in0=ts_[:], in1=p3[:, 0:512], op=ALU.mult), reads=['td%d' % si, k3], writes=['td%d' % si])
                    P.op('gpsimd', lambda e, ts_=ts_, ex=ex, jc=jc, hoff=hoff: e.tensor_tensor(out=hT[:, hoff + jc, :], in0=ts_[:], in1=gb[:, ex, :], op=ALU.mult),
                         reads=['td%d' % si, 'gb%d' % ex], writes=[hkey])
                else:
                    P.op('vector', lambda e, ts_=ts_, p3=p3, jc=jc: e.tensor_tensor(out=hT[:, jc, :], in0=ts_[:], in1=p3[:, 0:512], op=ALU.mult),
                         reads=['td%d' % si, k3], writes=[hkey])
            for m in range(8):
                i2 = rot('w2', 3)
                P.dma('sync', lambda e, i2=i2, ex=ex, m=m: e.dma_start(out=w2r[i2][:].rearrange("p k c -> p (k c)"), in_=w2d[ex, m]), 'w2r%d' % i2, writes=['w2r%d' % i2])
                ps, pk = nb()
                for jc in range(J):
                    P.mm(ps[:, 0:512], w2r[i2][:, jc, :], hT[:, hoff + jc, :], jc == 0, jc == J - 1, reads=['w2r%d' % i2, hkey], writes=[pk])
                if not moe:
                    P.op('vector', lambda e, ps=ps, m=m: e.scalar_tensor_tensor(out=xr[:, m, :], in0=xr[:, m, :], scalar=ALPHA, in1=ps[:, 0:512], op0=ALU.mult, op1=ALU.add),
                         reads=['xr%d' % m, pk], writes=['xr%d' % m])
                elif ex == 0:
                    P.op('vector', lambda e, ps=ps, m=m: e.scalar_tensor_tensor(out=xr[:, m, :], in0=xr[:, m, :], scalar=ALPHA, in1=ps[:, 0:512], op0=ALU.mult, op1=ALU.add),
                         reads=['xr%d' % m, pk], writes=['xr%d' % m])
                else:
                    P.op('vector', lambda e, ps=ps, m=m: e.tensor_tensor(out=xr[:, m, :], in0=xr[:, m, :], in1=ps[:, 0:512], op=ALU.add),
                         reads=[pk, 'xr%d' % m], writes=['xr%d' % m])
        layer_norm(PC_LN2W, PC_LN2B, False, out_dram_t0=t0)
    P.emit()
    return nc


def l3_consts():
    c = np.zeros((128, CS_N), np.float32)
    i = np.arange(128)
    c[:, CS_BONES:CS_BONES + 128] = (i[:, None] // 64 == i[None, :] // 64)
    c[:, CS_ONES:CS_ONES + 128] = 1.0
    c[:, CS_IDENT:CS_IDENT + 128] = np.eye(128)
    for ch in range(2):
        for h in range(4):
            c[h, CS_E4 + ch * 128:CS_E4 + (ch + 1) * 128] = ((ch * 2 + i // 64) == h)
    for ch in range(4):
        for h in range(8):
            c[h, CS_E8 + ch * 128:CS_E8 + (ch + 1) * 128] = ((ch * 2 + i // 64) == h)
    for ex in range(8):
        c[ex, CS_EE + ex * 128:CS_EE + (ex + 1) * 128] = 1.0
    return c


def l3_gate_columns():
    cols = []
    for m in range(8):
        for b in range(4):
            cols.append(np.arange(b * 1024 + m * 128, b * 1024 + (m + 1) * 128))
    cols.append(_cols('a_o'))
    cols.append(_cols('c_gate'))
    return np.concatenate(cols)


def _pack_kxc(w, cw):
    K_, C_ = w.shape
    g = C_ // cw
    return np.ascontiguousarray(w.reshape(K_ // 128, 128, g, cw).transpose(2, 1, 0, 3).reshape(g, 128, (K_ // 128) * cw))


def _pack_w2(w):
    J_ = w.shape[0] // 128
    return np.ascontiguousarray(w.reshape(J_, 128, 8, 128).transpose(2, 1, 0, 3).reshape(8, 128, J_ * 128))


def l3_pcol(inp, l):
    p = np.zeros((128, 40), np.float32)
    p[:, PC_ANW:PC_ANW + 2] = inp['a_norm_w'][l].reshape(2, 128).T
    p[:, PC_CNW] = np.tile(inp['c_norm_w'][l], 2)
    p[0:8, PC_SINK] = inp['d_sink'][l]
    p[:, PC_LN1W:PC_LN1W + 8] = inp['ln1_w'][l].reshape(8, 128).T
    p[:, PC_LN1B:PC_LN1B + 8] = inp['ln1_b'][l].reshape(8, 128).T
    p[:, PC_LN2W:PC_LN2W + 8] = inp['ln2_w'][l].reshape(8, 128).T
    p[:, PC_LN2B:PC_LN2B + 8] = inp['ln2_b'][l].reshape(8, 128).T
    return p


def l3_weight_maps(wb, inp, l):
    d = {}
    d['wg'] = _pack_kxc(wb['wgsel'][l], 512)
    wbr = np.concatenate([wb['w_branch_a'][l], wb['w_branch_b'][l], wb['w_branch_c'][l], wb['w_branch_d'][l]], 0)
    d['wbr'] = np.ascontiguousarray(wbr.reshape(10, 128, 1024).transpose(1, 0, 2).reshape(128, 10 * 1024))
    d['wout'] = np.ascontiguousarray(wb['w_out'][l].reshape(8, 128, 1024).transpose(1, 0, 2).reshape(128, 8 * 1024))
    d['pcol'] = l3_pcol(inp, l)
    j = l // 2
    if l % 2 == 0:
        d['w1p'] = _pack_kxc(wb['ffn_w1'][j], 128)[None]
        d['w3p'] = _pack_kxc(wb['ffn_w3'][j], 128)[None]
        d['w2p'] = _pack_w2(wb['ffn_w2'][j])[None]
    else:
        d['w1p'] = np.stack([_pack_kxc(wb['moe_w1'][j, e], 128) for e in range(8)])
        d['w3p'] = np.stack([_pack_kxc(wb['moe_w3'][j, e], 128) for e in range(8)])
        d['w2p'] = np.stack([_pack_w2(wb['moe_w2'][j, e]) for e in range(8)])
        d['router'] = np.ascontiguousarray(inp['moe_router'][j].reshape(8, 128, 8).transpose(1, 0, 2).reshape(128, 64))
    return d


def l3_host_inputs(xT_full, hM, oD, accB, accD, wmap, cst3):
    accBT = np.ascontiguousarray(accB[:, :, :, :64].reshape(3, SEQ, 256).transpose(0, 2, 1))
    lBT = np.ascontiguousarray(accB[:, :, :, 64].transpose(0, 2, 1))
    accDT = np.ascontiguousarray(accD[:, :, :64].reshape(SEQ, 512).T)
    lDT = np.ascontiguousarray(accD[:, :, 64].T)
    maps = []
    for c in range(NCORES):
        sl = slice(c * TC, (c + 1) * TC)
        m = dict(xT=np.ascontiguousarray(xT_full[:, sl]), hA=np.ascontiguousarray(hM[:, :, sl]), oCd=np.ascontiguousarray(oD[:, :, sl]),
                 accB=np.ascontiguousarray(accBT[:, :, sl]), lB=np.ascontiguousarray(lBT[:, :, sl]),
                 accD=np.ascontiguousarray(accDT[:, sl]), lD=np.ascontiguousarray(lDT[:, sl]), cst3=cst3)
        m.update(wmap)
        maps.append(m)
    return maps


_NC_CACHE = {}


def _get_nc(name):
    if name not in _NC_CACHE:
        _NC_CACHE[name] = {'l0': build_l0, 'l1': build_l1, 'l2a': build_l2a, 'l2m': build_l2m, 'l2d': build_l2d,
                           'l3d': lambda: build_l3(False), 'l3m': lambda: build_l3(True)}[name]()
    return _NC_CACHE[name]


def _launch(nc, maps):
    return run_bass_kernel_spmd(nc, maps, core_ids=list(range(NCORES))).results


def run_layer(xT_full, l, inp, consts, wb, cst3):
    r1 = _launch(_get_nc('l1'), l1_host_inputs(xT_full, l, inp, consts))
    oA = np.concatenate([r['oA'] for r in r1], axis=1)
    oB = np.concatenate([r['oB'] for r in r1], axis=1)
    oC = np.concatenate([r['oC'] for r in r1], axis=1)
    oG = np.concatenate([r['oG'] for r in r1], axis=1)
    ra = _launch(_get_nc('l2a'), l2a_host_inputs(oA, oB, inp, l))
    accB, accD = l2a_host_outputs(ra)
    hM = l2m_host_outputs(_launch(_get_nc('l2m'), l2m_host_inputs(oA, oG)))
    oD = l2d_host_outputs(_launch(_get_nc('l2d'), l2d_host_inputs(oC, oG)))
    wmap = l3_weight_maps(wb, inp, l)
    r3 = _launch(_get_nc('l3m' if l % 2 else 'l3d'), l3_host_inputs(xT_full, hM, oD, accB, accD, wmap, cst3))
    return np.concatenate([r['xo'] for r in r3], axis=1)


def cast_all_weights(inp):
    gc = l3_gate_columns()
    arrs = {'wgsel': np.ascontiguousarray(inp['w_in'][:, :, gc])}
    for n in ('w_branch_a', 'w_branch_b', 'w_branch_c', 'w_branch_d', 'w_out', 'ffn_w1', 'ffn_w3', 'ffn_w2', 'moe_w1', 'moe_w3', 'moe_w2'):
        arrs[n] = inp[n]
    return cast_weights_on_device(arrs, _get_nc('l0'), _launch)


def kernel(**inputs):
    inp = {k: np.asarray(v) for k, v in inputs.items()}
    consts = make_consts(inp)
    cst3 = l3_consts()
    wb = cast_all_weights(inp)
    xT = np.ascontiguousarray(inp['x'][0].T)
    for l in range(DEPTH):
        xT = run_layer(xT, l, inp, consts, wb, cst3)
    return np.ascontiguousarray(xT.T)[None].astype(np.float32)
```

```python
import contextlib
import numpy as np
import ml_dtypes
import concourse.bass as bass
import concourse.mybir as mybir
from concourse.bass_utils import run_bass_kernel_spmd

F32 = mybir.dt.float32
BF16 = mybir.dt.bfloat16
AF = mybir.ActivationFunctionType
ALU = mybir.AluOpType
AX = mybir.AxisListType
NPBF = ml_dtypes.bfloat16

NCORES = 8
SEQ = 16384
D = 1024
TC = SEQ // NCORES
DEPTH = 4
ALPHA = (2 * DEPTH) ** 0.25
LN_EPS = 1e-5


class Prog:
    ENGS = ('tensor', 'vector', 'scalar', 'gpsimd', 'sync')

    def __init__(self, nc):
        self.nc = nc
        self.stack = contextlib.ExitStack()
        self.recs = {e: [] for e in self.ENGS}
        self.cnt = {}
        self.lastw = {}
        self.rd = {}
        self.seen = {e: {} for e in self.ENGS}
        self.final = []
        self.nsb = 0
        self.pskeys = set()
        self.strict = False

    def sb(self, name, shape, dt):
        return self.stack.enter_context(self.nc.sbuf_tensor("sb_" + name, list(shape), dt))

    def ps(self, name, shape=None, dt=F32):
        self.pskeys.add(name)
        return self.stack.enter_context(self.nc.psum_tensor("pp_" + name, [128, 512], F32))

    def _deps(self, eng, reads, writes):
        deps = {}

        def add(tok):
            if tok is None:
                return
            k, v = tok
            if k == eng and eng == 'tensor':
                return
            if deps.get(k, 0) < v:
                deps[k] = v

        for key in reads:
            add(self.lastw.get(key))
            if key in self.pskeys:
                for t in self.rd.get(key, {}).items():
                    add(t)
        for key in writes:
            add(self.lastw.get(key))
            for t in self.rd.get(key, {}).items():
                add(t)
        out = []
        for k, v in deps.items():
            if self.seen[eng].get(k, 0) < v:
                self.seen[eng][k] = v
                out.append((k, v))
        return out

    def _commit(self, tok, reads, writes):
        for key in reads:
            d = self.rd.setdefault(key, {})
            if d.get(tok[0], 0) < tok[1]:
                d[tok[0]] = tok[1]
        for key in writes:
            self.lastw[key] = tok
            self.rd[key] = {}

    def op(self, eng, fn, reads=(), writes=()):
        waits = self._deps(eng, reads, writes)
        self.cnt[eng] = self.cnt.get(eng, 0) + 1
        tok = (eng, self.cnt[eng])
        self.recs[eng].append(dict(fn=fn, waits=waits, tok=tok, dma=False))
        self._commit(tok, reads, writes)
        return tok

    def dma(self, eng, fn, sem, reads=(), writes=(), final=False):
        waits = self._deps(eng, reads, writes)
        sk = 'dma:' + sem
        self.cnt[sk] = self.cnt.get(sk, 0) + 16
        tok = (sk, self.cnt[sk])
        self.recs[eng].append(dict(fn=fn, waits=waits, tok=tok, dma=True))
        self._commit(tok, reads, writes)
        if final:
            self.final.append((eng, tok))
        return tok

    def mm(self, out, lhsT, rhs, start, stop, reads, writes):
        return self.op('tensor', lambda e: e.matmul(out, lhsT=lhsT, rhs=rhs, start=start, stop=stop),
                       reads=reads, writes=writes)

    def emit(self):
        nc = self.nc
        fin = {}
        for eng, tok in self.final:
            d = fin.setdefault(eng, {})
            if d.get(tok[0], 0) < tok[1]:
                d[tok[0]] = tok[1]
        for eng, d in fin.items():
            self.recs[eng].append(dict(fn=None, waits=list(d.items()), tok=None, dma=False))
        waited = {}
        for e in self.ENGS:
            for r in self.recs[e]:
                for (k, v) in r['waits']:
                    waited.setdefault(k, set()).add(v)
        rank = {}
        for k, vals in waited.items():
            if k in self.ENGS:
                rank[k] = {v: i + 1 for i, v in enumerate(sorted(vals))}
        semkeys = sorted(set(list(waited.keys()) + [k for k in self.cnt if k.startswith('dma:')]))
        sems = {}
        for i, k in enumerate(semkeys):
            sems[k] = self.stack.enter_context(nc.semaphore("s%d_%s" % (i, k.replace(':', '_'))))
        recs = self.recs
        with nc.Block() as block:
            for e in self.ENGS:
                if not recs[e]:
                    continue

                def body(engobj, e=e):
                    wset = waited.get(e, ())
                    for r in recs[e]:
                        for (k, v) in r['waits']:
                            val = rank[k][v] if k in rank else v
                            engobj.wait_ge(sems[k], val)
                        if r['fn'] is None:
                            continue
                        ins = r['fn'](engobj)
                        if r['dma']:
                            ins.then_inc(sems[r['tok'][0]], 16)
                        elif r['tok'][1] in wset:
                            ins.then_inc(sems[e], 1)

                getattr(block, e)(body)
        self.stack.close()


def _new_nc():
    return bass.Bass("TRN2", target_bir_lowering=False)


W_IN_OFF = {}
_off = 0
for _n, _s in (('gates', 4096), ('a_q', 256), ('a_k', 256), ('a_v', 256), ('a_o', 256), ('a_if', 16),
               ('b_q', 768), ('b_k', 768), ('b_v', 768), ('c_qkv', 768), ('c_gate', 256), ('c_beta', 8),
               ('c_decay', 8), ('d_q', 512), ('d_k', 128), ('d_v', 128)):
    W_IN_OFF[_n] = (_off, _s)
    _off += _s
N_IN = _off


def _cols(name):
    o, s = W_IN_OFF[name]
    return np.arange(o, o + s)


def l1_columns():
    plain = np.concatenate([_cols('a_q'), _cols('a_k'), _cols('a_v'), _cols('b_v'), _cols('d_v'),
                            -np.ones(128, np.int64)])
    rope = np.concatenate([_cols('b_q'), _cols('b_k'), _cols('d_q'), _cols('d_k')])
    r = np.arange(rope.size)
    swap = rope[(r // 64) * 64 + ((r % 64) + 32) % 64]
    inter = np.stack([rope.reshape(-1, 128), swap.reshape(-1, 128)], axis=1).reshape(-1)
    conv = _cols('c_qkv')
    gate = -np.ones(128, np.int64)
    oi, _ = W_IN_OFF['a_if']
    for d in range(2):
        for h in range(4):
            gate[0 + d * 4 + h] = oi + d * 8 + h
            gate[32 + d * 4 + h] = oi + d * 8 + 4 + h
            gate[64 + d * 4 + h] = W_IN_OFF['c_beta'][0] + d * 4 + h
            gate[96 + d * 4 + h] = W_IN_OFF['c_decay'][0] + d * 4 + h
    return np.concatenate([plain, inter, conv, gate])


L1_NPLAIN, L1_NROPE, L1_NCONV = 1664, 2176, 768
L1_NC = (L1_NPLAIN + 128) + 2 * L1_NROPE + L1_NCONV + 128
L1_WG = 512


def build_l1():
    nc = _new_nc()
    T = TC
    NT = T // 512
    xT = nc.dram_tensor("xT", [D, T], F32, kind="ExternalInput").ap()
    xhT = nc.dram_tensor("xhT", [D, 4], F32, kind="ExternalInput").ap()
    w1 = nc.dram_tensor("w1", [D, L1_NC], F32, kind="ExternalInput").ap()
    cosd = nc.dram_tensor("cos", [128, T], F32, kind="ExternalInput").ap()
    sind = nc.dram_tensor("sin", [128, T], F32, kind="ExternalInput").ap()
    convw = nc.dram_tensor("convw", [768, 5], F32, kind="ExternalInput").ap()
    gpar = nc.dram_tensor("gpar", [128, 2], F32, kind="ExternalInput").ap()
    bones = nc.dram_tensor("bones", [128, 128], F32, kind="ExternalInput").ap()
    oA = nc.dram_tensor("oA", [L1_NPLAIN, T], BF16, kind="ExternalOutput").ap()
    oB = nc.dram_tensor("oB", [L1_NROPE, T], BF16, kind="ExternalOutput").ap()
    oC = nc.dram_tensor("oC", [L1_NCONV, T], BF16, kind="ExternalOutput").ap()
    oG = nc.dram_tensor("oG", [128, T], F32, kind="ExternalOutput").ap()

    P = Prog(nc)
    xf = [P.sb("xf%d" % i, [128, T], F32) for i in range(2)]
    xb = P.sb("xb", [128, 8, T], BF16)
    xhf = P.sb("xhf", [128, 8, 4], F32)
    xhb = P.sb("xhb", [128, 8, 4], BF16)
    cos_t = P.sb("cos_t", [128, T], F32)
    sin_t = P.sb("sin_t", [128, T], F32)
    cw = P.sb("cw", [128, 6, 5], F32)
    gp = P.sb("gp", [128, 2], F32)
    gneg = P.sb("gneg", [128, 2], F32)
    bo = P.sb("bo", [128, 128], F32)
    wst = [P.sb("wst%d" % i, [128, 8, L1_WG], F32) for i in range(2)]
    wbf = [P.sb("wbf%d" % i, [128, 8, L1_WG], BF16) for i in range(2)]
    ot = [P.sb("ot%d" % i, [128, 512], BF16) for i in range(3)]
    otg = P.sb("otg", [128, T], F32)
    r1 = [P.sb("r1_%d" % i, [128, 512], F32) for i in range(2)]
    r2 = [P.sb("r2_%d" % i, [128, 512], F32) for i in range(2)]
    cbuf = P.sb("cbuf", [128, T + 4], F32)
    cacc = P.sb("cacc", [128, T], F32)
    csq = P.sb("csq", [128, T], F32)
    crs = P.sb("crs", [128, 512], F32)
    cout = P.sb("cout", [128, T], BF16)
    gt1 = P.sb("gt1", [128, 512], F32)
    eps6 = P.sb("eps6", [128, 1], F32)
    P.op('vector', lambda e: e.memset(eps6[:], 1e-6), writes=['eps6'])
    pss = [P.ps("ps%d" % i, [128, 512]) for i in range(6)]
    psh = P.ps("psh", [128, 4])
    psn = P.ps("psn", [128, 512])

    P.dma('sync', lambda e: e.dma_start(out=xhf[:], in_=xhT.rearrange("(k p) t -> p k t", p=128)), 'xhf', writes=['xhf'])
    P.dma('sync', lambda e: e.dma_start(out=cos_t[:], in_=cosd), 'cos', writes=['cos'])
    P.dma('sync', lambda e: e.dma_start(out=sin_t[:], in_=sind), 'sin', writes=['sin'])
    P.dma('sync', lambda e: e.dma_start(out=cw[:], in_=convw.rearrange("(c p) j -> p c j", p=128)), 'cw', writes=['cw'])
    P.dma('sync', lambda e: e.dma_start(out=gp[:], in_=gpar), 'gp', writes=['gp'])
    P.dma('sync', lambda e: e.dma_start(out=bo[:], in_=bones), 'bo', writes=['bo'])
    xbk = ['xb%d' % k for k in range(8)]
    for k in range(8):
        eng = 'vector' if k % 2 == 0 else 'gpsimd'
        P.dma('sync', lambda e, k=k: e.dma_start(out=xf[k % 2][:], in_=xT[k * 128:(k + 1) * 128, :]), 'xf%d' % (k % 2), writes=['xf%d' % (k % 2)])
        P.op(eng, lambda e, k=k: e.tensor_copy(out=xb[:, k, :], in_=xf[k % 2][:]), reads=['xf%d' % (k % 2)], writes=[xbk[k]])
    P.op('vector', lambda e: e.tensor_copy(out=xhb[:], in_=xhf[:]), reads=['xhf'], writes=['xhb'])
    P.op('vector', lambda e: e.tensor_scalar(out=gneg[:, 0:1], in0=gp[:, 0:1], scalar1=-1.0, scalar2=None, op0=ALU.mult),
         reads=['gp'], writes=['gneg'])
    P.op('scalar', lambda e: e.activation(out=gneg[:, 1:2], in_=gp[:, 1:2], func=AF.Exp), reads=['gp', 'gneg'], writes=['gneg'])
    P.op('vector', lambda e: e.tensor_scalar(out=gneg[:, 1:2], in0=gneg[:, 1:2], scalar1=-1.0, scalar2=None, op0=ALU.mult),
         reads=['gneg'], writes=['gneg'])

    ngroups = (L1_NC + L1_WG - 1) // L1_WG
    state = dict(oti=0, psi=0, r=0)
    loaded = set()

    def load_group(g):
        if g in loaded or g >= ngroups:
            return
        loaded.add(g)
        s = g % 2
        c0 = g * L1_WG
        cw_ = min(L1_WG, L1_NC - c0)
        P.dma('sync', lambda e: e.dma_start(out=wst[s][:, :, 0:cw_],
                                            in_=w1[:, c0:c0 + cw_].rearrange("(k p) c -> p k c", p=128)),
              'wst%d' % s, writes=['wst%d' % s])
        for k in range(8):
            eng = ('vector', 'gpsimd')[k % 2]
            P.op(eng, lambda e, k=k: e.tensor_copy(out=wbf[s][:, k, 0:cw_], in_=wst[s][:, k, 0:cw_]),
                 reads=['wst%d' % s], writes=['wbf%d_%d' % (s, k)])

    def wkeys(s):
        return ['wbf%d_%d' % (s, k) for k in range(8)]

    def ensure(col):
        g = col // L1_WG
        load_group(g)
        load_group(g + 1)
        return g % 2, col % L1_WG

    def proj(ps, psk, col, t0, n, halo=False):
        s, cl = ensure(col)
        for k in range(8):
            rhs = xhb[:, k, :] if halo else xb[:, k, t0:t0 + n]
            P.mm(ps[:, 0:n], wbf[s][:, k, cl:cl + 128], rhs, k == 0, k == 7,
                 reads=wkeys(s) + (['xhb'] if halo else xbk), writes=[psk])

    def next_ps():
        i = state['psi']
        state['psi'] = (i + 1) % 6
        return pss[i], 'ps%d' % i

    def next_ot():
        i = state['oti']
        state['oti'] = (i + 1) % 3
        return ot[i], 'ot%d' % i

    n_plain, n_rope, n_conv = L1_NPLAIN // 128, L1_NROPE // 128, L1_NCONV // 128
    for c in range(n_plain):
        scale = 0.125 if 2 <= c < 4 else 1.0
        for tt in range(NT):
            ps, psk = next_ps()
            proj(ps, psk, c * 128, tt * 512, 512)
            o, ok = next_ot()
            P.op('scalar', lambda e, o=o, ps=ps, scale=scale: e.activation(out=o[:], in_=ps[:], func=AF.Copy, scale=scale),
                 reads=[psk], writes=[ok])
            P.dma('gpsimd', lambda e, o=o, c=c, tt=tt: e.dma_start(out=oA[c * 128:(c + 1) * 128, tt * 512:(tt + 1) * 512], in_=o[:]),
                  ok, reads=[ok], final=True)
    base = L1_NPLAIN + 128
    for c in range(n_rope):
        for tt in range(NT):
            psa, pska = next_ps()
            proj(psa, pska, base + c * 256, tt * 512, 512)
            psb, pskb = next_ps()
            proj(psb, pskb, base + c * 256 + 128, tt * 512, 512)
            i = state['r']
            state['r'] = (i + 1) % 2
            P.op('vector', lambda e, i=i, psa=psa, tt=tt: e.tensor_tensor(out=r1[i][:], in0=psa[:], in1=cos_t[:, tt * 512:(tt + 1) * 512], op=ALU.mult),
                 reads=[pska, 'cos'], writes=['r1_%d' % i])
            P.op('vector', lambda e, i=i, psb=psb, tt=tt: e.tensor_tensor(out=r2[i][:], in0=psb[:], in1=sin_t[:, tt * 512:(tt + 1) * 512], op=ALU.mult),
                 reads=[pskb, 'sin'], writes=['r2_%d' % i])
            o, ok = next_ot()
            P.op('gpsimd', lambda e, i=i, o=o: e.tensor_tensor(out=o[:], in0=r1[i][:], in1=r2[i][:], op=ALU.add),
                 reads=['r1_%d' % i, 'r2_%d' % i], writes=[ok])
            P.dma('gpsimd', lambda e, o=o, c=c, tt=tt: e.dma_start(out=oB[c * 128:(c + 1) * 128, tt * 512:(tt + 1) * 512], in_=o[:]),
                  ok, reads=[ok], final=True)
    base = L1_NPLAIN + 128 + 2 * L1_NROPE
    for c in range(n_conv):
        col = base + c * 128
        proj(psh, 'psh', col, 0, 4, halo=True)
        P.op('scalar', lambda e: e.copy(out=cbuf[:, 0:2], in_=psh[:, 0:2]), reads=['psh'], writes=['cbuf'])
        P.op('scalar', lambda e: e.copy(out=cbuf[:, T + 2:T + 4], in_=psh[:, 2:4]), reads=['psh'], writes=['cbuf'])
        for tt in range(NT):
            ps, psk = next_ps()
            proj(ps, psk, col, tt * 512, 512)
            P.op('scalar', lambda e, ps=ps, tt=tt: e.copy(out=cbuf[:, 2 + tt * 512:2 + (tt + 1) * 512], in_=ps[:]),
                 reads=[psk], writes=['cbuf'])
        P.op('vector', lambda e, c=c: e.tensor_scalar(out=cacc[:], in0=cbuf[:, 0:T], scalar1=cw[:, c, 0:1], scalar2=None, op0=ALU.mult),
             reads=['cbuf', 'cw'], writes=['cacc'])
        for j in range(1, 5):
            P.op('vector', lambda e, c=c, j=j: e.scalar_tensor_tensor(out=cacc[:], in0=cbuf[:, j:j + T], scalar=cw[:, c, j:j + 1], in1=cacc[:],
                                                                      op0=ALU.mult, op1=ALU.add),
                 reads=['cbuf', 'cw', 'cacc'], writes=['cacc'])
        if c >= 4:
            P.op('scalar', lambda e: e.activation(out=cout[:], in_=cacc[:], func=AF.Silu), reads=['cacc'], writes=['cout'])
        else:
            P.op('scalar', lambda e: e.activation(out=cacc[:], in_=cacc[:], func=AF.Silu), reads=['cacc'], writes=['cacc'])
            P.op('gpsimd', lambda e: e.tensor_tensor(out=csq[:], in0=cacc[:], in1=cacc[:], op=ALU.mult), reads=['cacc'], writes=['csq'])
            sc = 0.125 if c < 2 else 1.0
            for tt in range(NT):
                P.mm(psn[:], bo[:], csq[:, tt * 512:(tt + 1) * 512], True, True, reads=['bo', 'csq'], writes=['psn'])
                P.op('scalar', lambda e: e.activation(out=crs[:], in_=psn[:], func=AF.Sqrt, bias=eps6[:, 0:1], scale=1.0),
                     reads=['psn', 'eps6'], writes=['crs'])
                P.op('vector', lambda e: e.reciprocal(out=crs[:], in_=crs[:]), reads=['crs'], writes=['crs'])
                P.op('vector', lambda e, tt=tt, sc=sc: e.scalar_tensor_tensor(out=cout[:, tt * 512:(tt + 1) * 512], in0=cacc[:, tt * 512:(tt + 1) * 512],
                                                                             scalar=sc, in1=crs[:], op0=ALU.mult, op1=ALU.mult),
                     reads=['cacc', 'crs'], writes=['cout'])
        P.dma('gpsimd', lambda e, c=c: e.dma_start(out=oC[c * 128:(c + 1) * 128, :], in_=cout[:]), 'cout', reads=['cout'], final=True)
    col = base + L1_NCONV
    for tt in range(NT):
        ps, psk = next_ps()
        proj(ps, psk, col, tt * 512, 512)
        sl = slice(tt * 512, (tt + 1) * 512)
        P.op('scalar', lambda e, ps=ps, sl=sl: e.activation(out=otg[0:32, sl], in_=ps[0:32, :], func=AF.Identity, bias=gp[0:32, 0:1], scale=1.0),
             reads=[psk, 'gp'], writes=['otg'])
        P.op('scalar', lambda e, ps=ps: e.activation(out=gt1[32:64, :], in_=ps[32:64, :], func=AF.Exp, bias=gneg[32:64, 0:1], scale=-1.0),
             reads=[psk, 'gneg'], writes=['gt1'])
        P.op('scalar', lambda e: e.activation(out=gt1[32:64, :], in_=gt1[32:64, :], func=AF.Ln, bias=1.0, scale=1.0),
             reads=['gt1'], writes=['gt1'])
        P.op('vector', lambda e, sl=sl: e.tensor_scalar(out=otg[32:64, sl], in0=gt1[32:64, :], scalar1=-1.0, scalar2=None, op0=ALU.mult),
             reads=['gt1'], writes=['otg'])
        P.op('scalar', lambda e, ps=ps, sl=sl: e.activation(out=otg[64:96, sl], in_=ps[64:96, :], func=AF.Sigmoid),
             reads=[psk], writes=['otg'])
        P.op('scalar', lambda e, ps=ps: e.activation(out=gt1[96:128, :], in_=ps[96:128, :], func=AF.Exp, bias=gp[96:128, 0:1], scale=1.0),
             reads=[psk, 'gp'], writes=['gt1'])
        P.op('scalar', lambda e: e.activation(out=gt1[96:128, :], in_=gt1[96:128, :], func=AF.Ln, bias=1.0, scale=1.0),
             reads=['gt1'], writes=['gt1'])
        P.op('vector', lambda e, sl=sl: e.tensor_scalar(out=otg[96:128, sl], in0=gt1[96:128, :], scalar1=gneg[96:128, 1:2], scalar2=None, op0=ALU.mult),
             reads=['gt1', 'gneg'], writes=['otg'])
    P.dma('gpsimd', lambda e: e.dma_start(out=oG[:, :], in_=otg[:]), 'otg', reads=['otg'], final=True)
    P.emit()
    return nc


def rope_tables_np():
    inv = (10000.0 ** (-np.arange(0, 64, 2, dtype=np.float32) / np.float32(64))).astype(np.float32)
    ang = np.arange(SEQ, dtype=np.float32)[:, None] * inv[None, :]
    return np.cos(ang).astype(np.float32), np.sin(ang).astype(np.float32)


def l1_host_inputs(xT_full, layer, inp, consts):
    w1 = consts['w1'][layer]
    maps = []
    xpad = np.zeros((D, SEQ + 4), np.float32)
    xpad[:, 2:SEQ + 2] = xT_full
    for c in range(NCORES):
        t0 = c * TC
        xh = np.concatenate([xpad[:, t0:t0 + 2], xpad[:, t0 + TC + 2:t0 + TC + 4]], axis=1)
        maps.append(dict(xT=np.ascontiguousarray(xT_full[:, t0:t0 + TC]), xhT=np.ascontiguousarray(xh), w1=w1,
                         cos=consts['cos128'][:, t0:t0 + TC], sin=consts['sin128'][:, t0:t0 + TC],
                         convw=consts['convw'][layer], gpar=consts['gpar'][layer], bones=consts['bones']))
    return maps


def make_consts(inp):
    c = {}
    cols = l1_columns()
    w_in = inp['w_in']
    w1 = []
    for l in range(DEPTH):
        w = np.zeros((D, L1_NC), np.float32)
        m = cols >= 0
        w[:, m] = w_in[l][:, cols[m]]
        w1.append(w)
    c['w1'] = w1
    cos, sin = rope_tables_np()
    r = np.arange(128)
    c['cos128'] = np.ascontiguousarray(cos[:, r % 32].T)
    sgn = np.where((r % 64) < 32, -1.0, 1.0).astype(np.float32)
    c['sin128'] = np.ascontiguousarray((sin[:, r % 32] * sgn[None, :]).T)
    c['convw'] = [np.ascontiguousarray(inp['c_conv_w'][l].T) for l in range(DEPTH)]
    gpar = []
    for l in range(DEPTH):
        g = np.zeros((128, 2), np.float32)
        for d in range(2):
            for h in range(4):
                g[d * 4 + h, 0] = inp['a_gate_bias'][l, d, 0, h]
                g[32 + d * 4 + h, 0] = inp['a_gate_bias'][l, d, 1, h]
                g[96 + d * 4 + h, 0] = inp['c_dt_bias'][l, d, h]
                g[96 + d * 4 + h, 1] = inp['c_a_log'][l, d, h]
        gpar.append(g)
    c['gpar'] = gpar
    i = np.arange(128)
    c['bones'] = (i[:, None] // 64 == i[None, :] // 64).astype(np.float32)
    return c


JQ = 1024
NB_JOBS = 6
ND_JOBS = 2


def build_l2a():
    nc = _new_nc()
    BqT = nc.dram_tensor("BqT", [NB_JOBS, 64, 4, JQ], BF16, kind="ExternalInput").ap()
    BkT = nc.dram_tensor("BkT", [NB_JOBS, 64, 4, JQ + 128], BF16, kind="ExternalInput").ap()
    Bv = nc.dram_tensor("Bv", [NB_JOBS, 4, JQ + 128, 65], BF16, kind="ExternalInput").ap()
    DqT = nc.dram_tensor("DqT", [ND_JOBS, 64, 2, 8, 4, 128], BF16, kind="ExternalInput").ap()
    DkT = nc.dram_tensor("DkT", [ND_JOBS, 64, 2, JQ + 256], BF16, kind="ExternalInput").ap()
    Dv = nc.dram_tensor("Dv", [ND_JOBS, 2, JQ + 256, 65], BF16, kind="ExternalInput").ap()
    sinkd = nc.dram_tensor("sink", [128, 8], F32, kind="ExternalInput").ap()
    masks = nc.dram_tensor("masks", [128, 3, 512], BF16, kind="ExternalInput").ap()
    yB = nc.dram_tensor("yB", [NB_JOBS, JQ, 260], F32, kind="ExternalOutput").ap()
    yD = nc.dram_tensor("yD", [ND_JOBS, JQ, 520], F32, kind="ExternalOutput").ap()

    P = Prog(nc)
    mk = P.sb("mk", [128, 3, 512], BF16)
    esink = P.sb("esink", [128, 8], F32)
    qt = [P.sb("qt%d" % i, [64, 4, JQ], BF16) for i in range(2)]
    kt = [P.sb("kt%d" % i, [64, 4, JQ + 128], BF16) for i in range(2)]
    vt = [P.sb("vt%d" % i, [128, 4, 9, 65], BF16) for i in range(2)]
    dq = [P.sb("dq%d" % i, [64, 2, 8, 4, 128], BF16) for i in range(2)]
    dk = [P.sb("dk%d" % i, [64, 2, JQ + 256], BF16) for i in range(2)]
    dv = [P.sb("dv%d" % i, [128, 2, 10, 65], BF16) for i in range(2)]
    pt = [P.sb("pt%d" % i, [128, 512], BF16) for i in range(4)]
    ob = [P.sb("ob%d" % i, [128, 260], F32) for i in range(2)]
    od = [P.sb("od%d" % i, [128, 260], F32) for i in range(2)]
    den = P.sb("den", [128, 4], F32)
    pss = [P.ps("pss%d" % i, [128, 512]) for i in range(4)]
    pso = [P.ps("pso%d" % i) for i in range(2)]

    P.dma('sync', lambda e: e.dma_start(out=mk[:], in_=masks), 'mk', writes=['mk'])
    P.dma('sync', lambda e: e.dma_start(out=esink[:], in_=sinkd), 'esink', writes=['esink'])
    P.op('scalar', lambda e: e.activation(out=esink[:], in_=esink[:], func=AF.Exp), reads=['esink'], writes=['esink'])
    st = dict(ps=0, pt=0, po=0, ob=0, od=0)

    def nxt(name, n):
        i = st[name]
        st[name] = (i + 1) % n
        return i

    for j in range(NB_JOBS):
        s = j % 2
        P.dma('sync', lambda e, j=j, s=s: e.dma_start(out=qt[s][:], in_=BqT[j]), 'qt%d' % s, writes=['qt%d' % s])
        P.dma('sync', lambda e, j=j, s=s: e.dma_start(out=kt[s][:], in_=BkT[j]), 'kt%d' % s, writes=['kt%d' % s])
        for h in range(4):
            P.dma('sync', lambda e, j=j, s=s, h=h: e.dma_start(out=vt[s][:, h, :, :], in_=Bv[j, h].rearrange("(n p) c -> p n c", p=128)),
                  'vt%d_%d' % (s, h), writes=['vt%d_%d' % (s, h)])
        for i in range(JQ // 128):
            po = nxt('po', 2)
            for hp in range(2):
                pi = nxt('ps', 4)
                ps = pss[pi]
                for hh in range(2):
                    h = hp * 2 + hh
                    for tl in range(2):
                        P.mm(ps[:, (hh * 2 + tl) * 128:(hh * 2 + tl + 1) * 128], kt[s][:, h, (i + tl) * 128:(i + tl + 1) * 128],
                             qt[s][:, h, i * 128:(i + 1) * 128], True, True, reads=['kt%d' % s, 'qt%d' % s], writes=['pss%d' % pi])
                ti = nxt('pt', 4)
                P.op('scalar', lambda e, ti=ti, ps=ps: e.activation(out=pt[ti][:], in_=ps[:], func=AF.Exp, scale=0.125),
                     reads=['pss%d' % pi], writes=['pt%d' % ti])
                P.op('vector', lambda e, ti=ti: e.tensor_tensor(out=pt[ti][:], in0=pt[ti][:], in1=mk[:, 0, :], op=ALU.mult),
                     reads=['pt%d' % ti, 'mk'], writes=['pt%d' % ti])
                for hh in range(2):
                    h = hp * 2 + hh
                    for tl in range(2):
                        P.mm(pso[po][:, h * 65:(h + 1) * 65], pt[ti][:, (hh * 2 + tl) * 128:(hh * 2 + tl + 1) * 128], vt[s][:, h, i + tl, :],
                             tl == 0, tl == 1, reads=['pt%d' % ti, 'vt%d_%d' % (s, h)], writes=['pso%d' % po])
            oi = nxt('ob', 2)
            P.op('scalar', lambda e, oi=oi, po=po: e.copy(out=ob[oi][:], in_=pso[po][:, 0:260]),
                 reads=['pso%d' % po], writes=['ob%d' % oi])
            P.dma('gpsimd', lambda e, oi=oi, j=j, i=i: e.dma_start(out=yB[j, i * 128:(i + 1) * 128, :], in_=ob[oi][:]),
                  'ob%d' % oi, reads=['ob%d' % oi], final=True)
    for j in range(ND_JOBS):
        s = j % 2
        P.dma('sync', lambda e, j=j, s=s: e.dma_start(out=dq[s][:], in_=DqT[j]), 'dq%d' % s, writes=['dq%d' % s])
        P.dma('sync', lambda e, j=j, s=s: e.dma_start(out=dk[s][:], in_=DkT[j]), 'dk%d' % s, writes=['dk%d' % s])
        for g in range(2):
            P.dma('sync', lambda e, j=j, s=s, g=g: e.dma_start(out=dv[s][:, g, :, :], in_=Dv[j, g].rearrange("(n p) c -> p n c", p=128)),
                  'dv%d_%d' % (s, g), writes=['dv%d_%d' % (s, g)])
        for i in range(JQ // 128):
            for g in range(2):
                po = nxt('po', 2)
                tis = []
                for tl in range(3):
                    pi = nxt('ps', 4)
                    ps = pss[pi]
                    P.mm(ps[:], dk[s][:, g, (i + tl) * 128:(i + tl + 1) * 128], dq[s][:, g, i, :, :].rearrange("p h q -> p (h q)"),
                         True, True, reads=['dk%d' % s, 'dq%d' % s], writes=['pss%d' % pi])
                    ti = nxt('pt', 4)
                    tis.append(ti)
                    P.op('scalar', lambda e, ti=ti, ps=ps: e.activation(out=pt[ti][:], in_=ps[:], func=AF.Exp, scale=0.125),
                         reads=['pss%d' % pi], writes=['pt%d' % ti])
                    if tl != 1:
                        mi = 1 if tl == 0 else 2
                        P.op('vector', lambda e, ti=ti, mi=mi: e.tensor_tensor(out=pt[ti][:], in0=pt[ti][:], in1=mk[:, mi, :], op=ALU.mult),
                             reads=['pt%d' % ti, 'mk'], writes=['pt%d' % ti])
                for h in range(4):
                    for tl in range(3):
                        P.mm(pso[po][:, h * 65:(h + 1) * 65], pt[tis[tl]][:, h * 128:(h + 1) * 128], dv[s][:, g, i + tl, :],
                             tl == 0, tl == 2, reads=['pt%d' % tis[tl], 'dv%d_%d' % (s, g)], writes=['pso%d' % po])
                oi = nxt('od', 2)
                P.op('scalar', lambda e, oi=oi, po=po: e.copy(out=od[oi][:], in_=pso[po][:, 0:260]),
                     reads=['pso%d' % po], writes=['od%d' % oi])
                P.dma('gpsimd', lambda e, oi=oi, j=j, i=i, g=g: e.dma_start(out=yD[j, i * 128:(i + 1) * 128, g * 260:(g + 1) * 260], in_=od[oi][:]),
                      'od%d' % oi, reads=['od%d' % oi], final=True)
    P.emit()
    return nc


B_DIL = (1, 4, 16)


def l2a_host_inputs(oA, oB, inp, layer):
    bq = oB[0:768]
    bk = oB[768:1536]
    dqa = oB[1536:2048]
    dka = oB[2048:2176]
    bv = oA[768:1536]
    dva = oA[1536:1664]
    maps = [dict() for _ in range(NCORES)]
    BqT = np.zeros((NCORES, NB_JOBS, 64, 4, JQ), NPBF)
    BkT = np.zeros((NCORES, NB_JOBS, 64, 4, JQ + 128), NPBF)
    Bvv = np.zeros((NCORES, NB_JOBS, 4, JQ + 128, 65), NPBF)
    for g, r in enumerate(B_DIL):
        n = SEQ // r
        def strided(a):
            a = a.reshape(4, 64, n, r).transpose(0, 1, 3, 2)
            out = np.zeros((4, 64, r, n + 128), NPBF)
            out[..., 64:64 + n] = a
            return out
        q = strided(bq[g * 256:(g + 1) * 256])
        k = strided(bk[g * 256:(g + 1) * 256])
        v = strided(bv[g * 256:(g + 1) * 256])
        valid = np.zeros((n + 128,), NPBF)
        valid[64:64 + n] = 1
        ppr = n // JQ
        for jid in range(16):
            rho, pj = jid // ppr, jid % ppr
            c, jl = jid // 2, g * 2 + jid % 2
            BqT[c, jl] = q[:, :, rho, 64 + pj * JQ:64 + (pj + 1) * JQ].transpose(1, 0, 2)
            BkT[c, jl] = k[:, :, rho, pj * JQ:pj * JQ + JQ + 128].transpose(1, 0, 2)
            Bvv[c, jl, :, :, 0:64] = v[:, :, rho, pj * JQ:pj * JQ + JQ + 128].transpose(0, 2, 1)
            Bvv[c, jl, :, :, 64] = valid[pj * JQ:pj * JQ + JQ + 128][None, :]
    DqT = np.zeros((NCORES, ND_JOBS, 64, 2, 8, 4, 128), NPBF)
    DkT = np.zeros((NCORES, ND_JOBS, 64, 2, JQ + 256), NPBF)
    Dvv = np.zeros((NCORES, ND_JOBS, 2, JQ + 256, 65), NPBF)
    kpad = np.zeros((2, 64, SEQ + 256), NPBF)
    kpad[:, :, 128:128 + SEQ] = dka.reshape(2, 64, SEQ)
    vpad = np.zeros((2, 64, SEQ + 256), NPBF)
    vpad[:, :, 128:128 + SEQ] = dva.reshape(2, 64, SEQ)
    valid = np.zeros((SEQ + 256,), NPBF)
    valid[128:128 + SEQ] = 1
    dq4 = dqa.reshape(2, 4, 64, SEQ)
    for jid in range(16):
        c, jl = jid // 2, jid % 2
        t0 = jid * JQ
        DqT[c, jl] = dq4[:, :, :, t0:t0 + JQ].reshape(2, 4, 64, 8, 128).transpose(2, 0, 3, 1, 4)
        DkT[c, jl] = kpad[:, :, t0:t0 + JQ + 256].transpose(1, 0, 2)
        Dvv[c, jl, :, :, 0:64] = vpad[:, :, t0:t0 + JQ + 256].transpose(0, 2, 1)
        Dvv[c, jl, :, :, 64] = valid[t0:t0 + JQ + 256][None, :]
    kk = np.arange(128)
    ge = (kk[:, None] >= kk[None, :]).astype(np.float32)
    le = (kk[:, None] <= kk[None, :]).astype(np.float32)
    masks = np.stack([np.concatenate([ge, le, ge, le], 1), np.concatenate([ge] * 4, 1), np.concatenate([le] * 4, 1)], 1).astype(NPBF)
    sink = np.ascontiguousarray(np.broadcast_to(inp['d_sink'][layer][None, :], (128, 8))).astype(np.float32)
    for c in range(NCORES):
        maps[c] = dict(BqT=BqT[c], BkT=BkT[c], Bv=Bvv[c], DqT=DqT[c], DkT=DkT[c], Dv=Dvv[c], sink=sink, masks=masks)
    return maps


def l2a_host_outputs(res):
    accB = np.zeros((3, SEQ, 4, 65), np.float32)
    yD = np.zeros((SEQ, 520), np.float32)
    for g, r in enumerate(B_DIL):
        n = SEQ // r
        ppr = n // JQ
        tmp = np.zeros((r, n, 260), np.float32)
        for jid in range(16):
            rho, pj = jid // ppr, jid % ppr
            c, jl = jid // 2, g * 2 + jid % 2
            tmp[rho, pj * JQ:(pj + 1) * JQ] = res[c]['yB'][jl]
        accB[g] = tmp.transpose(1, 0, 2).reshape(SEQ, 4, 65)
    for jid in range(16):
        c, jl = jid // 2, jid % 2
        yD[jid * JQ:(jid + 1) * JQ] = res[c]['yD'][jl]
    return accB, yD.reshape(SEQ, 8, 65)


NCH = SEQ // 128


def build_l2m():
    nc = _new_nc()
    qTd = nc.dram_tensor("qT", [64, SEQ], BF16, kind="ExternalInput").ap()
    kTd = nc.dram_tensor("kT", [64, SEQ], BF16, kind="ExternalInput").ap()
    ktd = nc.dram_tensor("ktok", [SEQ, 64], BF16, kind="ExternalInput").ap()
    vtd = nc.dram_tensor("vtok", [SEQ, 64], BF16, kind="ExternalInput").ap()
    igd = nc.dram_tensor("ig", [128, NCH], F32, kind="ExternalInput").ap()
    lfd = nc.dram_tensor("lf", [128, NCH], F32, kind="ExternalInput").ap()
    cst = nc.dram_tensor("cst", [128, 3, 128], F32, kind="ExternalInput").ap()
    yM = nc.dram_tensor("yM", [SEQ, 64], F32, kind="ExternalOutput").ap()

    P = Prog(nc)
    qT = P.sb("qT", [64, SEQ], BF16)
    kT = P.sb("kT", [64, SEQ], BF16)
    kt = P.sb("kt", [128, NCH, 64], BF16)
    vt = P.sb("vt", [128, NCH, 64], BF16)
    ig = P.sb("ig", [128, NCH], F32)
    lf = P.sb("lf", [128, NCH], F32)
    cs = P.sb("cs", [128, 3, 128], F32)
    gS = P.sb("gS", [128, NCH], F32)
    gS2 = P.sb("gS2", [128, NCH], F32)
    eb = P.sb("eb", [128, NCH], F32)
    dec = P.sb("dec", [128, NCH], F32)
    dn = P.sb("dn", [128, NCH], F32)
    Va = P.sb("Va", [128, NCH, 65], BF16)
    Va2 = P.sb("Va2", [128, NCH, 65], BF16)
    raw = P.sb("raw", [128, NCH, 65], F32)
    sm = [P.sb("sm%d" % i, [128, 128], BF16) for i in range(2)]
    C = P.sb("C", [64, 65], F32)
    Cb = P.sb("Cb", [64, 65], BF16)
    psA = P.ps("psA")[:, 0:NCH]
    psB = P.ps("psB")[:, 0:NCH]
    psS = [P.ps("psS%d" % i)[:, 0:128] for i in range(2)]
    psO = [P.ps("psO%d" % i)[:, 0:65] for i in range(2)]
    psC = P.ps("psC")[0:64, 0:65]

    for (t, d, k) in ((qT, qTd, 'qT'), (kT, kTd, 'kT'), (ig, igd, 'ig'), (lf, lfd, 'lf'), (cs, cst, 'cs')):
        P.dma('sync', lambda e, t=t, d=d: e.dma_start(out=t[:], in_=d), k, writes=[k])
    P.dma('sync', lambda e: e.dma_start(out=kt[:], in_=ktd.rearrange("(n p) d -> p n d", p=128)), 'kt', writes=['kt'])
    P.dma('sync', lambda e: e.dma_start(out=vt[:], in_=vtd.rearrange("(n p) d -> p n d", p=128)), 'vt', writes=['vt'])
    P.mm(psA[:], cs[:, 0, :], lf[:], True, True, reads=['cs', 'lf'], writes=['psA'])
    P.mm(psB[:], cs[:, 1, :], lf[:], True, True, reads=['cs', 'lf'], writes=['psB'])
    P.op('vector', lambda e: e.tensor_tensor(out=gS[:], in0=ig[:], in1=psA[:], op=ALU.subtract), reads=['ig', 'psA'], writes=['gS'])
    P.op('scalar', lambda e: e.activation(out=gS[:], in_=gS[:], func=AF.Exp), reads=['gS'], writes=['gS'])
    P.op('scalar', lambda e: e.activation(out=eb[:], in_=psA[:], func=AF.Exp), reads=['psA'], writes=['eb'])
    P.op('scalar', lambda e: e.activation(out=dec[:], in_=psB[:], func=AF.Exp), reads=['psB'], writes=['dec'])
    P.op('vector', lambda e: e.tensor_tensor(out=gS2[:], in0=gS[:], in1=dec[:], op=ALU.mult), reads=['gS', 'dec'], writes=['gS2'])
    P.op('vector', lambda e: e.tensor_copy(out=Va[:, :, 64], in_=gS[:]), reads=['gS'], writes=['Va'])
    P.op('vector', lambda e: e.tensor_copy(out=Va2[:, :, 64], in_=gS2[:]), reads=['gS2'], writes=['Va2'])
    for n in range(NCH):
        P.op('vector', lambda e, n=n: e.tensor_scalar(out=Va[:, n, 0:64], in0=vt[:, n, :], scalar1=gS[:, n:n + 1], scalar2=None, op0=ALU.mult),
             reads=['vt', 'gS', 'Va'], writes=['Va'])
        P.op('gpsimd', lambda e, n=n: e.tensor_scalar(out=Va2[:, n, 0:64], in0=vt[:, n, :], scalar1=gS2[:, n:n + 1], scalar2=None, op0=ALU.mult),
             reads=['vt', 'gS2', 'Va2'], writes=['Va2'])
    P.op('vector', lambda e: e.memset(C[:], 0.0), writes=['C'])
    for n in range(NCH):
        s = n % 2
        sl = slice(n * 128, (n + 1) * 128)
        P.mm(psS[s][:], kT[:, sl], qT[:, sl], True, True, reads=['kT', 'qT'], writes=['psS%d' % s])
        P.op('vector', lambda e, s=s: e.tensor_tensor(out=sm[s][:], in0=psS[s][:], in1=cs[:, 2, :], op=ALU.mult),
             reads=['psS%d' % s, 'cs'], writes=['sm%d' % s])
        P.mm(psO[s][:], sm[s][:], Va[:, n, :], True, n == 0, reads=['sm%d' % s, 'Va'], writes=['psO%d' % s])
        if n > 0:
            P.mm(psO[s][:], qT[:, sl], Cb[:], False, True, reads=['qT', 'Cb'], writes=['psO%d' % s])
        P.op('scalar', lambda e, s=s, n=n: e.copy(out=raw[:, n, :], in_=psO[s][:]), reads=['psO%d' % s], writes=['raw'])
        if n < NCH - 1:
            P.mm(psC[:], kt[:, n, :], Va2[:, n, :], True, True, reads=['kt', 'Va2'], writes=['psC'])
            P.op('vector', lambda e, n=n: e.scalar_tensor_tensor(out=C[:], in0=C[:], scalar=dec[0:64, n:n + 1], in1=psC[:], op0=ALU.mult, op1=ALU.add),
                 reads=['C', 'dec', 'psC'], writes=['C'])
            P.op('scalar', lambda e: e.copy(out=Cb[:], in_=C[:]), reads=['C'], writes=['Cb'])
    P.op('vector', lambda e: e.tensor_tensor(out=dn[:], in0=raw[:, :, 64], in1=eb[:], op=ALU.mult), reads=['raw', 'eb'], writes=['dn'])
    P.op('scalar', lambda e: e.activation(out=dn[:], in_=dn[:], func=AF.Abs), reads=['dn'], writes=['dn'])
    P.op('vector', lambda e: e.tensor_scalar(out=dn[:], in0=dn[:], scalar1=1.0, scalar2=None, op0=ALU.max), reads=['dn'], writes=['dn'])
    P.op('vector', lambda e: e.reciprocal(out=dn[:], in_=dn[:]), reads=['dn'], writes=['dn'])
    P.op('vector', lambda e: e.tensor_tensor(out=dn[:], in0=dn[:], in1=eb[:], op=ALU.mult), reads=['dn', 'eb'], writes=['dn'])
    for n in range(NCH):
        eng = ('vector', 'gpsimd')[n % 2]
        P.op(eng, lambda e, n=n: e.tensor_scalar(out=raw[:, n, 0:64], in0=raw[:, n, 0:64], scalar1=dn[:, n:n + 1], scalar2=None, op0=ALU.mult),
             reads=['raw', 'dn'], writes=['raw'])
    P.dma('sync', lambda e: e.dma_start(out=yM.rearrange("(n p) d -> p n d", p=128), in_=raw[:, :, 0:64]), 'raw', reads=['raw'], final=True)
    P.emit()
    return nc


def seq_consts():
    i = np.arange(128)
    triT = (i[:, None] <= i[None, :]).astype(np.float32)
    ones = np.ones((128, 128), np.float32)
    return np.ascontiguousarray(np.stack([triT, ones, triT], 1))


def l2m_host_inputs(oA, oG):
    cst = seq_consts()
    maps = []
    for u in range(NCORES):
        d, h = u // 4, u % 4
        fl = (lambda a: a[..., ::-1]) if d == 1 else (lambda a: a)
        qT = np.ascontiguousarray(fl(oA[h * 64:(h + 1) * 64]))
        kT = np.ascontiguousarray(fl(oA[256 + h * 64:256 + (h + 1) * 64]))
        vT = fl(oA[512 + h * 64:512 + (h + 1) * 64])
        ig = fl(oG[d * 4 + h])
        lf = fl(oG[32 + d * 4 + h])
        maps.append(dict(qT=qT, kT=kT, ktok=np.ascontiguousarray(kT.T), vtok=np.ascontiguousarray(vT.T),
                         ig=np.ascontiguousarray(ig.reshape(NCH, 128).T), lf=np.ascontiguousarray(lf.reshape(NCH, 128).T), cst=cst))
    return maps


def l2m_host_outputs(res):
    out = np.zeros((2, 256, SEQ), np.float32)
    for u in range(NCORES):
        d, h = u // 4, u % 4
        y = res[u]['yM']
        if d == 1:
            y = y[::-1]
        out[d, h * 64:(h + 1) * 64] = y.T
    return out


def build_l2d():
    nc = _new_nc()
    qTd = nc.dram_tensor("qT", [64, SEQ], BF16, kind="ExternalInput").ap()
    kTd = nc.dram_tensor("kT", [64, SEQ], BF16, kind="ExternalInput").ap()
    ktd = nc.dram_tensor("ktok", [SEQ, 64], BF16, kind="ExternalInput").ap()
    vtd = nc.dram_tensor("vtok", [SEQ, 64], BF16, kind="ExternalInput").ap()
    btd = nc.dram_tensor("beta", [128, NCH], F32, kind="ExternalInput").ap()
    gd = nc.dram_tensor("g", [128, NCH], F32, kind="ExternalInput").ap()
    cst = nc.dram_tensor("cst", [128, 5, 128], F32, kind="ExternalInput").ap()
    yC = nc.dram_tensor("yC", [SEQ, 64], F32, kind="ExternalOutput").ap()

    P = Prog(nc)
    G = 4
    R = 2 * G
    qT = P.sb("qT", [64, SEQ], BF16)
    kT = P.sb("kT", [64, SEQ], BF16)
    kt = P.sb("kt", [128, NCH, 64], BF16)
    vt = P.sb("vt", [128, NCH, 64], BF16)
    kdec = P.sb("kdec", [128, NCH, 64], BF16)
    oout = P.sb("oout", [128, NCH, 64], F32)
    beta = P.sb("beta", [128, NCH], F32)
    g = P.sb("g", [128, NCH], F32)
    cs = P.sb("cs", [128, 5, 128], F32)
    gc = P.sb("gc", [128, NCH], F32)
    ngc = P.sb("ngc", [128, NCH], F32)
    egc = P.sb("egc", [128, NCH], F32)
    negc = P.sb("negc", [128, NCH], F32)
    egl = P.sb("egl", [128, NCH], F32)
    kfac = P.sb("kfac", [128, NCH], F32)
    Gb = [P.sb("Gb%d" % i, [128, 128], F32) for i in range(G)]
    decL = [P.sb("decL%d" % i, [128, 128], F32) for i in range(G)]
    decT = [P.sb("decT%d" % i, [128, 128], F32) for i in range(G)]
    Ak = [[P.sb("Ak%d_%d" % (i, j), [128, 128], F32) for j in range(2)] for i in range(G)]
    Bk = [[P.sb("Bk%d_%d" % (i, j), [128, 128], F32) for j in range(2)] for i in range(G)]
    Yk = [[P.sb("Yk%d_%d" % (i, j), [128, 128], F32) for j in range(2)] for i in range(G)]
    TT = [P.sb("TT%d" % i, [128, 128], F32) for i in range(R)]
    aT = [P.sb("aT%d" % i, [128, 128], BF16) for i in range(R)]
    Z = P.sb("Z", [128, 64], F32)
    vn = P.sb("vn", [128, 64], BF16)
    o2s = P.sb("o2s", [128, 64], F32)
    S = P.sb("S", [64, 64], F32)
    Sb = P.sb("Sb", [64, 64], BF16)
    tb = [P.ps("tbk%d" % i) for i in range(6)]
    psRA = P.ps("psRA")
    psRB = P.ps("psRB")
    H0, H1 = slice(0, 128), slice(128, 256)
    st = dict(b=0)

    def nbk():
        i = st['b']
        st['b'] = (i + 1) % 6
        return tb[i], 'tbk%d' % i

    for (t, d, k) in ((qT, qTd, 'qT'), (kT, kTd, 'kT'), (beta, btd, 'beta'), (g, gd, 'g'), (cs, cst, 'cs')):
        P.dma('sync', lambda e, t=t, d=d: e.dma_start(out=t[:], in_=d), k, writes=[k])
    P.dma('sync', lambda e: e.dma_start(out=kt[:], in_=ktd.rearrange("(n p) d -> p n d", p=128)), 'kt', writes=['kt'])
    P.dma('sync', lambda e: e.dma_start(out=vt[:], in_=vtd.rearrange("(n p) d -> p n d", p=128)), 'vt', writes=['vt'])
    TRI, ONES, IDN, M1, M2 = (cs[:, i, :] for i in range(5))
    psG_, pgk = nbk()
    P.mm(psG_[:, H0], TRI, g[:], True, True, reads=['cs', 'g'], writes=[pgk])
    P.mm(psG_[:, H1], ONES, g[:], True, True, reads=['cs', 'g'], writes=[pgk])
    P.op('vector', lambda e: e.tensor_copy(out=gc[:], in_=psG_[:, H0]), reads=[pgk], writes=['gc'])
    P.op('vector', lambda e: e.tensor_scalar(out=ngc[:], in0=psG_[:, H0], scalar1=-1.0, scalar2=None, op0=ALU.mult), reads=[pgk], writes=['ngc'])
    P.op('scalar', lambda e: e.activation(out=egc[:], in_=psG_[:, H0], func=AF.Exp), reads=[pgk], writes=['egc'])
    P.op('vector', lambda e: e.tensor_scalar(out=negc[:], in0=egc[:], scalar1=-1.0, scalar2=None, op0=ALU.mult), reads=['egc'], writes=['negc'])
    P.op('scalar', lambda e: e.activation(out=egl[:], in_=psG_[:, H1], func=AF.Exp), reads=[pgk], writes=['egl'])
    P.op('vector', lambda e: e.tensor_tensor(out=kfac[:], in0=psG_[:, H1], in1=gc[:], op=ALU.subtract), reads=[pgk, 'gc'], writes=['kfac'])
    P.op('scalar', lambda e: e.activation(out=kfac[:], in_=kfac[:], func=AF.Exp), reads=['kfac'], writes=['kfac'])
    for n in range(NCH):
        P.op('gpsimd', lambda e, n=n: e.tensor_scalar(out=kdec[:, n, :], in0=kt[:, n, :], scalar1=kfac[:, n:n + 1], scalar2=None, op0=ALU.mult),
             reads=['kt', 'kfac'], writes=['kdec%d' % (n // G)])
    P.op('vector', lambda e: e.memset(S[:], 0.0), writes=['S'])

    def tstages(n):
        gi = n % G
        r = n % R
        sl = slice(n * 128, (n + 1) * 128)
        kA = ['Ak%d_0' % gi, 'Ak%d_1' % gi]
        kB = ['Bk%d_0' % gi, 'Bk%d_1' % gi]
        kY = ['Yk%d_0' % gi, 'Yk%d_1' % gi]
        A_, B_, Y_ = Ak[gi], Bk[gi], Yk[gi]
        cur = {}
        stages = []

        def s0():
            P.op('gpsimd', lambda e: e.tensor_scalar(out=Gb[gi][:], in0=ONES, scalar1=g[:, n:n + 1], scalar2=None, op0=ALU.mult),
                 reads=['cs', 'g'], writes=['Gb%d' % gi])
        stages.append(s0)

        def s1():
            a, ak = nbk()
            cur['a'] = (a, ak)
            P.mm(a[:, H0], Gb[gi][:], TRI, True, False, reads=['Gb%d' % gi, 'cs'], writes=[ak])
            P.mm(a[:, H0], IDN, M1, False, True, reads=['cs'], writes=[ak])
            P.mm(a[:, H1], Gb[gi][:], TRI, True, False, reads=['Gb%d' % gi, 'cs'], writes=[ak])
            P.mm(a[:, H1], IDN, M2, False, True, reads=['cs'], writes=[ak])
        stages.append(s1)

        def s2():
            a, ak = cur['a']
            P.op('scalar', lambda e: e.activation(out=decL[gi][:], in_=a[:, H0], func=AF.Exp, bias=gc[:, n:n + 1], scale=-1.0),
                 reads=[ak, 'gc'], writes=['decL%d' % gi])
            P.op('scalar', lambda e: e.activation(out=decT[gi][:], in_=a[:, H1], func=AF.Exp, bias=ngc[:, n:n + 1], scale=1.0),
                 reads=[ak, 'ngc'], writes=['decT%d' % gi])
        stages.append(s2)

        def s2b():
            b_, bk_ = nbk()
            cur['b'] = (b_, bk_)
            P.mm(b_[:, H0], kT[:, sl], kT[:, sl], True, True, reads=['kT'], writes=[bk_])
            P.mm(b_[:, H1], kT[:, sl], qT[:, sl], True, True, reads=['kT', 'qT'], writes=[bk_])
        stages.append(s2b)

        def s3():
            b_, bk_ = cur['b']
            P.op('vector', lambda e: e.scalar_tensor_tensor(out=A_[0][:], in0=b_[:, H0], scalar=beta[:, n:n + 1], in1=decL[gi][:], op0=ALU.mult, op1=ALU.mult),
                 reads=[bk_, 'beta', 'decL%d' % gi], writes=[kA[0]])
            P.op('vector', lambda e: e.tensor_tensor(out=aT[r][:], in0=b_[:, H1], in1=decT[gi][:], op=ALU.mult),
                 reads=[bk_, 'decT%d' % gi], writes=['aT%d' % r])
        stages.append(s3)

        def s4():
            c, ck = nbk()
            cur['c'] = (c, ck)
            P.mm(c[:, H0], A_[0][:], IDN, True, True, reads=[kA[0], 'cs'], writes=[ck])
        stages.append(s4)

        def s5():
            c, ck = cur['c']
            P.op('scalar', lambda e: e.copy(out=B_[0][:], in_=c[:, H0]), reads=[ck], writes=[kB[0]])
            P.op('vector', lambda e: e.tensor_tensor(out=Y_[0][:], in0=IDN, in1=c[:, H0], op=ALU.subtract), reads=[ck, 'cs'], writes=[kY[0]])
        stages.append(s5)
        for k in range(1, 7):
            a_i, b_i = (k - 1) % 2, k % 2

            def sa(k=k, a_i=a_i, b_i=b_i):
                d, dk = nbk()
                cur['d'] = (d, dk)
                P.mm(d[:, H0], B_[a_i][:], A_[a_i][:], True, True, reads=[kB[a_i], kA[a_i]], writes=[dk])
                if k < 6:
                    P.mm(d[:, H1], A_[a_i][:], B_[a_i][:], True, True, reads=[kB[a_i], kA[a_i]], writes=[dk])
            stages.append(sa)

            def sb_(k=k, a_i=a_i, b_i=b_i):
                d, dk = cur['d']
                P.op('scalar', lambda e: e.copy(out=A_[b_i][:], in_=d[:, H0]), reads=[dk], writes=[kA[b_i]])
                if k < 6:
                    P.op('vector', lambda e: e.tensor_copy(out=B_[b_i][:], in_=d[:, H1]), reads=[dk], writes=[kB[b_i]])
            stages.append(sb_)

            def sc(k=k, a_i=a_i, b_i=b_i):
                y, yk = nbk()
                cur['y'] = (y, yk)
                P.mm(y[:, H0], A_[b_i][:], Y_[a_i][:], True, True, reads=[kA[b_i], kY[a_i]], writes=[yk])
            stages.append(sc)

            def sd(k=k, a_i=a_i, b_i=b_i):
                y, yk = cur['y']
                if k < 6:
                    P.op('vector', lambda e: e.tensor_tensor(out=Y_[b_i][:], in0=Y_[a_i][:], in1=y[:, H0], op=ALU.add),
                         reads=[yk, kY[a_i]], writes=[kY[b_i]])
                else:
                    P.op('vector', lambda e: e.tensor_tensor(out=Y_[b_i][:], in0=Y_[a_i][:], in1=y[:, H0], op=ALU.add),
                         reads=[yk, kY[a_i]], writes=[kY[b_i]])
            stages.append(sd)

        def sf():
            P.op('gpsimd', lambda e: e.tensor_scalar(out=TT[r][:], in0=Y_[0][:], scalar1=beta[:, n:n + 1], scalar2=None, op0=ALU.mult),
                 reads=[kY[0], 'beta'], writes=['TT%d' % r])
        stages.append(sf)
        return stages

    def rsteps(n):
        r = n % R
        sl = slice(n * 128, (n + 1) * 128)
        kd = 'kdec%d' % (n // G)
        steps = []
        if n > 0:
            def r0():
                P.mm(psRA[:, 0:64], kT[:, sl], Sb[:], True, True, reads=['kT', 'Sb'], writes=['psRA'])
                P.mm(psRA[:, 64:128], qT[:, sl], Sb[:], True, True, reads=['qT', 'Sb'], writes=['psRA'])

            def r1():
                P.op('vector', lambda e: e.scalar_tensor_tensor(out=Z[:], in0=psRA[:, 0:64], scalar=negc[:, n:n + 1], in1=vt[:, n, :], op0=ALU.mult, op1=ALU.add),
                     reads=['psRA', 'negc', 'vt'], writes=['Z'])
                P.op('vector', lambda e: e.tensor_scalar(out=o2s[:], in0=psRA[:, 64:128], scalar1=egc[:, n:n + 1], scalar2=None, op0=ALU.mult),
                     reads=['psRA', 'egc'], writes=['o2s'])
        else:
            def r0():
                pass

            def r1():
                P.op('vector', lambda e: e.tensor_copy(out=Z[:], in_=vt[:, n, :]), reads=['vt'], writes=['Z'])
                P.op('vector', lambda e: e.memset(o2s[:], 0.0), writes=['o2s'])
        steps += [r0, r1]

        def r2():
            P.mm(psRB[:, 0:64], TT[r][:], Z[:], True, True, reads=['TT%d' % r, 'Z'], writes=['psRB'])

        def r3():
            P.op('scalar', lambda e: e.copy(out=vn[:], in_=psRB[:, 0:64]), reads=['psRB'], writes=['vn'])

        def r4():
            if n < NCH - 1:
                P.mm(psRB[0:64, 128:192], kdec[:, n, :], vn[:], True, True, reads=[kd, 'vn'], writes=['psRB'])
            P.mm(psRB[:, 64:128], aT[r][:], vn[:], True, True, reads=['aT%d' % r, 'vn'], writes=['psRB'])

        def r5():
            if n < NCH - 1:
                P.op('vector', lambda e: e.scalar_tensor_tensor(out=S[:], in0=S[:], scalar=egl[0:64, n:n + 1], in1=psRB[0:64, 128:192], op0=ALU.mult, op1=ALU.add),
                     reads=['S', 'egl', 'psRB'], writes=['S'])

        def r6():
            if n < NCH - 1:
                P.op('scalar', lambda e: e.copy(out=Sb[:], in_=S[:]), reads=['S'], writes=['Sb'])
            P.op('vector', lambda e: e.tensor_tensor(out=oout[:, n, :], in0=o2s[:], in1=psRB[:, 64:128], op=ALU.add),
                 reads=['psRB', 'o2s'], writes=['oout'])
        steps += [r2, r3, r4, r5, r6]
        return steps

    ngroups = NCH // G
    for gi in range(ngroups + 1):
        tl = [tstages(gi * G + j) for j in range(G)] if gi < ngroups else []
        rl = []
        if gi >= 1:
            for j in range(G):
                rl += rsteps((gi - 1) * G + j)
        nst = len(tl[0]) if tl else 0
        for s in range(max(nst, len(rl))):
            if s < nst:
                for j in range(G):
                    tl[j][s]()
            if s < len(rl):
                rl[s]()
    P.dma('sync', lambda e: e.dma_start(out=yC.rearrange("(n p) d -> p n d", p=128), in_=oout[:]), 'oout', reads=['oout'], final=True)
    P.emit()
    return nc


def l2d_consts():
    i = np.arange(128)
    triT = (i[:, None] <= i[None, :]).astype(np.float32)
    ones = np.ones((128, 128), np.float32)
    ident = np.eye(128, dtype=np.float32)
    BIG = 1.0e5
    m1 = np.where(i[None, :] < i[:, None], 0.0, BIG).astype(np.float32)
    m2 = np.where(i[None, :] >= i[:, None], 0.0, -BIG).astype(np.float32)
    return np.ascontiguousarray(np.stack([triT, ones, ident, m1, m2], 1))


def l2d_host_inputs(oC, oG):
    cst = l2d_consts()
    maps = []
    for u in range(NCORES):
        d, h = u // 4, u % 4
        fl = (lambda a: a[..., ::-1]) if d == 1 else (lambda a: a)
        qT = np.ascontiguousarray(fl(oC[h * 64:(h + 1) * 64]))
        kT = np.ascontiguousarray(fl(oC[256 + h * 64:256 + (h + 1) * 64]))
        vT = fl(oC[512 + h * 64:512 + (h + 1) * 64])
        bt = fl(oG[64 + d * 4 + h])
        gg = fl(oG[96 + d * 4 + h])
        maps.append(dict(qT=qT, kT=kT, ktok=np.ascontiguousarray(kT.T), vtok=np.ascontiguousarray(vT.T),
                         beta=np.ascontiguousarray(bt.reshape(NCH, 128).T), g=np.ascontiguousarray(gg.reshape(NCH, 128).T), cst=cst))
    return maps


def l2d_host_outputs(res):
    out = np.zeros((2, 256, SEQ), np.float32)
    for u in range(NCORES):
        d, h = u // 4, u % 4
        y = res[u]['yC']
        if d == 1:
            y = y[::-1]
        out[d, h * 64:(h + 1) * 64] = y.T
    return out


CAST_TF = 4096
CAST_NT = 16


def build_l0():
    nc = _new_nc()
    src = nc.dram_tensor("src", [128, CAST_NT * CAST_TF], F32, kind="ExternalInput").ap()
    dst = nc.dram_tensor("dst", [128, CAST_NT * CAST_TF], BF16, kind="ExternalOutput").ap()
    P = Prog(nc)
    a = [P.sb("ca%d" % i, [128, CAST_TF], F32) for i in range(3)]
    b = [P.sb("cb%d" % i, [128, CAST_TF], BF16) for i in range(3)]
    for t in range(CAST_NT):
        s = t % 3
        sl = slice(t * CAST_TF, (t + 1) * CAST_TF)
        P.dma('sync', lambda e, s=s, sl=sl: e.dma_start(out=a[s][:], in_=src[:, sl]), 'ca%d' % s, writes=['ca%d' % s])
        eng = ('vector', 'gpsimd', 'scalar')[t % 3]
        if eng == 'scalar':
            P.op(eng, lambda e, s=s: e.copy(out=b[s][:], in_=a[s][:]), reads=['ca%d' % s], writes=['cb%d' % s])
        else:
            P.op(eng, lambda e, s=s: e.tensor_copy(out=b[s][:], in_=a[s][:]), reads=['ca%d' % s], writes=['cb%d' % s])
        P.dma('gpsimd', lambda e, s=s, sl=sl: e.dma_start(out=dst[:, sl], in_=b[s][:]), 'cb%d' % s, reads=['cb%d' % s], final=True)
    P.emit()
    return nc


def cast_weights_on_device(arrs, nc0, launch):
    names = list(arrs.keys())
    flat = np.concatenate([np.ascontiguousarray(arrs[n]).reshape(-1) for n in names])
    per_launch = NCORES * 128 * CAST_NT * CAST_TF
    nl = (flat.size + per_launch - 1) // per_launch
    buf = np.zeros(nl * per_launch, np.float32)
    buf[:flat.size] = flat
    buf = buf.reshape(nl, NCORES, 128, CAST_NT * CAST_TF)
    out = np.zeros(buf.shape, NPBF)
    for i in range(nl):
        res = launch(nc0, [dict(src=buf[i, c]) for c in range(NCORES)])
        for c in range(NCORES):
            out[i, c] = res[c]['dst']
    out = out.reshape(-1)
    ret = {}
    off = 0
    for n in names:
        sz = arrs[n].size
        ret[n] = out[off:off + sz].reshape(arrs[n].shape)
        off += sz
    return ret


NPASS = TC // 512
PC_ANW, PC_CNW, PC_SINK, PC_LN1W, PC_LN1B, PC_LN2W, PC_LN2B = 0, 2, 3, 4, 12, 20, 28
CS_BONES, CS_ONES, CS_IDENT, CS_E4, CS_E8, CS_EE = 0, 128, 256, 384, 640, 1152
CS_N = 1152 + 1024


def build_l3(moe, dbg=False):
    nc = _new_nc()
    T = TC
    E = 8 if moe else 1
    J = 11 if moe else 22
    xT = nc.dram_tensor("xT", [D, T], F32, kind="ExternalInput").ap()
    hA = nc.dram_tensor("hA", [2, 256, T], F32, kind="ExternalInput").ap()
    oCd = nc.dram_tensor("oCd", [2, 256, T], F32, kind="ExternalInput").ap()
    accB = nc.dram_tensor("accB", [3, 256, T], F32, kind="ExternalInput").ap()
    lB = nc.dram_tensor("lB", [3, 4, T], F32, kind="ExternalInput").ap()
    accD = nc.dram_tensor("accD", [512, T], F32, kind="ExternalInput").ap()
    lD = nc.dram_tensor("lD", [8, T], F32, kind="ExternalInput").ap()
    wg = nc.dram_tensor("wg", [9, 128, 8 * 512], BF16, kind="ExternalInput").ap()
    wbr = nc.dram_tensor("wbr", [128, 10 * 1024], BF16, kind="ExternalInput").ap()
    wout = nc.dram_tensor("wout", [128, 8 * 1024], BF16, kind="ExternalInput").ap()
    pcol = nc.dram_tensor("pcol", [128, 40], F32, kind="ExternalInput").ap()
    cst3 = nc.dram_tensor("cst3", [128, CS_N], F32, kind="ExternalInput").ap()
    w1d = nc.dram_tensor("w1p", [E, J, 128, 8 * 128], BF16, kind="ExternalInput").ap()
    w3d = nc.dram_tensor("w3p", [E, J, 128, 8 * 128], BF16, kind="ExternalInput").ap()
    w2d = nc.dram_tensor("w2p", [E, 8, 128, J * 128], BF16, kind="ExternalInput").ap()
    if moe:
        rtd = nc.dram_tensor("router", [128, 8 * 8], F32, kind="ExternalInput").ap()
    xo = nc.dram_tensor("xo", [D, T], F32, kind="ExternalOutput").ap()
    if dbg:
        dbg_g = nc.dram_tensor("dbg_g", [8, T], F32, kind="ExternalOutput").ap()
        dbg_x = nc.dram_tensor("dbg_x", [D, T], F32, kind="ExternalOutput").ap()
        dbg_l = nc.dram_tensor("dbg_l", [NPASS, 128, 64], F32, kind="ExternalOutput").ap()

    P = Prog(nc)
    xr = P.sb("xr", [128, 8, 512], F32)
    xb = P.sb("xb", [128, 8, 512], BF16)
    yb = P.sb("yb", [128, 10, 512], BF16)
    mg = P.sb("mg", [128, 8, 512], BF16)
    hT = P.sb("hT", [128, 22, 512], BF16)
    wbr_t = P.sb("wbr_t", [128, 10, 1024], BF16)
    wout_t = P.sb("wout_t", [128, 8, 1024], BF16)
    wgr = [P.sb("wgr%d" % i, [128, 8, 512], BF16) for i in range(2)]
    w1r = [P.sb("w1r%d" % i, [128, 8, 128], BF16) for i in range(4)]
    w3r = [P.sb("w3r%d" % i, [128, 8, 128], BF16) for i in range(4)]
    w2r = [P.sb("w2r%d" % i, [128, J, 128], BF16) for i in range(3)]
    pc = P.sb("pc", [128, 40], F32)
    cs = P.sb("cs", [128, CS_N], F32)
    esink = P.sb("esink", [8, 1], F32)
    eps5 = P.sb("eps5", [128, 1], F32)
    eps6 = P.sb("eps6", [128, 1], F32)
    class _TT(dict):
        ALIAS = {'ts0': 'td0', 'ts1': 'td1', 'tsq0': 'tm0', 'tsq1': 'tm1', 'tn0': 'tacc0', 'tn1': 'tacc1'}
    tt = _TT({n: P.sb(n, [128, 512], F32) for n in ('ta', 'tb', 'tc', 'td0', 'td1', 'tm0', 'tm1', 'tacc0', 'tacc1', 'tmean', 'tmsq', 'trstd')})
    l4 = [P.sb("l4_%d" % i, [4, 512], F32) for i in range(3)]
    l8 = P.sb("l8", [8, 512], F32)
    if moe:
        rt = P.sb("rt", [128, 8, 8], F32)
        lg = P.sb("lg", [128, 4, 8], F32)
        lg2 = P.sb("lg2", [128, 8], F32)
        gsel = P.sb("gsel", [128, 8], F32)
        m1 = P.sb("m1", [128, 4], F32)
        gts = P.sb("gts", [128, 4, 8], F32)
        gT = P.sb("gT", [8, 512], F32)
        gb = P.sb("gb", [128, 8, 512], F32)
    banks = [P.ps("bk%d" % i) for i in range(8)]
    st = dict(b=0, w1=0, w3=0, w2=0, wg=0, td=0, tm=0, tacc=0, tsq=0, tn=0, ts=0)

    def nb():
        i = st['b']
        st['b'] = (i + 1) % 8
        return banks[i], 'bk%d' % i

    def rot(name, n):
        i = st[name]
        st[name] = (i + 1) % n
        return i

    BONES = cs[:, CS_BONES:CS_BONES + 128]
    ONES = cs[:, CS_ONES:CS_ONES + 128]
    IDN = cs[:, CS_IDENT:CS_IDENT + 128]

    P.dma('sync', lambda e: e.dma_start(out=pc[:], in_=pcol), 'pc', writes=['pc'])
    P.dma('sync', lambda e: e.dma_start(out=cs[:], in_=cst3), 'cs', writes=['cs'])
    P.dma('sync', lambda e: e.dma_start(out=wbr_t[:].rearrange("p k c -> p (k c)"), in_=wbr), 'wbr', writes=['wbr'])
    P.dma('sync', lambda e: e.dma_start(out=wout_t[:].rearrange("p k c -> p (k c)"), in_=wout), 'wout', writes=['wout'])
    if moe:
        P.dma('sync', lambda e: e.dma_start(out=rt[:].rearrange("p k c -> p (k c)"), in_=rtd), 'rt', writes=['rt'])
    P.op('vector', lambda e: e.memset(eps5[:], 1e-5), writes=['eps5'])
    P.op('vector', lambda e: e.memset(eps6[:], 1e-6), writes=['eps6'])
    P.op('scalar', lambda e: e.activation(out=esink[:], in_=pc[0:8, PC_SINK:PC_SINK + 1], func=AF.Exp), reads=['pc'], writes=['esink'])

    def proj(wt, wkey, col, rhs_t, rhs_key):
        ps, pk = nb()
        for k in range(8):
            P.mm(ps[:, 0:512], wt[:, k, col:col + 128], rhs_t[:, k, :], k == 0, k == 7, reads=[wkey, rhs_key], writes=[pk])
        return ps, pk

    def load_wg(gi):
        s = rot('wg', 2)
        P.dma('sync', lambda e: e.dma_start(out=wgr[s][:].rearrange("p k c -> p (k c)"), in_=wg[gi]), 'wgr%d' % s, writes=['wgr%d' % s])
        return wgr[s], 'wgr%d' % s

    def rsqrt_from_psum(ps, pk, scale, epst, ekey, out_t, okey):
        P.op('scalar', lambda e: e.activation(out=out_t[:], in_=ps[:, 0:512], func=AF.Sqrt, bias=epst[:, 0:1], scale=scale),
             reads=[pk, ekey], writes=[okey])
        P.op('vector', lambda e: e.reciprocal(out=out_t[:], in_=out_t[:]), reads=[okey], writes=[okey])

    def layer_norm(wc, bc, make_bf16, out_dram_t0=None):
        s1, k1 = nb()
        for k in range(8):
            P.mm(s1[:, 0:512], ONES, xr[:, k, :], k == 0, k == 7, reads=['cs', 'xr%d' % k], writes=[k1])
        s2, k2 = nb()
        for k in range(8):
            i = rot('tsq', 2)
            tq = tt['tm%d' % i]
            P.op('scalar', lambda e, tq=tq, k=k: e.activation(out=tq[:], in_=xr[:, k, :], func=AF.Square), reads=['xr%d' % k], writes=['tm%d' % i])
            P.mm(s2[:, 0:512], ONES, tq[:], k == 0, k == 7, reads=['cs', 'tm%d' % i], writes=[k2])
        tmean, tmsq, trstd = tt['tmean'], tt['tmsq'], tt['trstd']
        P.op('vector', lambda e: e.tensor_scalar(out=tmean[:], in0=s1[:, 0:512], scalar1=1.0 / D, scalar2=None, op0=ALU.mult), reads=[k1], writes=['tmean'])
        P.op('gpsimd', lambda e: e.tensor_tensor(out=tmsq[:], in0=tmean[:], in1=tmean[:], op=ALU.mult), reads=['tmean'], writes=['tmsq'])
        P.op('vector', lambda e: e.scalar_tensor_tensor(out=trstd[:], in0=s2[:, 0:512], scalar=1.0 / D, in1=tmsq[:], op0=ALU.mult, op1=ALU.subtract),
             reads=[k2, 'tmsq'], writes=['trstd'])
        P.op('scalar', lambda e: e.activation(out=trstd[:], in_=trstd[:], func=AF.Sqrt, bias=eps5[:, 0:1], scale=1.0), reads=['trstd', 'eps5'], writes=['trstd'])
        P.op('vector', lambda e: e.reciprocal(out=trstd[:], in_=trstd[:]), reads=['trstd'], writes=['trstd'])
        for k in range(8):
            i = rot('tn', 2)
            tn = tt['tacc%d' % i]
            P.op('vector', lambda e, tn=tn, k=k: e.tensor_tensor(out=tn[:], in0=xr[:, k, :], in1=tmean[:], op=ALU.subtract),
                 reads=['xr%d' % k, 'tmean'], writes=['tacc%d' % i])
            P.op('gpsimd', lambda e, tn=tn: e.tensor_tensor(out=tn[:], in0=tn[:], in1=trstd[:], op=ALU.mult), reads=['tacc%d' % i, 'trstd'], writes=['tacc%d' % i])
            P.op('scalar', lambda e, tn=tn, k=k: e.activation(out=xr[:, k, :], in_=tn[:], func=AF.Identity, bias=pc[:, bc + k:bc + k + 1], scale=pc[:, wc + k:wc + k + 1]),
                 reads=['tacc%d' % i, 'pc'], writes=['xr%d' % k])
            if make_bf16:
                P.op('gpsimd', lambda e, k=k: e.tensor_copy(out=xb[:, k, :], in_=xr[:, k, :]), reads=['xr%d' % k], writes=['xb'])
            if out_dram_t0 is not None:
                t0 = out_dram_t0
                P.dma('gpsimd', lambda e, k=k, t0=t0: e.dma_start(out=xo[k * 128:(k + 1) * 128, t0:t0 + 512], in_=xr[:, k, :]),
                      'xr%d' % k, reads=['xr%d' % k], final=True)

    xrk = ['xr%d' % k for k in range(8)]
    for ps_i in range(NPASS):
        t0 = ps_i * 512
        tsl = slice(t0, t0 + 512)
        for k in range(8):
            P.dma('sync', lambda e, k=k, tsl=tsl: e.dma_start(out=xr[:, k, :], in_=xT[k * 128:(k + 1) * 128, tsl]), 'xr%d' % k, writes=['xr%d' % k])
            P.op(('vector', 'gpsimd')[k % 2], lambda e, k=k: e.tensor_copy(out=xb[:, k, :], in_=xr[:, k, :]), reads=['xr%d' % k], writes=['xb'])
        ta, tb, tc = tt['ta'], tt['tb'], tt['tc']
        wg8, wg8k = load_wg(8)
        for c in range(2):
            rs = slice(c * 128, (c + 1) * 128)
            P.dma('sync', lambda e, rs=rs, tsl=tsl: e.dma_start(out=ta[:], in_=hA[0, rs, tsl]), 'ta', writes=['ta'])
            P.dma('sync', lambda e, rs=rs, tsl=tsl: e.dma_start(out=tb[:], in_=hA[1, rs, tsl]), 'tb', writes=['tb'])
            P.op('vector', lambda e: e.tensor_tensor(out=ta[:], in0=ta[:], in1=tb[:], op=ALU.add), reads=['ta', 'tb'], writes=['ta'])
            ps, pk = nb()
            P.mm(ps[:, 0:512], BONES, ta[:], True, True, reads=['cs', 'ta'], writes=[pk])
            P.op('vector', lambda e, ps=ps: e.scalar_tensor_tensor(out=tb[:], in0=ps[:, 0:512], scalar=-1.0 / 64, in1=ta[:], op0=ALU.mult, op1=ALU.add),
                 reads=[pk, 'ta'], writes=['tb'])
            P.op('gpsimd', lambda e: e.tensor_tensor(out=tc[:], in0=tb[:], in1=tb[:], op=ALU.mult), reads=['tb'], writes=['tc'])
            ps2, pk2 = nb()
            P.mm(ps2[:, 0:512], BONES, tc[:], True, True, reads=['cs', 'tc'], writes=[pk2])
            rsqrt_from_psum(ps2, pk2, 1.0 / 64, eps5, 'eps5', tc, 'tc')
            P.op('gpsimd', lambda e: e.tensor_tensor(out=tb[:], in0=tb[:], in1=tc[:], op=ALU.mult), reads=['tb', 'tc'], writes=['tb'])
            psg, pkg = proj(wg8, wg8k, c * 128, xb, 'xb')
            i = rot('td', 2)
            td = tt['td%d' % i]
            P.op('scalar', lambda e, td=td, psg=psg: e.activation(out=td[:], in_=psg[:, 0:512], func=AF.Sigmoid), reads=[pkg], writes=['td%d' % i])
            P.op('vector', lambda e, td=td, c=c: e.scalar_tensor_tensor(out=yb[:, c, :], in0=tb[:], scalar=pc[:, PC_ANW + c:PC_ANW + c + 1], in1=td[:],
                                                                        op0=ALU.mult, op1=ALU.mult), reads=['tb', 'pc', 'td%d' % i], writes=['yb'])
        for c in range(2):
            rs = slice(c * 128, (c + 1) * 128)
            P.dma('sync', lambda e, rs=rs, tsl=tsl: e.dma_start(out=ta[:], in_=oCd[0, rs, tsl]), 'ta', writes=['ta'])
            P.dma('sync', lambda e, rs=rs, tsl=tsl: e.dma_start(out=tb[:], in_=oCd[1, rs, tsl]), 'tb', writes=['tb'])
            P.op('vector', lambda e: e.tensor_tensor(out=ta[:], in0=ta[:], in1=tb[:], op=ALU.add), reads=['ta', 'tb'], writes=['ta'])
            P.op('gpsimd', lambda e: e.tensor_tensor(out=tc[:], in0=ta[:], in1=ta[:], op=ALU.mult), reads=['ta'], writes=['tc'])
            ps2, pk2 = nb()
            P.mm(ps2[:, 0:512], BONES, tc[:], True, True, reads=['cs', 'tc'], writes=[pk2])
            rsqrt_from_psum(ps2, pk2, 1.0 / 64, eps6, 'eps6', tc, 'tc')
            P.op('gpsimd', lambda e: e.tensor_tensor(out=tb[:], in0=ta[:], in1=tc[:], op=ALU.mult), reads=['ta', 'tc', 'tb'], writes=['tb'])
            psg, pkg = proj(wg8, wg8k, 256 + c * 128, xb, 'xb')
            i = rot('td', 2)
            td = tt['td%d' % i]
            P.op('scalar', lambda e, td=td, psg=psg: e.activation(out=td[:], in_=psg[:, 0:512], func=AF.Silu), reads=[pkg], writes=['td%d' % i])
            P.op('vector', lambda e, td=td, c=c: e.scalar_tensor_tensor(out=yb[:, 4 + c, :], in0=tb[:], scalar=pc[:, PC_CNW:PC_CNW + 1], in1=td[:],
                                                                        op0=ALU.mult, op1=ALU.mult), reads=['tb', 'pc', 'td%d' % i], writes=['yb'])
        for gi in range(3):
            P.dma('sync', lambda e, gi=gi, tsl=tsl: e.dma_start(out=l4[gi][:], in_=lB[gi, :, tsl]), 'l4_%d' % gi, writes=['l4_%d' % gi])
        P.op('vector', lambda e: e.tensor_tensor(out=l4[0][:], in0=l4[0][:], in1=l4[1][:], op=ALU.add), reads=['l4_0', 'l4_1'], writes=['l4_0'])
        P.op('vector', lambda e: e.tensor_tensor(out=l4[0][:], in0=l4[0][:], in1=l4[2][:], op=ALU.add), reads=['l4_0', 'l4_2'], writes=['l4_0'])
        P.op('vector', lambda e: e.reciprocal(out=l4[0][:], in_=l4[0][:]), reads=['l4_0'], writes=['l4_0'])
        for c in range(2):
            rs = slice(c * 128, (c + 1) * 128)
            P.dma('sync', lambda e, rs=rs, tsl=tsl: e.dma_start(out=ta[:], in_=accB[0, rs, tsl]), 'ta', writes=['ta'])
            P.dma('sync', lambda e, rs=rs, tsl=tsl: e.dma_start(out=tb[:], in_=accB[1, rs, tsl]), 'tb', writes=['tb'])
            P.dma('sync', lambda e, rs=rs, tsl=tsl: e.dma_start(out=tc[:], in_=accB[2, rs, tsl]), 'tc', writes=['tc'])
            P.op('vector', lambda e: e.tensor_tensor(out=ta[:], in0=ta[:], in1=tb[:], op=ALU.add), reads=['ta', 'tb'], writes=['ta'])
            P.op('gpsimd', lambda e: e.tensor_tensor(out=ta[:], in0=ta[:], in1=tc[:], op=ALU.add), reads=['ta', 'tc'], writes=['ta'])
            ps, pk = nb()
            P.mm(ps[:, 0:512], cs[0:4, CS_E4 + c * 128:CS_E4 + (c + 1) * 128], l4[0][:], True, True, reads=['cs', 'l4_0'], writes=[pk])
            P.op('vector', lambda e, ps=ps, c=c: e.tensor_tensor(out=yb[:, 2 + c, :], in0=ta[:], in1=ps[:, 0:512], op=ALU.mult), reads=['ta', pk], writes=['yb'])
        P.dma('sync', lambda e, tsl=tsl: e.dma_start(out=l8[:], in_=lD[:, tsl]), 'l8', writes=['l8'])
        P.op('vector', lambda e: e.tensor_scalar(out=l8[:], in0=l8[:], scalar1=esink[:, 0:1], scalar2=None, op0=ALU.add), reads=['l8', 'esink'], writes=['l8'])
        P.op('vector', lambda e: e.reciprocal(out=l8[:], in_=l8[:]), reads=['l8'], writes=['l8'])
        for c in range(4):
            rs = slice(c * 128, (c + 1) * 128)
            P.dma('sync', lambda e, rs=rs, tsl=tsl: e.dma_start(out=ta[:], in_=accD[rs, tsl]), 'ta', writes=['ta'])
            ps, pk = nb()
            P.mm(ps[:, 0:512], cs[0:8, CS_E8 + c * 128:CS_E8 + (c + 1) * 128], l8[:], True, True, reads=['cs', 'l8'], writes=[pk])
            P.op('vector', lambda e, ps=ps, c=c: e.tensor_tensor(out=yb[:, 6 + c, :], in0=ta[:], in1=ps[:, 0:512], op=ALU.mult), reads=['ta', pk], writes=['yb'])
        BR = ((0, 1), (2, 3), (4, 5), (6, 7, 8, 9))
        for m in range(8):
            wgm, wgk = load_wg(m)
            ai = rot('tacc', 2)
            tacc = tt['tacc%d' % ai]
            for b in range(4):
                psg, pkg = proj(wgm, wgk, b * 128, xb, 'xb')
                i = rot('td', 2)
                td = tt['td%d' % i]
                P.op('scalar', lambda e, td=td, psg=psg: e.activation(out=td[:], in_=psg[:, 0:512], func=AF.Sigmoid), reads=[pkg], writes=['td%d' % i])
                psb, pkb = nb()
                for kk, kc in enumerate(BR[b]):
                    P.mm(psb[:, 0:512], wbr_t[:, kc, m * 128:(m + 1) * 128], yb[:, kc, :], kk == 0, kk == len(BR[b]) - 1, reads=['wbr', 'yb'], writes=[pkb])
                if b == 0:
                    P.op('vector', lambda e, td=td, psb=psb, tacc=tacc: e.tensor_tensor(out=tacc[:], in0=td[:], in1=psb[:, 0:512], op=ALU.mult),
                         reads=['td%d' % i, pkb], writes=['tacc%d' % ai])
                else:
                    j = rot('tm', 2)
                    tm = tt['tm%d' % j]
                    P.op('vector', lambda e, td=td, psb=psb, tm=tm: e.tensor_tensor(out=tm[:], in0=td[:], in1=psb[:, 0:512], op=ALU.mult),
                         reads=['td%d' % i, pkb], writes=['tm%d' % j])
                    if b < 3:
                        P.op('gpsimd', lambda e, tm=tm, tacc=tacc: e.tensor_tensor(out=tacc[:], in0=tacc[:], in1=tm[:], op=ALU.add),
                             reads=['tacc%d' % ai, 'tm%d' % j], writes=['tacc%d' % ai])
                    else:
                        P.op('gpsimd', lambda e, tm=tm, tacc=tacc, m=m: e.tensor_tensor(out=mg[:, m, :], in0=tacc[:], in1=tm[:], op=ALU.add),
                             reads=['tacc%d' % ai, 'tm%d' % j], writes=['mg'])
        for m in range(8):
            ps, pk = nb()
            for k in range(8):
                P.mm(ps[:, 0:512], wout_t[:, k, m * 128:(m + 1) * 128], mg[:, k, :], k == 0, k == 7, reads=['wout', 'mg'], writes=[pk])
            P.op('vector', lambda e, ps=ps, m=m: e.scalar_tensor_tensor(out=xr[:, m, :], in0=xr[:, m, :], scalar=ALPHA, in1=ps[:, 0:512], op0=ALU.mult, op1=ALU.add),
                 reads=['xr%d' % m, pk], writes=['xr%d' % m])
        layer_norm(PC_LN1W, PC_LN1B, True)
        if moe:
            P.strict = True
            for sbt in range(4):
                ps, pk = nb()
                for k in range(8):
                    P.mm(ps[:, 0:8], xr[:, k, sbt * 128:(sbt + 1) * 128], rt[:, k, :], k == 0, k == 7, reads=['xr%d' % k, 'rt'], writes=[pk])
                P.op('vector', lambda e, ps=ps, sbt=sbt: e.tensor_copy(out=lg[:, sbt, :], in_=ps[:, 0:8]), reads=[pk], writes=['lg'])
                P.op('vector', lambda e, sbt=sbt: e.tensor_reduce(out=m1[:, 0:1], in_=lg[:, sbt, :], axis=AX.X, op=ALU.max), reads=['lg'], writes=['m1'])
                P.op('vector', lambda e, sbt=sbt: e.tensor_scalar(out=gsel[:], in0=lg[:, sbt, :], scalar1=m1[:, 0:1], scalar2=None, op0=ALU.is_equal),
                     reads=['lg', 'm1'], writes=['gsel'])
                P.op('vector', lambda e, sbt=sbt: e.scalar_tensor_tensor(out=lg2[:], in0=gsel[:], scalar=-1.0e30, in1=lg[:, sbt, :], op0=ALU.mult, op1=ALU.add),
                     reads=['gsel', 'lg'], writes=['lg2'])
                P.op('vector', lambda e: e.tensor_reduce(out=m1[:, 1:2], in_=lg2[:], axis=AX.X, op=ALU.max), reads=['lg2'], writes=['m1'])
                P.op('vector', lambda e, sbt=sbt: e.tensor_scalar(out=gsel[:], in0=lg[:, sbt, :], scalar1=m1[:, 1:2], scalar2=None, op0=ALU.is_ge),
                     reads=['lg', 'm1'], writes=['gsel'])
                P.op('vector', lambda e: e.tensor_scalar(out=m1[:, 2:3], in0=m1[:, 0:1], scalar1=-1.0, scalar2=None, op0=ALU.mult), reads=['m1'], writes=['m1'])
                P.op('scalar', lambda e, sbt=sbt: e.activation(out=lg2[:], in_=lg[:, sbt, :], func=AF.Exp, bias=m1[:, 2:3], scale=1.0), reads=['lg', 'm1'], writes=['lg2'])
                P.op('vector', lambda e: e.tensor_tensor(out=lg2[:], in0=lg2[:], in1=gsel[:], op=ALU.mult), reads=['lg2', 'gsel'], writes=['lg2'])
                P.op('vector', lambda e: e.tensor_reduce(out=m1[:, 3:4], in_=lg2[:], axis=AX.X, op=ALU.add), reads=['lg2'], writes=['m1'])
                P.op('vector', lambda e: e.reciprocal(out=m1[:, 3:4], in_=m1[:, 3:4]), reads=['m1'], writes=['m1'])
                P.op('vector', lambda e, sbt=sbt: e.tensor_scalar(out=gts[:, sbt, :], in0=lg2[:], scalar1=m1[:, 3:4], scalar2=None, op0=ALU.mult),
                     reads=['lg2', 'm1'], writes=['gts'])
                ps2, pk2 = nb()
                P.mm(ps2[0:8, 0:128], gts[:, sbt, :], IDN, True, True, reads=['gts', 'cs'], writes=[pk2])
                P.op('vector', lambda e, ps2=ps2, sbt=sbt: e.tensor_copy(out=gT[:, sbt * 128:(sbt + 1) * 128], in_=ps2[0:8, 0:128]), reads=[pk2], writes=['gT'])
            P.strict = False
            if dbg:
                P.dma('gpsimd', lambda e, t0=t0: e.dma_start(out=dbg_g[:, t0:t0 + 512], in_=gT[:]), 'gT', reads=['gT'], final=True)
                P.dma('gpsimd', lambda e, ps_i=ps_i: e.dma_start(out=dbg_l[ps_i, :, 0:32], in_=lg[:].rearrange("p a b -> p (a b)")), 'lg', reads=['lg'], final=True)
                P.dma('gpsimd', lambda e, ps_i=ps_i: e.dma_start(out=dbg_l[ps_i, :, 32:64], in_=gts[:].rearrange("p a b -> p (a b)")), 'gts', reads=['gts'], final=True)
                for k in range(8):
                    P.dma('gpsimd', lambda e, t0=t0, k=k: e.dma_start(out=dbg_x[k * 128:(k + 1) * 128, t0:t0 + 512], in_=xr[:, k, :]), 'xr%d' % k, reads=['xr%d' % k], final=True)
            for ex in range(8):
                ps, pk = nb()
                P.mm(ps[:, 0:512], cs[0:8, CS_EE + ex * 128:CS_EE + (ex + 1) * 128], gT[:], True, True, reads=['cs', 'gT'], writes=[pk])
                P.op('scalar', lambda e, ps=ps, ex=ex: e.copy(out=gb[:, ex, :], in_=ps[:, 0:512]), reads=[pk], writes=['gb%d' % ex])
        for ex in range(E):
            hoff = (ex % 2) * 11 if moe else 0
            hkey = 'hT%d' % (ex % 2)
            for jc in range(J):
                i1 = rot('w1', 4)
                P.dma('sync', lambda e, i1=i1, ex=ex, jc=jc: e.dma_start(out=w1r[i1][:].rearrange("p k c -> p (k c)"), in_=w1d[ex, jc]), 'w1r%d' % i1, writes=['w1r%d' % i1])
                i3 = rot('w3', 4)
                P.dma('sync', lambda e, i3=i3, ex=ex, jc=jc: e.dma_start(out=w3r[i3][:].rearrange("p k c -> p (k c)"), in_=w3d[ex, jc]), 'w3r%d' % i3, writes=['w3r%d' % i3])
                p1, k1 = proj(w1r[i1], 'w1r%d' % i1, 0, xb, 'xb')
                p3, k3 = proj(w3r[i3], 'w3r%d' % i3, 0, xb, 'xb')
                si = rot('ts', 2)
                ts_ = tt['td%d' % si]
                P.op('scalar', lambda e, ts_=ts_, p1=p1: e.activation(out=ts_[:], in_=p1[:, 0:512], func=AF.Silu), reads=[k1], writes=['td%d' % si])
                if moe:
                    P.op('vector', lambda e, ts_=ts_, p3=p3: e.tensor_tensor(out=ts_[:], in0=ts_[:], in1=p3[:, 0:512], op=ALU.mult), reads=['td%d' % si, k3], writes=['td%d' % si])
                    P.op('gpsimd', lambda e, ts_=ts_, ex=ex, jc=jc, hoff=hoff: e.tensor_tensor(out=hT[:, hoff + jc, :], in0=ts_[:], in1=gb[:, ex, :], op=ALU.mult),
                         reads=['td%d' % si, 'gb%d' % ex], writes=[hkey])
                else:
                    P.op('vector', lambda e, ts_=ts_, p3=p3, jc=jc: e.tensor_tensor(out=hT[:, jc, :], in0=ts_[:], in1=p3[:, 0:512], op=ALU.mult),
                         reads=['td%d' % si, k3], writes=[hkey])
            for m in range(8):
                i2 = rot('w2', 3)
                P.dma('sync', lambda e, i2=i2, ex=ex, m=m: e.dma_start(out=w2r[i2][:].rearrange("p k c -> p (k c)"), in_=w2d[ex, m]), 'w2r%d' % i2, writes=['w2r%d' % i2])
                ps, pk = nb()
                for jc in range(J):
                    P.mm(ps[:, 0:512], w2r[i2][:, jc, :], hT[:, hoff + jc, :], jc == 0, jc == J - 1, reads=['w2r%d' % i2, hkey], writes=[pk])
                if not moe:
                    P.op('vector', lambda e, ps=ps, m=m: e.scalar_tensor_tensor(out=xr[:, m, :], in0=xr[:, m, :], scalar=ALPHA, in1=ps[:, 0:512], op0=ALU.mult, op1=ALU.add),
                         reads=['xr%d' % m, pk], writes=['xr%d' % m])
                elif ex == 0:
                    P.op('vector', lambda e, ps=ps, m=m: e.scalar_tensor_tensor(out=xr[:, m, :], in0=xr[:, m, :], scalar=ALPHA, in1=ps[:, 0:512], op0=ALU.mult, op1=ALU.add),
                         reads=['xr%d' % m, pk], writes=['xr%d' % m])
                else:
                    P.op('vector', lambda e, ps=ps, m=m: e.tensor_tensor(out=xr[:, m, :], in0=xr[:, m, :], in1=ps[:, 0:512], op=ALU.add),
                         reads=[pk, 'xr%d' % m], writes=['xr%d' % m])
        layer_norm(PC_LN2W, PC_LN2B, False, out_dram_t0=t0)
    P.emit()
    return nc


def l3_consts():
    c = np.zeros((128, CS_N), np.float32)
    i = np.arange(128)
    c[:, CS_BONES:CS_BONES + 128] = (i[:, None] // 64 == i[None, :] // 64)
    c[:, CS_ONES:CS_ONES + 128] = 1.0
    c[:, CS_IDENT:CS_IDENT + 128] = np.eye(128)
    for ch in range(2):
        for h in range(4):
            c[h, CS_E4 + ch * 128:CS_E4 + (ch + 1) * 128] = ((ch * 2 + i // 64) == h)
    for ch in range(4):
        for h in range(8):
            c[h, CS_E8 + ch * 128:CS_E8 + (ch + 1) * 128] = ((ch * 2 + i // 64) == h)
    for ex in range(8):
        c[ex, CS_EE + ex * 128:CS_EE + (ex + 1) * 128] = 1.0
    return c


def l3_gate_columns():
    cols = []
    for m in range(8):
        for b in range(4):
            cols.append(np.arange(b * 1024 + m * 128, b * 1024 + (m + 1) * 128))
    cols.append(_cols('a_o'))
    cols.append(_cols('c_gate'))
    return np.concatenate(cols)


def _pack_kxc(w, cw):
    K_, C_ = w.shape
    g = C_ // cw
    return np.ascontiguousarray(w.reshape(K_ // 128, 128, g, cw).transpose(2, 1, 0, 3).reshape(g, 128, (K_ // 128) * cw))


def _pack_w2(w):
    J_ = w.shape[0] // 128
    return np.ascontiguousarray(w.reshape(J_, 128, 8, 128).transpose(2, 1, 0, 3).reshape(8, 128, J_ * 128))


def l3_pcol(inp, l):
    p = np.zeros((128, 40), np.float32)
    p[:, PC_ANW:PC_ANW + 2] = inp['a_norm_w'][l].reshape(2, 128).T
    p[:, PC_CNW] = np.tile(inp['c_norm_w'][l], 2)
    p[0:8, PC_SINK] = inp['d_sink'][l]
    p[:, PC_LN1W:PC_LN1W + 8] = inp['ln1_w'][l].reshape(8, 128).T
    p[:, PC_LN1B:PC_LN1B + 8] = inp['ln1_b'][l].reshape(8, 128).T
    p[:, PC_LN2W:PC_LN2W + 8] = inp['ln2_w'][l].reshape(8, 128).T
    p[:, PC_LN2B:PC_LN2B + 8] = inp['ln2_b'][l].reshape(8, 128).T
    return p


def l3_weight_maps(wb, inp, l):
    d = {}
    d['wg'] = _pack_kxc(wb['wgsel'][l], 512)
    wbr = np.concatenate([wb['w_branch_a'][l], wb['w_branch_b'][l], wb['w_branch_c'][l], wb['w_branch_d'][l]], 0)
    d['wbr'] = np.ascontiguousarray(wbr.reshape(10, 128, 1024).transpose(1, 0, 2).reshape(128, 10 * 1024))
    d['wout'] = np.ascontiguousarray(wb['w_out'][l].reshape(8, 128, 1024).transpose(1, 0, 2).reshape(128, 8 * 1024))
    d['pcol'] = l3_pcol(inp, l)
    j = l // 2
    if l % 2 == 0:
        d['w1p'] = _pack_kxc(wb['ffn_w1'][j], 128)[None]
        d['w3p'] = _pack_kxc(wb['ffn_w3'][j], 128)[None]
        d['w2p'] = _pack_w2(wb['ffn_w2'][j])[None]
    else:
        d['w1p'] = np.stack([_pack_kxc(wb['moe_w1'][j, e], 128) for e in range(8)])
        d['w3p'] = np.stack([_pack_kxc(wb['moe_w3'][j, e], 128) for e in range(8)])
        d['w2p'] = np.stack([_pack_w2(wb['moe_w2'][j, e]) for e in range(8)])
        d['router'] = np.ascontiguousarray(inp['moe_router'][j].reshape(8, 128, 8).transpose(1, 0, 2).reshape(128, 64))
    return d


def l3_host_inputs(xT_full, hM, oD, accB, accD, wmap, cst3):
    accBT = np.ascontiguousarray(accB[:, :, :, :64].reshape(3, SEQ, 256).transpose(0, 2, 1))
    lBT = np.ascontiguousarray(accB[:, :, :, 64].transpose(0, 2, 1))
    accDT = np.ascontiguousarray(accD[:, :, :64].reshape(SEQ, 512).T)
    lDT = np.ascontiguousarray(accD[:, :, 64].T)
    maps = []
    for c in range(NCORES):
        sl = slice(c * TC, (c + 1) * TC)
        m = dict(xT=np.ascontiguousarray(xT_full[:, sl]), hA=np.ascontiguousarray(hM[:, :, sl]), oCd=np.ascontiguousarray(oD[:, :, sl]),
                 accB=np.ascontiguousarray(accBT[:, :, sl]), lB=np.ascontiguousarray(lBT[:, :, sl]),
                 accD=np.ascontiguousarray(accDT[:, sl]), lD=np.ascontiguousarray(lDT[:, sl]), cst3=cst3)
        m.update(wmap)
        maps.append(m)
    return maps


_NC_CACHE = {}


def _get_nc(name):
    if name not in _NC_CACHE:
        _NC_CACHE[name] = {'l0': build_l0, 'l1': build_l1, 'l2a': build_l2a, 'l2m': build_l2m, 'l2d': build_l2d,
                           'l3d': lambda: build_l3(False), 'l3m': lambda: build_l3(True)}[name]()
    return _NC_CACHE[name]


_PROFILE = []


def _launch(nc, maps):
    if _PROFILE:
        r = run_bass_kernel_spmd(nc, maps, core_ids=list(range(NCORES)), trace=True)
        _PROFILE.append(r.exec_time_ns)
        return r.results
    return run_bass_kernel_spmd(nc, maps, core_ids=list(range(NCORES))).results


def run_layer(xT_full, l, inp, consts, wb, cst3):
    r1 = _launch(_get_nc('l1'), l1_host_inputs(xT_full, l, inp, consts))
    oA = np.concatenate([r['oA'] for r in r1], axis=1)
    oB = np.concatenate([r['oB'] for r in r1], axis=1)
    oC = np.concatenate([r['oC'] for r in r1], axis=1)
    oG = np.concatenate([r['oG'] for r in r1], axis=1)
    ra = _launch(_get_nc('l2a'), l2a_host_inputs(oA, oB, inp, l))
    accB, accD = l2a_host_outputs(ra)
    hM = l2m_host_outputs(_launch(_get_nc('l2m'), l2m_host_inputs(oA, oG)))
    oD = l2d_host_outputs(_launch(_get_nc('l2d'), l2d_host_inputs(oC, oG)))
    wmap = l3_weight_maps(wb, inp, l)
    r3 = _launch(_get_nc('l3m' if l % 2 else 'l3d'), l3_host_inputs(xT_full, hM, oD, accB, accD, wmap, cst3))
    return np.concatenate([r['xo'] for r in r3], axis=1)


def cast_all_weights(inp):
    gc = l3_gate_columns()
    arrs = {'wgsel': np.ascontiguousarray(inp['w_in'][:, :, gc])}
    for n in ('w_branch_a', 'w_branch_b', 'w_branch_c', 'w_branch_d', 'w_out', 'ffn_w1', 'ffn_w3', 'ffn_w2', 'moe_w1', 'moe_w3', 'moe_w2'):
        arrs[n] = inp[n]
    return cast_weights_on_device(arrs, _get_nc('l0'), _launch)


def kernel(**inputs):
    inp = {k: np.asarray(v) for k, v in inputs.items()}
    consts = make_consts(inp)
    cst3 = l3_consts()
    wb = cast_all_weights(inp)
    xT = np.ascontiguousarray(inp['x'][0].T)
    for l in range(DEPTH):
        xT = run_layer(xT, l, inp, consts, wb, cst3)
    return np.ascontiguousarray(xT.T)[None].astype(np.float32)
```

```python
import contextlib
import numpy as np
import ml_dtypes
import concourse.bass as bass
import concourse.mybir as mybir
from concourse.bass_utils import run_bass_kernel_spmd

F32 = mybir.dt.float32
BF16 = mybir.dt.bfloat16
AF = mybir.ActivationFunctionType
ALU = mybir.AluOpType
AX = mybir.AxisListType
NPBF = ml_dtypes.bfloat16

NCORES = 8
SEQ = 16384
D = 1024
TC = SEQ // NCORES
DEPTH = 4
ALPHA = (2 * DEPTH) ** 0.25
LN_EPS = 1e-5


class Prog:
    ENGS = ('tensor', 'vector', 'scalar', 'gpsimd', 'sync')

    def __init__(self, nc):
        self.nc = nc
        self.stack = contextlib.ExitStack()
        self.recs = {e: [] for e in self.ENGS}
        self.cnt = {}
        self.lastw = {}
        self.rd = {}
        self.seen = {e: {} for e in self.ENGS}
        self.final = []
        self.nsb = 0
        self.pskeys = set()
        self.strict = False

    def sb(self, name, shape, dt):
        return self.stack.enter_context(self.nc.sbuf_tensor("sb_" + name, list(shape), dt))

    def ps(self, name, shape=None, dt=F32):
        self.pskeys.add(name)
        return self.stack.enter_context(self.nc.psum_tensor("pp_" + name, [128, 512], F32))

    def _deps(self, eng, reads, writes):
        deps = {}

        def add(tok):
            if tok is None:
                return
            k, v = tok
            if k == eng and eng == 'tensor':
                return
            if deps.get(k, 0) < v:
                deps[k] = v

        for key in reads:
            add(self.lastw.get(key))
            if key in self.pskeys:
                for t in self.rd.get(key, {}).items():
                    add(t)
        for key in writes:
            add(self.lastw.get(key))
            for t in self.rd.get(key, {}).items():
                add(t)
        out = []
        for k, v in deps.items():
            if self.seen[eng].get(k, 0) < v:
                self.seen[eng][k] = v
                out.append((k, v))
        return out

    def _commit(self, tok, reads, writes):
        for key in reads:
            d = self.rd.setdefault(key, {})
            if d.get(tok[0], 0) < tok[1]:
                d[tok[0]] = tok[1]
        for key in writes:
            self.lastw[key] = tok
            self.rd[key] = {}

    def op(self, eng, fn, reads=(), writes=()):
        waits = self._deps(eng, reads, writes)
        self.cnt[eng] = self.cnt.get(eng, 0) + 1
        tok = (eng, self.cnt[eng])
        self.recs[eng].append(dict(fn=fn, waits=waits, tok=tok, dma=False))
        self._commit(tok, reads, writes)
        return tok

    def dma(self, eng, fn, sem, reads=(), writes=(), final=False):
        waits = self._deps(eng, reads, writes)
        sk = 'dma:' + sem
        self.cnt[sk] = self.cnt.get(sk, 0) + 16
        tok = (sk, self.cnt[sk])
        self.recs[eng].append(dict(fn=fn, waits=waits, tok=tok, dma=True))
        self._commit(tok, reads, writes)
        if final:
            self.final.append((eng, tok))
        return tok

    def mm(self, out, lhsT, rhs, start, stop, reads, writes):
        return self.op('tensor', lambda e: e.matmul(out, lhsT=lhsT, rhs=rhs, start=start, stop=stop),
                       reads=reads, writes=writes)

    def emit(self):
        nc = self.nc
        fin = {}
        for eng, tok in self.final:
            d = fin.setdefault(eng, {})
            if d.get(tok[0], 0) < tok[1]:
                d[tok[0]] = tok[1]
        for eng, d in fin.items():
            self.recs[eng].append(dict(fn=None, waits=list(d.items()), tok=None, dma=False))
        waited = {}
        for e in self.ENGS:
            for r in self.recs[e]:
                for (k, v) in r['waits']:
                    waited.setdefault(k, set()).add(v)
        rank = {}
        for k, vals in waited.items():
            if k in self.ENGS:
                rank[k] = {v: i + 1 for i, v in enumerate(sorted(vals))}
        semkeys = sorted(set(list(waited.keys()) + [k for k in self.cnt if k.startswith('dma:')]))
        sems = {}
        for i, k in enumerate(semkeys):
            sems[k] = self.stack.enter_context(nc.semaphore("s%d_%s" % (i, k.replace(':', '_'))))
        recs = self.recs
        with nc.Block() as block:
            for e in self.ENGS:
                if not recs[e]:
                    continue

                def body(engobj, e=e):
                    wset = waited.get(e, ())
                    for r in recs[e]:
                        for (k, v) in r['waits']:
                            val = rank[k][v] if k in rank else v
                            engobj.wait_ge(sems[k], val)
                        if r['fn'] is None:
                            continue
                        ins = r['fn'](engobj)
                        if r['dma']:
                            ins.then_inc(sems[r['tok'][0]], 16)
                        elif r['tok'][1] in wset:
                            ins.then_inc(sems[e], 1)

                getattr(block, e)(body)
        self.stack.close()


def bcast_last(ap2d, n):
    pat = [list(p) for p in ap2d.ap]
    return bass.AP(ap2d.tensor, ap2d.offset, pat + [[0, n]])


def _new_nc():
    return bass.Bass("TRN2", target_bir_lowering=False)


W_IN_OFF = {}
_off = 0
for _n, _s in (('gates', 4096), ('a_q', 256), ('a_k', 256), ('a_v', 256), ('a_o', 256), ('a_if', 16),
               ('b_q', 768), ('b_k', 768), ('b_v', 768), ('c_qkv', 768), ('c_gate', 256), ('c_beta', 8),
               ('c_decay', 8), ('d_q', 512), ('d_k', 128), ('d_v', 128)):
    W_IN_OFF[_n] = (_off, _s)
    _off += _s
N_IN = _off


def _cols(name):
    o, s = W_IN_OFF[name]
    return np.arange(o, o + s)


def l1_columns():
    plain = np.concatenate([_cols('a_q'), _cols('a_k'), _cols('a_v'), _cols('b_v'), _cols('d_v'),
                            -np.ones(128, np.int64)])
    rope = np.concatenate([_cols('b_q'), _cols('b_k'), _cols('d_q'), _cols('d_k')])
    r = np.arange(rope.size)
    swap = rope[(r // 64) * 64 + ((r % 64) + 32) % 64]
    inter = np.stack([rope.reshape(-1, 128), swap.reshape(-1, 128)], axis=1).reshape(-1)
    conv = _cols('c_qkv')
    gate = -np.ones(128, np.int64)
    oi, _ = W_IN_OFF['a_if']
    for d in range(2):
        for h in range(4):
            gate[0 + d * 4 + h] = oi + d * 8 + h
            gate[32 + d * 4 + h] = oi + d * 8 + 4 + h
            gate[64 + d * 4 + h] = W_IN_OFF['c_beta'][0] + d * 4 + h
            gate[96 + d * 4 + h] = W_IN_OFF['c_decay'][0] + d * 4 + h
    return np.concatenate([plain, inter, conv, gate])


L1_NPLAIN, L1_NROPE, L1_NCONV = 1664, 2176, 768
L1_NC = (L1_NPLAIN + 128) + 2 * L1_NROPE + L1_NCONV + 128
L1_WG = 512


def build_l1():
    nc = _new_nc()
    T = TC
    NT = T // 512
    xT = nc.dram_tensor("xT", [D, T], F32, kind="ExternalInput").ap()
    xhT = nc.dram_tensor("xhT", [D, 4], F32, kind="ExternalInput").ap()
    w1 = nc.dram_tensor("w1", [D, L1_NC], F32, kind="ExternalInput").ap()
    cosd = nc.dram_tensor("cos", [128, T], F32, kind="ExternalInput").ap()
    sind = nc.dram_tensor("sin", [128, T], F32, kind="ExternalInput").ap()
    convw = nc.dram_tensor("convw", [768, 5], F32, kind="ExternalInput").ap()
    gpar = nc.dram_tensor("gpar", [128, 2], F32, kind="ExternalInput").ap()
    bones = nc.dram_tensor("bones", [128, 128], F32, kind="ExternalInput").ap()
    oA = nc.dram_tensor("oA", [L1_NPLAIN, T], BF16, kind="ExternalOutput").ap()
    oB = nc.dram_tensor("oB", [L1_NROPE, T], BF16, kind="ExternalOutput").ap()
    oC = nc.dram_tensor("oC", [L1_NCONV, T], BF16, kind="ExternalOutput").ap()
    oG = nc.dram_tensor("oG", [128, T], F32, kind="ExternalOutput").ap()

    P = Prog(nc)
    xf = [P.sb("xf%d" % i, [128, T], F32) for i in range(2)]
    xb = P.sb("xb", [128, 8, T], BF16)
    xhf = P.sb("xhf", [128, 8, 4], F32)
    xhb = P.sb("xhb", [128, 8, 4], BF16)
    cos_t = P.sb("cos_t", [128, T], F32)
    sin_t = P.sb("sin_t", [128, T], F32)
    cw = P.sb("cw", [128, 6, 5], F32)
    gp = P.sb("gp", [128, 2], F32)
    gneg = P.sb("gneg", [128, 2], F32)
    bo = P.sb("bo", [128, 128], F32)
    wst = [P.sb("wst%d" % i, [128, 8, L1_WG], F32) for i in range(2)]
    wbf = [P.sb("wbf%d" % i, [128, 8, L1_WG], BF16) for i in range(2)]
    ot = [P.sb("ot%d" % i, [128, 512], BF16) for i in range(3)]
    otg = P.sb("otg", [128, T], F32)
    r1 = [P.sb("r1_%d" % i, [128, 512], F32) for i in range(2)]
    r2 = [P.sb("r2_%d" % i, [128, 512], F32) for i in range(2)]
    cbuf = P.sb("cbuf", [128, T + 4], F32)
    cacc = P.sb("cacc", [128, T], F32)
    csq = P.sb("csq", [128, T], F32)
    crs = P.sb("crs", [128, 512], F32)
    cout = P.sb("cout", [128, T], BF16)
    gt1 = P.sb("gt1", [128, 512], F32)
    eps6 = P.sb("eps6", [128, 1], F32)
    P.op('vector', lambda e: e.memset(eps6[:], 1e-6), writes=['eps6'])
    pss = [P.ps("ps%d" % i, [128, 512]) for i in range(6)]
    psh = P.ps("psh", [128, 4])
    psn = P.ps("psn", [128, 512])

    P.dma('sync', lambda e: e.dma_start(out=xhf[:], in_=xhT.rearrange("(k p) t -> p k t", p=128)), 'xhf', writes=['xhf'])
    P.dma('sync', lambda e: e.dma_start(out=cos_t[:], in_=cosd), 'cos', writes=['cos'])
    P.dma('sync', lambda e: e.dma_start(out=sin_t[:], in_=sind), 'sin', writes=['sin'])
    P.dma('sync', lambda e: e.dma_start(out=cw[:], in_=convw.rearrange("(c p) j -> p c j", p=128)), 'cw', writes=['cw'])
    P.dma('sync', lambda e: e.dma_start(out=gp[:], in_=gpar), 'gp', writes=['gp'])
    P.dma('sync', lambda e: e.dma_start(out=bo[:], in_=bones), 'bo', writes=['bo'])
    xbk = ['xb%d' % k for k in range(8)]
    for k in range(8):
        eng = 'vector' if k % 2 == 0 else 'gpsimd'
        P.dma('sync', lambda e, k=k: e.dma_start(out=xf[k % 2][:], in_=xT[k * 128:(k + 1) * 128, :]), 'xf%d' % (k % 2), writes=['xf%d' % (k % 2)])
        P.op(eng, lambda e, k=k: e.tensor_copy(out=xb[:, k, :], in_=xf[k % 2][:]), reads=['xf%d' % (k % 2)], writes=[xbk[k]])
    P.op('vector', lambda e: e.tensor_copy(out=xhb[:], in_=xhf[:]), reads=['xhf'], writes=['xhb'])
    P.op('vector', lambda e: e.tensor_scalar(out=gneg[:, 0:1], in0=gp[:, 0:1], scalar1=-1.0, scalar2=None, op0=ALU.mult),
         reads=['gp'], writes=['gneg'])
    P.op('scalar', lambda e: e.activation(out=gneg[:, 1:2], in_=gp[:, 1:2], func=AF.Exp), reads=['gp', 'gneg'], writes=['gneg'])
    P.op('vector', lambda e: e.tensor_scalar(out=gneg[:, 1:2], in0=gneg[:, 1:2], scalar1=-1.0, scalar2=None, op0=ALU.mult),
         reads=['gneg'], writes=['gneg'])

    ngroups = (L1_NC + L1_WG - 1) // L1_WG
    state = dict(oti=0, psi=0, r=0)
    loaded = set()

    def load_group(g):
        if g in loaded or g >= ngroups:
            return
        loaded.add(g)
        s = g % 2
        c0 = g * L1_WG
        cw_ = min(L1_WG, L1_NC - c0)
        P.dma('sync', lambda e: e.dma_start(out=wst[s][:, :, 0:cw_],
                                            in_=w1[:, c0:c0 + cw_].rearrange("(k p) c -> p k c", p=128)),
              'wst%d' % s, writes=['wst%d' % s])
        for k in range(8):
            eng = ('vector', 'gpsimd')[k % 2]
            P.op(eng, lambda e, k=k: e.tensor_copy(out=wbf[s][:, k, 0:cw_], in_=wst[s][:, k, 0:cw_]),
                 reads=['wst%d' % s], writes=['wbf%d_%d' % (s, k)])

    def wkeys(s):
        return ['wbf%d_%d' % (s, k) for k in range(8)]

    def ensure(col):
        g = col // L1_WG
        load_group(g)
        load_group(g + 1)
        return g % 2, col % L1_WG

    def proj(ps, psk, col, t0, n, halo=False):
        s, cl = ensure(col)
        for k in range(8):
            rhs = xhb[:, k, :] if halo else xb[:, k, t0:t0 + n]
            P.mm(ps[:, 0:n], wbf[s][:, k, cl:cl + 128], rhs, k == 0, k == 7,
                 reads=wkeys(s) + (['xhb'] if halo else xbk), writes=[psk])

    def next_ps():
        i = state['psi']
        state['psi'] = (i + 1) % 6
        return pss[i], 'ps%d' % i

    def next_ot():
        i = state['oti']
        state['oti'] = (i + 1) % 3
        return ot[i], 'ot%d' % i

    n_plain, n_rope, n_conv = L1_NPLAIN // 128, L1_NROPE // 128, L1_NCONV // 128
    for c in range(n_plain):
        scale = 0.125 if 2 <= c < 4 else 1.0
        for tt in range(NT):
            ps, psk = next_ps()
            proj(ps, psk, c * 128, tt * 512, 512)
            o, ok = next_ot()
            P.op('scalar', lambda e, o=o, ps=ps, scale=scale: e.activation(out=o[:], in_=ps[:], func=AF.Copy, scale=scale),
                 reads=[psk], writes=[ok])
            P.dma('gpsimd', lambda e, o=o, c=c, tt=tt: e.dma_start(out=oA[c * 128:(c + 1) * 128, tt * 512:(tt + 1) * 512], in_=o[:]),
                  ok, reads=[ok], final=True)
    base = L1_NPLAIN + 128
    for c in range(n_rope):
        for tt in range(NT):
            psa, pska = next_ps()
            proj(psa, pska, base + c * 256, tt * 512, 512)
            psb, pskb = next_ps()
            proj(psb, pskb, base + c * 256 + 128, tt * 512, 512)
            i = state['r']
            state['r'] = (i + 1) % 2
            P.op('vector', lambda e, i=i, psa=psa, tt=tt: e.tensor_tensor(out=r1[i][:], in0=psa[:], in1=cos_t[:, tt * 512:(tt + 1) * 512], op=ALU.mult),
                 reads=[pska, 'cos'], writes=['r1_%d' % i])
            P.op('vector', lambda e, i=i, psb=psb, tt=tt: e.tensor_tensor(out=r2[i][:], in0=psb[:], in1=sin_t[:, tt * 512:(tt + 1) * 512], op=ALU.mult),
                 reads=[pskb, 'sin'], writes=['r2_%d' % i])
            o, ok = next_ot()
            P.op('gpsimd', lambda e, i=i, o=o: e.tensor_tensor(out=o[:], in0=r1[i][:], in1=r2[i][:], op=ALU.add),
                 reads=['r1_%d' % i, 'r2_%d' % i], writes=[ok])
            P.dma('gpsimd', lambda e, o=o, c=c, tt=tt: e.dma_start(out=oB[c * 128:(c + 1) * 128, tt * 512:(tt + 1) * 512], in_=o[:]),
                  ok, reads=[ok], final=True)
    base = L1_NPLAIN + 128 + 2 * L1_NROPE
    for c in range(n_conv):
        col = base + c * 128
        proj(psh, 'psh', col, 0, 4, halo=True)
        P.op('scalar', lambda e: e.copy(out=cbuf[:, 0:2], in_=psh[:, 0:2]), reads=['psh'], writes=['cbuf'])
        P.op('scalar', lambda e: e.copy(out=cbuf[:, T + 2:T + 4], in_=psh[:, 2:4]), reads=['psh'], writes=['cbuf'])
        for tt in range(NT):
            ps, psk = next_ps()
            proj(ps, psk, col, tt * 512, 512)
            P.op('scalar', lambda e, ps=ps, tt=tt: e.copy(out=cbuf[:, 2 + tt * 512:2 + (tt + 1) * 512], in_=ps[:]),
                 reads=[psk], writes=['cbuf'])
        P.op('vector', lambda e, c=c: e.tensor_scalar(out=cacc[:], in0=cbuf[:, 0:T], scalar1=cw[:, c, 0:1], scalar2=None, op0=ALU.mult),
             reads=['cbuf', 'cw'], writes=['cacc'])
        for j in range(1, 5):
            P.op('vector', lambda e, c=c, j=j: e.scalar_tensor_tensor(out=cacc[:], in0=cbuf[:, j:j + T], scalar=cw[:, c, j:j + 1], in1=cacc[:],
                                                                      op0=ALU.mult, op1=ALU.add),
                 reads=['cbuf', 'cw', 'cacc'], writes=['cacc'])
        if c >= 4:
            P.op('scalar', lambda e: e.activation(out=cout[:], in_=cacc[:], func=AF.Silu), reads=['cacc'], writes=['cout'])
        else:
            P.op('scalar', lambda e: e.activation(out=cacc[:], in_=cacc[:], func=AF.Silu), reads=['cacc'], writes=['cacc'])
            P.op('gpsimd', lambda e: e.tensor_tensor(out=csq[:], in0=cacc[:], in1=cacc[:], op=ALU.mult), reads=['cacc'], writes=['csq'])
            sc = 0.125 if c < 2 else 1.0
            for tt in range(NT):
                P.mm(psn[:], bo[:], csq[:, tt * 512:(tt + 1) * 512], True, True, reads=['bo', 'csq'], writes=['psn'])
                P.op('scalar', lambda e: e.activation(out=crs[:], in_=psn[:], func=AF.Sqrt, bias=eps6[:, 0:1], scale=1.0),
                     reads=['psn', 'eps6'], writes=['crs'])
                P.op('vector', lambda e: e.reciprocal(out=crs[:], in_=crs[:]), reads=['crs'], writes=['crs'])
                P.op('vector', lambda e, tt=tt, sc=sc: e.scalar_tensor_tensor(out=cout[:, tt * 512:(tt + 1) * 512], in0=cacc[:, tt * 512:(tt + 1) * 512],
                                                                             scalar=sc, in1=crs[:], op0=ALU.mult, op1=ALU.mult),
                     reads=['cacc', 'crs'], writes=['cout'])
        P.dma('gpsimd', lambda e, c=c: e.dma_start(out=oC[c * 128:(c + 1) * 128, :], in_=cout[:]), 'cout', reads=['cout'], final=True)
    col = base + L1_NCONV
    for tt in range(NT):
        ps, psk = next_ps()
        proj(ps, psk, col, tt * 512, 512)
        sl = slice(tt * 512, (tt + 1) * 512)
        P.op('scalar', lambda e, ps=ps, sl=sl: e.activation(out=otg[0:32, sl], in_=ps[0:32, :], func=AF.Identity, bias=gp[0:32, 0:1], scale=1.0),
             reads=[psk, 'gp'], writes=['otg'])
        P.op('scalar', lambda e, ps=ps: e.activation(out=gt1[32:64, :], in_=ps[32:64, :], func=AF.Exp, bias=gneg[32:64, 0:1], scale=-1.0),
             reads=[psk, 'gneg'], writes=['gt1'])
        P.op('scalar', lambda e: e.activation(out=gt1[32:64, :], in_=gt1[32:64, :], func=AF.Ln, bias=1.0, scale=1.0),
             reads=['gt1'], writes=['gt1'])
        P.op('vector', lambda e, sl=sl: e.tensor_scalar(out=otg[32:64, sl], in0=gt1[32:64, :], scalar1=-1.0, scalar2=None, op0=ALU.mult),
             reads=['gt1'], writes=['otg'])
        P.op('scalar', lambda e, ps=ps, sl=sl: e.activation(out=otg[64:96, sl], in_=ps[64:96, :], func=AF.Sigmoid),
             reads=[psk], writes=['otg'])
        P.op('scalar', lambda e, ps=ps: e.activation(out=gt1[96:128, :], in_=ps[96:128, :], func=AF.Exp, bias=gp[96:128, 0:1], scale=1.0),
             reads=[psk, 'gp'], writes=['gt1'])
        P.op('scalar', lambda e: e.activation(out=gt1[96:128, :], in_=gt1[96:128, :], func=AF.Ln, bias=1.0, scale=1.0),
             reads=['gt1'], writes=['gt1'])
        P.op('vector', lambda e, sl=sl: e.tensor_scalar(out=otg[96:128, sl], in0=gt1[96:128, :], scalar1=gneg[96:128, 1:2], scalar2=None, op0=ALU.mult),
             reads=['gt1', 'gneg'], writes=['otg'])
    P.dma('gpsimd', lambda e: e.dma_start(out=oG[:, :], in_=otg[:]), 'otg', reads=['otg'], final=True)
    P.emit()
    return nc


def rope_tables_np():
    inv = (10000.0 ** (-np.arange(0, 64, 2, dtype=np.float32) / np.float32(64))).astype(np.float32)
    ang = np.arange(SEQ, dtype=np.float32)[:, None] * inv[None, :]
    return np.cos(ang).astype(np.float32), np.sin(ang).astype(np.float32)


def l1_host_inputs(xT_full, layer, inp, consts):
    w1 = consts['w1'][layer]
    maps = []
    xpad = np.zeros((D, SEQ + 4), np.float32)
    xpad[:, 2:SEQ + 2] = xT_full
    for c in range(NCORES):
        t0 = c * TC
        xh = np.concatenate([xpad[:, t0:t0 + 2], xpad[:, t0 + TC + 2:t0 + TC + 4]], axis=1)
        maps.append(dict(xT=np.ascontiguousarray(xT_full[:, t0:t0 + TC]), xhT=np.ascontiguousarray(xh), w1=w1,
                         cos=consts['cos128'][:, t0:t0 + TC], sin=consts['sin128'][:, t0:t0 + TC],
                         convw=consts['convw'][layer], gpar=consts['gpar'][layer], bones=consts['bones']))
    return maps


def make_consts(inp):
    c = {}
    cols = l1_columns()
    w_in = inp['w_in']
    w1 = []
    for l in range(DEPTH):
        w = np.zeros((D, L1_NC), np.float32)
        m = cols >= 0
        w[:, m] = w_in[l][:, cols[m]]
        w1.append(w)
    c['w1'] = w1
    cos, sin = rope_tables_np()
    r = np.arange(128)
    c['cos128'] = np.ascontiguousarray(cos[:, r % 32].T)
    sgn = np.where((r % 64) < 32, -1.0, 1.0).astype(np.float32)
    c['sin128'] = np.ascontiguousarray((sin[:, r % 32] * sgn[None, :]).T)
    c['convw'] = [np.ascontiguousarray(inp['c_conv_w'][l].T) for l in range(DEPTH)]
    gpar = []
    for l in range(DEPTH):
        g = np.zeros((128, 2), np.float32)
        for d in range(2):
            for h in range(4):
                g[d * 4 + h, 0] = inp['a_gate_bias'][l, d, 0, h]
                g[32 + d * 4 + h, 0] = inp['a_gate_bias'][l, d, 1, h]
                g[96 + d * 4 + h, 0] = inp['c_dt_bias'][l, d, h]
                g[96 + d * 4 + h, 1] = inp['c_a_log'][l, d, h]
        gpar.append(g)
    c['gpar'] = gpar
    i = np.arange(128)
    c['bones'] = (i[:, None] // 64 == i[None, :] // 64).astype(np.float32)
    return c


JQ = 1024
NB_JOBS = 6
ND_JOBS = 2


def build_l2a():
    nc = _new_nc()
    BqT = nc.dram_tensor("BqT", [NB_JOBS, 64, 4, JQ], BF16, kind="ExternalInput").ap()
    BkT = nc.dram_tensor("BkT", [NB_JOBS, 64, 4, JQ + 128], BF16, kind="ExternalInput").ap()
    Bv = nc.dram_tensor("Bv", [NB_JOBS, 4, JQ + 128, 65], BF16, kind="ExternalInput").ap()
    DqT = nc.dram_tensor("DqT", [ND_JOBS, 64, 2, 8, 4, 128], BF16, kind="ExternalInput").ap()
    DkT = nc.dram_tensor("DkT", [ND_JOBS, 64, 2, JQ + 256], BF16, kind="ExternalInput").ap()
    Dv = nc.dram_tensor("Dv", [ND_JOBS, 2, JQ + 256, 65], BF16, kind="ExternalInput").ap()
    sinkd = nc.dram_tensor("sink", [128, 8], F32, kind="ExternalInput").ap()
    masks = nc.dram_tensor("masks", [128, 3, 512], BF16, kind="ExternalInput").ap()
    yB = nc.dram_tensor("yB", [NB_JOBS, JQ, 260], F32, kind="ExternalOutput").ap()
    yD = nc.dram_tensor("yD", [ND_JOBS, JQ, 520], F32, kind="ExternalOutput").ap()

    P = Prog(nc)
    mk = P.sb("mk", [128, 3, 512], BF16)
    esink = P.sb("esink", [128, 8], F32)
    qt = [P.sb("qt%d" % i, [64, 4, JQ], BF16) for i in range(2)]
    kt = [P.sb("kt%d" % i, [64, 4, JQ + 128], BF16) for i in range(2)]
    vt = [P.sb("vt%d" % i, [128, 4, 9, 65], BF16) for i in range(2)]
    dq = [P.sb("dq%d" % i, [64, 2, 8, 4, 128], BF16) for i in range(2)]
    dk = [P.sb("dk%d" % i, [64, 2, JQ + 256], BF16) for i in range(2)]
    dv = [P.sb("dv%d" % i, [128, 2, 10, 65], BF16) for i in range(2)]
    pt = [P.sb("pt%d" % i, [128, 512], BF16) for i in range(4)]
    ob = [P.sb("ob%d" % i, [128, 260], F32) for i in range(2)]
    od = [P.sb("od%d" % i, [128, 260], F32) for i in range(2)]
    den = P.sb("den", [128, 4], F32)
    pss = [P.ps("pss%d" % i, [128, 512]) for i in range(4)]
    pso = [P.ps("pso%d" % i) for i in range(2)]

    P.dma('sync', lambda e: e.dma_start(out=mk[:], in_=masks), 'mk', writes=['mk'])
    P.dma('sync', lambda e: e.dma_start(out=esink[:], in_=sinkd), 'esink', writes=['esink'])
    P.op('scalar', lambda e: e.activation(out=esink[:], in_=esink[:], func=AF.Exp), reads=['esink'], writes=['esink'])
    st = dict(ps=0, pt=0, po=0, ob=0, od=0)

    def nxt(name, n):
        i = st[name]
        st[name] = (i + 1) % n
        return i

    for j in range(NB_JOBS):
        s = j % 2
        P.dma('sync', lambda e, j=j, s=s: e.dma_start(out=qt[s][:], in_=BqT[j]), 'qt%d' % s, writes=['qt%d' % s])
        P.dma('sync', lambda e, j=j, s=s: e.dma_start(out=kt[s][:], in_=BkT[j]), 'kt%d' % s, writes=['kt%d' % s])
        for h in range(4):
            P.dma('sync', lambda e, j=j, s=s, h=h: e.dma_start(out=vt[s][:, h, :, :], in_=Bv[j, h].rearrange("(n p) c -> p n c", p=128)),
                  'vt%d_%d' % (s, h), writes=['vt%d_%d' % (s, h)])
        for i in range(JQ // 128):
            po = nxt('po', 2)
            for hp in range(2):
                pi = nxt('ps', 4)
                ps = pss[pi]
                for hh in range(2):
                    h = hp * 2 + hh
                    for tl in range(2):
                        P.mm(ps[:, (hh * 2 + tl) * 128:(hh * 2 + tl + 1) * 128], kt[s][:, h, (i + tl) * 128:(i + tl + 1) * 128],
                             qt[s][:, h, i * 128:(i + 1) * 128], True, True, reads=['kt%d' % s, 'qt%d' % s], writes=['pss%d' % pi])
                ti = nxt('pt', 4)
                P.op('scalar', lambda e, ti=ti, ps=ps: e.activation(out=pt[ti][:], in_=ps[:], func=AF.Exp, scale=0.125),
                     reads=['pss%d' % pi], writes=['pt%d' % ti])
                P.op('vector', lambda e, ti=ti: e.tensor_tensor(out=pt[ti][:], in0=pt[ti][:], in1=mk[:, 0, :], op=ALU.mult),
                     reads=['pt%d' % ti, 'mk'], writes=['pt%d' % ti])
                for hh in range(2):
                    h = hp * 2 + hh
                    for tl in range(2):
                        P.mm(pso[po][:, h * 65:(h + 1) * 65], pt[ti][:, (hh * 2 + tl) * 128:(hh * 2 + tl + 1) * 128], vt[s][:, h, i + tl, :],
                             tl == 0, tl == 1, reads=['pt%d' % ti, 'vt%d_%d' % (s, h)], writes=['pso%d' % po])
            oi = nxt('ob', 2)
            P.op('scalar', lambda e, oi=oi, po=po: e.copy(out=ob[oi][:], in_=pso[po][:, 0:260]),
                 reads=['pso%d' % po], writes=['ob%d' % oi])
            P.dma('gpsimd', lambda e, oi=oi, j=j, i=i: e.dma_start(out=yB[j, i * 128:(i + 1) * 128, :], in_=ob[oi][:]),
                  'ob%d' % oi, reads=['ob%d' % oi], final=True)
    for j in range(ND_JOBS):
        s = j % 2
        P.dma('sync', lambda e, j=j, s=s: e.dma_start(out=dq[s][:], in_=DqT[j]), 'dq%d' % s, writes=['dq%d' % s])
        P.dma('sync', lambda e, j=j, s=s: e.dma_start(out=dk[s][:], in_=DkT[j]), 'dk%d' % s, writes=['dk%d' % s])
        for g in range(2):
            P.dma('sync', lambda e, j=j, s=s, g=g: e.dma_start(out=dv[s][:, g, :, :], in_=Dv[j, g].rearrange("(n p) c -> p n c", p=128)),
                  'dv%d_%d' % (s, g), writes=['dv%d_%d' % (s, g)])
        for i in range(JQ // 128):
            for g in range(2):
                po = nxt('po', 2)
                tis = []
                for tl in range(3):
                    pi = nxt('ps', 4)
                    ps = pss[pi]
                    P.mm(ps[:], dk[s][:, g, (i + tl) * 128:(i + tl + 1) * 128], dq[s][:, g, i, :, :].rearrange("p h q -> p (h q)"),
                         True, True, reads=['dk%d' % s, 'dq%d' % s], writes=['pss%d' % pi])
                    ti = nxt('pt', 4)
                    tis.append(ti)
                    P.op('scalar', lambda e, ti=ti, ps=ps: e.activation(out=pt[ti][:], in_=ps[:], func=AF.Exp, scale=0.125),
                         reads=['pss%d' % pi], writes=['pt%d' % ti])
                    if tl != 1:
                        mi = 1 if tl == 0 else 2
                        P.op('vector', lambda e, ti=ti, mi=mi: e.tensor_tensor(out=pt[ti][:], in0=pt[ti][:], in1=mk[:, mi, :], op=ALU.mult),
                             reads=['pt%d' % ti, 'mk'], writes=['pt%d' % ti])
                for h in range(4):
                    for tl in range(3):
                        P.mm(pso[po][:, h * 65:(h + 1) * 65], pt[tis[tl]][:, h * 128:(h + 1) * 128], dv[s][:, g, i + tl, :],
                             tl == 0, tl == 2, reads=['pt%d' % tis[tl], 'dv%d_%d' % (s, g)], writes=['pso%d' % po])
                oi = nxt('od', 2)
                P.op('scalar', lambda e, oi=oi, po=po: e.copy(out=od[oi][:], in_=pso[po][:, 0:260]),
                     reads=['pso%d' % po], writes=['od%d' % oi])
                P.dma('gpsimd', lambda e, oi=oi, j=j, i=i, g=g: e.dma_start(out=yD[j, i * 128:(i + 1) * 128, g * 260:(g + 1) * 260], in_=od[oi][:]),
                      'od%d' % oi, reads=['od%d' % oi], final=True)
    P.emit()
    return nc


B_DIL = (1, 4, 16)


def l2a_host_inputs(oA, oB, inp, layer):
    bq = oB[0:768]
    bk = oB[768:1536]
    dqa = oB[1536:2048]
    dka = oB[2048:2176]
    bv = oA[768:1536]
    dva = oA[1536:1664]
    maps = [dict() for _ in range(NCORES)]
    BqT = np.zeros((NCORES, NB_JOBS, 64, 4, JQ), NPBF)
    BkT = np.zeros((NCORES, NB_JOBS, 64, 4, JQ + 128), NPBF)
    Bvv = np.zeros((NCORES, NB_JOBS, 4, JQ + 128, 65), NPBF)
    for g, r in enumerate(B_DIL):
        n = SEQ // r
        def strided(a):
            a = a.reshape(4, 64, n, r).transpose(0, 1, 3, 2)
            out = np.zeros((4, 64, r, n + 128), NPBF)
            out[..., 64:64 + n] = a
            return out
        q = strided(bq[g * 256:(g + 1) * 256])
        k = strided(bk[g * 256:(g + 1) * 256])
        v = strided(bv[g * 256:(g + 1) * 256])
        valid = np.zeros((n + 128,), NPBF)
        valid[64:64 + n] = 1
        ppr = n // JQ
        for jid in range(16):
            rho, pj = jid // ppr, jid % ppr
            c, jl = jid // 2, g * 2 + jid % 2
            BqT[c, jl] = q[:, :, rho, 64 + pj * JQ:64 + (pj + 1) * JQ].transpose(1, 0, 2)
            BkT[c, jl] = k[:, :, rho, pj * JQ:pj * JQ + JQ + 128].transpose(1, 0, 2)
            Bvv[c, jl, :, :, 0:64] = v[:, :, rho, pj * JQ:pj * JQ + JQ + 128].transpose(0, 2, 1)
            Bvv[c, jl, :, :, 64] = valid[pj * JQ:pj * JQ + JQ + 128][None, :]
    DqT = np.zeros((NCORES, ND_JOBS, 64, 2, 8, 4, 128), NPBF)
    DkT = np.zeros((NCORES, ND_JOBS, 64, 2, JQ + 256), NPBF)
    Dvv = np.zeros((NCORES, ND_JOBS, 2, JQ + 256, 65), NPBF)
    kpad = np.zeros((2, 64, SEQ + 256), NPBF)
    kpad[:, :, 128:128 + SEQ] = dka.reshape(2, 64, SEQ)
    vpad = np.zeros((2, 64, SEQ + 256), NPBF)
    vpad[:, :, 128:128 + SEQ] = dva.reshape(2, 64, SEQ)
    valid = np.zeros((SEQ + 256,), NPBF)
    valid[128:128 + SEQ] = 1
    dq4 = dqa.reshape(2, 4, 64, SEQ)
    for jid in range(16):
        c, jl = jid // 2, jid % 2
        t0 = jid * JQ
        DqT[c, jl] = dq4[:, :, :, t0:t0 + JQ].reshape(2, 4, 64, 8, 128).transpose(2, 0, 3, 1, 4)
        DkT[c, jl] = kpad[:, :, t0:t0 + JQ + 256].transpose(1, 0, 2)
        Dvv[c, jl, :, :, 0:64] = vpad[:, :, t0:t0 + JQ + 256].transpose(0, 2, 1)
        Dvv[c, jl, :, :, 64] = valid[t0:t0 + JQ + 256][None, :]
    kk = np.arange(128)
    ge = (kk[:, None] >= kk[None, :]).astype(np.float32)
    le = (kk[:, None] <= kk[None, :]).astype(np.float32)
    masks = np.stack([np.concatenate([ge, le, ge, le], 1), np.concatenate([ge] * 4, 1), np.concatenate([le] * 4, 1)], 1).astype(NPBF)
    sink = np.ascontiguousarray(np.broadcast_to(inp['d_sink'][layer][None, :], (128, 8))).astype(np.float32)
    for c in range(NCORES):
        maps[c] = dict(BqT=BqT[c], BkT=BkT[c], Bv=Bvv[c], DqT=DqT[c], DkT=DkT[c], Dv=Dvv[c], sink=sink, masks=masks)
    return maps


def l2a_host_outputs(res):
    accB = np.zeros((3, SEQ, 4, 65), np.float32)
    yD = np.zeros((SEQ, 520), np.float32)
    for g, r in enumerate(B_DIL):
        n = SEQ // r
        ppr = n // JQ
        tmp = np.zeros((r, n, 260), np.float32)
        for jid in range(16):
            rho, pj = jid // ppr, jid % ppr
            c, jl = jid // 2, g * 2 + jid % 2
            tmp[rho, pj * JQ:(pj + 1) * JQ] = res[c]['yB'][jl]
        accB[g] = tmp.transpose(1, 0, 2).reshape(SEQ, 4, 65)
    for jid in range(16):
        c, jl = jid // 2, jid % 2
        yD[jid * JQ:(jid + 1) * JQ] = res[c]['yD'][jl]
    return accB, yD.reshape(SEQ, 8, 65)


NCH = SEQ // 128


def build_l2m():
    nc = _new_nc()
    qTd = nc.dram_tensor("qT", [64, SEQ], BF16, kind="ExternalInput").ap()
    kTd = nc.dram_tensor("kT", [64, SEQ], BF16, kind="ExternalInput").ap()
    ktd = nc.dram_tensor("ktok", [SEQ, 64], BF16, kind="ExternalInput").ap()
    vtd = nc.dram_tensor("vtok", [SEQ, 64], BF16, kind="ExternalInput").ap()
    igd = nc.dram_tensor("ig", [128, NCH], F32, kind="ExternalInput").ap()
    lfd = nc.dram_tensor("lf", [128, NCH], F32, kind="ExternalInput").ap()
    cst = nc.dram_tensor("cst", [128, 3, 128], F32, kind="ExternalInput").ap()
    yM = nc.dram_tensor("yM", [SEQ, 64], F32, kind="ExternalOutput").ap()

    P = Prog(nc)
    qT = P.sb("qT", [64, SEQ], BF16)
    kT = P.sb("kT", [64, SEQ], BF16)
    kt = P.sb("kt", [128, NCH, 64], BF16)
    vt = P.sb("vt", [128, NCH, 64], BF16)
    ig = P.sb("ig", [128, NCH], F32)
    lf = P.sb("lf", [128, NCH], F32)
    cs = P.sb("cs", [128, 3, 128], F32)
    gS = P.sb("gS", [128, NCH], F32)
    gS2 = P.sb("gS2", [128, NCH], F32)
    eb = P.sb("eb", [128, NCH], F32)
    dec = P.sb("dec", [128, NCH], F32)
    dn = P.sb("dn", [128, NCH], F32)
    Va = P.sb("Va", [128, NCH, 65], BF16)
    Va2 = P.sb("Va2", [128, NCH, 65], BF16)
    raw = P.sb("raw", [128, NCH, 65], F32)
    sm = [P.sb("sm%d" % i, [128, 128], BF16) for i in range(2)]
    C = P.sb("C", [64, 65], F32)
    Cb = P.sb("Cb", [64, 65], BF16)
    psA = P.ps("psA")[:, 0:NCH]
    psB = P.ps("psB")[:, 0:NCH]
    psS = [P.ps("psS%d" % i)[:, 0:128] for i in range(2)]
    psO = [P.ps("psO%d" % i)[:, 0:65] for i in range(2)]
    psC = P.ps("psC")[0:64, 0:65]

    for (t, d, k) in ((qT, qTd, 'qT'), (kT, kTd, 'kT'), (ig, igd, 'ig'), (lf, lfd, 'lf'), (cs, cst, 'cs')):
        P.dma('sync', lambda e, t=t, d=d: e.dma_start(out=t[:], in_=d), k, writes=[k])
    P.dma('sync', lambda e: e.dma_start(out=kt[:], in_=ktd.rearrange("(n p) d -> p n d", p=128)), 'kt', writes=['kt'])
    P.dma('sync', lambda e: e.dma_start(out=vt[:], in_=vtd.rearrange("(n p) d -> p n d", p=128)), 'vt', writes=['vt'])
    P.mm(psA[:], cs[:, 0, :], lf[:], True, True, reads=['cs', 'lf'], writes=['psA'])
    P.mm(psB[:], cs[:, 1, :], lf[:], True, True, reads=['cs', 'lf'], writes=['psB'])
    P.op('vector', lambda e: e.tensor_tensor(out=gS[:], in0=ig[:], in1=psA[:], op=ALU.subtract), reads=['ig', 'psA'], writes=['gS'])
    P.op('scalar', lambda e: e.activation(out=gS[:], in_=gS[:], func=AF.Exp), reads=['gS'], writes=['gS'])
    P.op('scalar', lambda e: e.activation(out=eb[:], in_=psA[:], func=AF.Exp), reads=['psA'], writes=['eb'])
    P.op('scalar', lambda e: e.activation(out=dec[:], in_=psB[:], func=AF.Exp), reads=['psB'], writes=['dec'])
    P.op('vector', lambda e: e.tensor_tensor(out=gS2[:], in0=gS[:], in1=dec[:], op=ALU.mult), reads=['gS', 'dec'], writes=['gS2'])
    P.op('vector', lambda e: e.tensor_copy(out=Va[:, :, 64], in_=gS[:]), reads=['gS'], writes=['Va'])
    P.op('vector', lambda e: e.tensor_copy(out=Va2[:, :, 64], in_=gS2[:]), reads=['gS2'], writes=['Va2'])
    P.op('vector', lambda e: e.tensor_tensor(out=Va[:, :, 0:64], in0=vt[:], in1=bcast_last(gS[:], 64), op=ALU.mult), reads=['vt', 'gS', 'Va'], writes=['Va'])
    P.op('vector', lambda e: e.tensor_tensor(out=Va2[:, :, 0:64], in0=vt[:], in1=bcast_last(gS2[:], 64), op=ALU.mult), reads=['vt', 'gS2', 'Va2'], writes=['Va2'])
    P.op('vector', lambda e: e.memset(C[:], 0.0), writes=['C'])
    for n in range(NCH):
        s = n % 2
        sl = slice(n * 128, (n + 1) * 128)
        P.mm(psS[s][:], kT[:, sl], qT[:, sl], True, True, reads=['kT', 'qT'], writes=['psS%d' % s])
        P.op('vector', lambda e, s=s: e.tensor_tensor(out=sm[s][:], in0=psS[s][:], in1=cs[:, 2, :], op=ALU.mult),
             reads=['psS%d' % s, 'cs'], writes=['sm%d' % s])
        P.mm(psO[s][:], sm[s][:], Va[:, n, :], True, n == 0, reads=['sm%d' % s, 'Va'], writes=['psO%d' % s])
        if n > 0:
            P.mm(psO[s][:], qT[:, sl], Cb[:], False, True, reads=['qT', 'Cb'], writes=['psO%d' % s])
        P.op('scalar', lambda e, s=s, n=n: e.copy(out=raw[:, n, :], in_=psO[s][:]), reads=['psO%d' % s], writes=['raw'])
        if n < NCH - 1:
            P.mm(psC[:], kt[:, n, :], Va2[:, n, :], True, True, reads=['kt', 'Va2'], writes=['psC'])
            P.op('vector', lambda e, n=n: e.scalar_tensor_tensor(out=C[:], in0=C[:], scalar=dec[0:64, n:n + 1], in1=psC[:], op0=ALU.mult, op1=ALU.add),
                 reads=['C', 'dec', 'psC'], writes=['C'])
            P.op('scalar', lambda e: e.copy(out=Cb[:], in_=C[:]), reads=['C'], writes=['Cb'])
    P.op('vector', lambda e: e.tensor_tensor(out=dn[:], in0=raw[:, :, 64], in1=eb[:], op=ALU.mult), reads=['raw', 'eb'], writes=['dn'])
    P.op('scalar', lambda e: e.activation(out=dn[:], in_=dn[:], func=AF.Abs), reads=['dn'], writes=['dn'])
    P.op('vector', lambda e: e.tensor_scalar(out=dn[:], in0=dn[:], scalar1=1.0, scalar2=None, op0=ALU.max), reads=['dn'], writes=['dn'])
    P.op('vector', lambda e: e.reciprocal(out=dn[:], in_=dn[:]), reads=['dn'], writes=['dn'])
    P.op('vector', lambda e: e.tensor_tensor(out=dn[:], in0=dn[:], in1=eb[:], op=ALU.mult), reads=['dn', 'eb'], writes=['dn'])
    P.op('vector', lambda e: e.tensor_tensor(out=raw[:, :, 0:64], in0=raw[:, :, 0:64], in1=bcast_last(dn[:], 64), op=ALU.mult), reads=['raw', 'dn'], writes=['raw'])
    P.dma('sync', lambda e: e.dma_start(out=yM.rearrange("(n p) d -> p n d", p=128), in_=raw[:, :, 0:64]), 'raw', reads=['raw'], final=True)
    P.emit()
    return nc


def seq_consts():
    i = np.arange(128)
    triT = (i[:, None] <= i[None, :]).astype(np.float32)
    ones = np.ones((128, 128), np.float32)
    return np.ascontiguousarray(np.stack([triT, ones, triT], 1))


def l2m_host_inputs(oA, oG):
    cst = seq_consts()
    maps = []
    for u in range(NCORES):
        d, h = u // 4, u % 4
        fl = (lambda a: a[..., ::-1]) if d == 1 else (lambda a: a)
        qT = np.ascontiguousarray(fl(oA[h * 64:(h + 1) * 64]))
        kT = np.ascontiguousarray(fl(oA[256 + h * 64:256 + (h + 1) * 64]))
        vT = fl(oA[512 + h * 64:512 + (h + 1) * 64])
        ig = fl(oG[d * 4 + h])
        lf = fl(oG[32 + d * 4 + h])
        maps.append(dict(qT=qT, kT=kT, ktok=np.ascontiguousarray(kT.T), vtok=np.ascontiguousarray(vT.T),
                         ig=np.ascontiguousarray(ig.reshape(NCH, 128).T), lf=np.ascontiguousarray(lf.reshape(NCH, 128).T), cst=cst))
    return maps


def l2m_host_outputs(res):
    out = np.zeros((2, 256, SEQ), np.float32)
    for u in range(NCORES):
        d, h = u // 4, u % 4
        y = res[u]['yM']
        if d == 1:
            y = y[::-1]
        out[d, h * 64:(h + 1) * 64] = y.T
    return out


def build_l2d():
    nc = _new_nc()
    qTd = nc.dram_tensor("qT", [64, SEQ], BF16, kind="ExternalInput").ap()
    kTd = nc.dram_tensor("kT", [64, SEQ], BF16, kind="ExternalInput").ap()
    ktd = nc.dram_tensor("ktok", [SEQ, 64], BF16, kind="ExternalInput").ap()
    vtd = nc.dram_tensor("vtok", [SEQ, 64], BF16, kind="ExternalInput").ap()
    btd = nc.dram_tensor("beta", [128, NCH], F32, kind="ExternalInput").ap()
    gd = nc.dram_tensor("g", [128, NCH], F32, kind="ExternalInput").ap()
    cst = nc.dram_tensor("cst", [128, 5, 128], F32, kind="ExternalInput").ap()
    yC = nc.dram_tensor("yC", [SEQ, 64], F32, kind="ExternalOutput").ap()

    P = Prog(nc)
    G = 4
    R = 2 * G
    qT = P.sb("qT", [64, SEQ], BF16)
    kT = P.sb("kT", [64, SEQ], BF16)
    kt = P.sb("kt", [128, NCH, 64], BF16)
    vt = P.sb("vt", [128, NCH, 64], BF16)
    kdec = P.sb("kdec", [128, NCH, 64], BF16)
    oout = P.sb("oout", [128, NCH, 64], F32)
    beta = P.sb("beta", [128, NCH], F32)
    g = P.sb("g", [128, NCH], F32)
    cs = P.sb("cs", [128, 5, 128], F32)
    gc = P.sb("gc", [128, NCH], F32)
    ngc = P.sb("ngc", [128, NCH], F32)
    egc = P.sb("egc", [128, NCH], F32)
    negc = P.sb("negc", [128, NCH], F32)
    egl = P.sb("egl", [128, NCH], F32)
    kfac = P.sb("kfac", [128, NCH], F32)
    Gball = P.sb("Gball", [128, G, 128], F32)
    Gb = [Gball[:, i, :] for i in range(G)]
    decL = [P.sb("decL%d" % i, [128, 128], F32) for i in range(G)]
    decT = [P.sb("decT%d" % i, [128, 128], F32) for i in range(G)]
    Ak = [[P.sb("Ak%d_%d" % (i, j), [128, 128], F32) for j in range(2)] for i in range(G)]
    Bk = [[P.sb("Bk%d_%d" % (i, j), [128, 128], F32) for j in range(2)] for i in range(G)]
    Yk = [[P.sb("Yk%d_%d" % (i, j), [128, 128], F32) for j in range(2)] for i in range(G)]
    TT = [P.sb("TT%d" % i, [128, 128], F32) for i in range(R)]
    aT = [P.sb("aT%d" % i, [128, 128], BF16) for i in range(R)]
    Z = P.sb("Z", [128, 64], F32)
    vn = P.sb("vn", [128, 64], BF16)
    o2s = P.sb("o2s", [128, 64], F32)
    S = P.sb("S", [64, 64], F32)
    Sb = P.sb("Sb", [64, 64], BF16)
    tb = [P.ps("tbk%d" % i) for i in range(6)]
    psRA = P.ps("psRA")
    psRB = P.ps("psRB")
    H0, H1 = slice(0, 128), slice(128, 256)
    st = dict(b=0)

    def nbk():
        i = st['b']
        st['b'] = (i + 1) % 6
        return tb[i], 'tbk%d' % i

    for (t, d, k) in ((qT, qTd, 'qT'), (kT, kTd, 'kT'), (beta, btd, 'beta'), (g, gd, 'g'), (cs, cst, 'cs')):
        P.dma('sync', lambda e, t=t, d=d: e.dma_start(out=t[:], in_=d), k, writes=[k])
    P.dma('sync', lambda e: e.dma_start(out=kt[:], in_=ktd.rearrange("(n p) d -> p n d", p=128)), 'kt', writes=['kt'])
    P.dma('sync', lambda e: e.dma_start(out=vt[:], in_=vtd.rearrange("(n p) d -> p n d", p=128)), 'vt', writes=['vt'])
    TRI, ONES, IDN, M1, M2 = (cs[:, i, :] for i in range(5))
    psG_, pgk = nbk()
    P.mm(psG_[:, H0], TRI, g[:], True, True, reads=['cs', 'g'], writes=[pgk])
    P.mm(psG_[:, H1], ONES, g[:], True, True, reads=['cs', 'g'], writes=[pgk])
    P.op('vector', lambda e: e.tensor_copy(out=gc[:], in_=psG_[:, H0]), reads=[pgk], writes=['gc'])
    P.op('vector', lambda e: e.tensor_scalar(out=ngc[:], in0=psG_[:, H0], scalar1=-1.0, scalar2=None, op0=ALU.mult), reads=[pgk], writes=['ngc'])
    P.op('scalar', lambda e: e.activation(out=egc[:], in_=psG_[:, H0], func=AF.Exp), reads=[pgk], writes=['egc'])
    P.op('vector', lambda e: e.tensor_scalar(out=negc[:], in0=egc[:], scalar1=-1.0, scalar2=None, op0=ALU.mult), reads=['egc'], writes=['negc'])
    P.op('scalar', lambda e: e.activation(out=egl[:], in_=psG_[:, H1], func=AF.Exp), reads=[pgk], writes=['egl'])
    P.op('vector', lambda e: e.tensor_tensor(out=kfac[:], in0=psG_[:, H1], in1=gc[:], op=ALU.subtract), reads=[pgk, 'gc'], writes=['kfac'])
    P.op('scalar', lambda e: e.activation(out=kfac[:], in_=kfac[:], func=AF.Exp), reads=['kfac'], writes=['kfac'])
    P.op('vector', lambda e: e.tensor_tensor(out=kdec[:], in0=kt[:], in1=bcast_last(kfac[:], 64), op=ALU.mult), reads=['kt', 'kfac'], writes=['kdec'])
    P.op('vector', lambda e: e.memset(S[:], 0.0), writes=['S'])

    def tstages(n):
        gi = n % G
        r = n % R
        sl = slice(n * 128, (n + 1) * 128)
        kA = ['Ak%d_0' % gi, 'Ak%d_1' % gi]
        kB = ['Bk%d_0' % gi, 'Bk%d_1' % gi]
        kY = ['Yk%d_0' % gi, 'Yk%d_1' % gi]
        A_, B_, Y_ = Ak[gi], Bk[gi], Yk[gi]
        cur = {}
        stages = []

        def s0():
            if gi == 0:
                P.op('vector', lambda e: e.tensor_copy(out=Gball[:], in_=bcast_last(g[:, n:n + G], 128)), reads=['g'], writes=['Gb%d' % i for i in range(G)])
        stages.append(s0)

        def s1():
            a, ak = nbk()
            cur['a'] = (a, ak)
            P.mm(a[:, H0], Gb[gi][:], TRI, True, False, reads=['Gb%d' % gi, 'cs'], writes=[ak])
            P.mm(a[:, H0], IDN, M1, False, True, reads=['cs'], writes=[ak])
            P.mm(a[:, H1], Gb[gi][:], TRI, True, False, reads=['Gb%d' % gi, 'cs'], writes=[ak])
            P.mm(a[:, H1], IDN, M2, False, True, reads=['cs'], writes=[ak])
        stages.append(s1)

        def s2():
            a, ak = cur['a']
            P.op('scalar', lambda e: e.activation(out=decL[gi][:], in_=a[:, H0], func=AF.Exp, bias=gc[:, n:n + 1], scale=-1.0),
                 reads=[ak, 'gc'], writes=['decL%d' % gi])
            P.op('scalar', lambda e: e.activation(out=decT[gi][:], in_=a[:, H1], func=AF.Exp, bias=ngc[:, n:n + 1], scale=1.0),
                 reads=[ak, 'ngc'], writes=['decT%d' % gi])
        stages.append(s2)

        def s2b():
            b_, bk_ = nbk()
            cur['b'] = (b_, bk_)
            P.mm(b_[:, H0], kT[:, sl], kT[:, sl], True, True, reads=['kT'], writes=[bk_])
            P.mm(b_[:, H1], kT[:, sl], qT[:, sl], True, True, reads=['kT', 'qT'], writes=[bk_])
        stages.append(s2b)

        def s3():
            b_, bk_ = cur['b']
            P.op('vector', lambda e: e.scalar_tensor_tensor(out=A_[0][:], in0=b_[:, H0], scalar=beta[:, n:n + 1], in1=decL[gi][:], op0=ALU.mult, op1=ALU.mult),
                 reads=[bk_, 'beta', 'decL%d' % gi], writes=[kA[0]])
            P.op('vector', lambda e: e.tensor_tensor(out=aT[r][:], in0=b_[:, H1], in1=decT[gi][:], op=ALU.mult),
                 reads=[bk_, 'decT%d' % gi], writes=['aT%d' % r])
        stages.append(s3)

        def s4():
            c, ck = nbk()
            cur['c'] = (c, ck)
            P.mm(c[:, H0], A_[0][:], IDN, True, True, reads=[kA[0], 'cs'], writes=[ck])
        stages.append(s4)

        def s5():
            c, ck = cur['c']
            P.op('scalar', lambda e: e.copy(out=B_[0][:], in_=c[:, H0]), reads=[ck], writes=[kB[0]])
            P.op('vector', lambda e: e.tensor_tensor(out=Y_[0][:], in0=IDN, in1=c[:, H0], op=ALU.subtract), reads=[ck, 'cs'], writes=[kY[0]])
        stages.append(s5)
        for k in range(1, 7):
            a_i, b_i = (k - 1) % 2, k % 2

            def sa(k=k, a_i=a_i, b_i=b_i):
                d, dk = nbk()
                cur['d'] = (d, dk)
                P.mm(d[:, H0], B_[a_i][:], A_[a_i][:], True, True, reads=[kB[a_i], kA[a_i]], writes=[dk])
                if k < 6:
                    P.mm(d[:, H1], A_[a_i][:], B_[a_i][:], True, True, reads=[kB[a_i], kA[a_i]], writes=[dk])
            stages.append(sa)

            def sb_(k=k, a_i=a_i, b_i=b_i):
                d, dk = cur['d']
                P.op('scalar', lambda e: e.copy(out=A_[b_i][:], in_=d[:, H0]), reads=[dk], writes=[kA[b_i]])
                if k < 6:
                    P.op('vector', lambda e: e.tensor_copy(out=B_[b_i][:], in_=d[:, H1]), reads=[dk], writes=[kB[b_i]])
            stages.append(sb_)

            def sc(k=k, a_i=a_i, b_i=b_i):
                y, yk = nbk()
                cur['y'] = (y, yk)
                P.mm(y[:, H0], A_[b_i][:], Y_[a_i][:], True, True, reads=[kA[b_i], kY[a_i]], writes=[yk])
            stages.append(sc)

            def sd(k=k, a_i=a_i, b_i=b_i):
                y, yk = cur['y']
                if k < 6:
                    P.op('vector', lambda e: e.tensor_tensor(out=Y_[b_i][:], in0=Y_[a_i][:], in1=y[:, H0], op=ALU.add),
                         reads=[yk, kY[a_i]], writes=[kY[b_i]])
                else:
                    P.op('vector', lambda e: e.tensor_tensor(out=Y_[b_i][:], in0=Y_[a_i][:], in1=y[:, H0], op=ALU.add),
                         reads=[yk, kY[a_i]], writes=[kY[b_i]])
            stages.append(sd)

        def sf():
            P.op('scalar', lambda e: e.activation(out=TT[r][:], in_=Y_[0][:], func=AF.Copy, scale=beta[:, n:n + 1]),
                 reads=[kY[0], 'beta'], writes=['TT%d' % r])
        stages.append(sf)
        return stages

    def rsteps(n):
        r = n % R
        sl = slice(n * 128, (n + 1) * 128)
        kd = 'kdec'
        steps = []
        if n > 0:
            def r0():
                P.mm(psRA[:, 0:64], kT[:, sl], Sb[:], True, True, reads=['kT', 'Sb'], writes=['psRA'])
                P.mm(psRA[:, 64:128], qT[:, sl], Sb[:], True, True, reads=['qT', 'Sb'], writes=['psRA'])

            def r1():
                P.op('vector', lambda e: e.scalar_tensor_tensor(out=Z[:], in0=psRA[:, 0:64], scalar=negc[:, n:n + 1], in1=vt[:, n, :], op0=ALU.mult, op1=ALU.add),
                     reads=['psRA', 'negc', 'vt'], writes=['Z'])
                P.op('vector', lambda e: e.tensor_scalar(out=o2s[:], in0=psRA[:, 64:128], scalar1=egc[:, n:n + 1], scalar2=None, op0=ALU.mult),
                     reads=['psRA', 'egc'], writes=['o2s'])
        else:
            def r0():
                pass

            def r1():
                P.op('vector', lambda e: e.tensor_copy(out=Z[:], in_=vt[:, n, :]), reads=['vt'], writes=['Z'])
                P.op('vector', lambda e: e.memset(o2s[:], 0.0), writes=['o2s'])
        steps += [r0, r1]

        def r2():
            P.mm(psRB[:, 0:64], TT[r][:], Z[:], True, True, reads=['TT%d' % r, 'Z'], writes=['psRB'])

        def r3():
            P.op('scalar', lambda e: e.copy(out=vn[:], in_=psRB[:, 0:64]), reads=['psRB'], writes=['vn'])

        def r4():
            if n < NCH - 1:
                P.mm(psRB[0:64, 128:192], kdec[:, n, :], vn[:], True, True, reads=[kd, 'vn'], writes=['psRB'])
            P.mm(psRB[:, 64:128], aT[r][:], vn[:], True, True, reads=['aT%d' % r, 'vn'], writes=['psRB'])

        def r5():
            if n < NCH - 1:
                P.op('vector', lambda e: e.scalar_tensor_tensor(out=S[:], in0=S[:], scalar=egl[0:64, n:n + 1], in1=psRB[0:64, 128:192], op0=ALU.mult, op1=ALU.add),
                     reads=['S', 'egl', 'psRB'], writes=['S'])

        def r6():
            if n < NCH - 1:
                P.op('scalar', lambda e: e.copy(out=Sb[:], in_=S[:]), reads=['S'], writes=['Sb'])
            P.op('vector', lambda e: e.tensor_tensor(out=oout[:, n, :], in0=o2s[:], in1=psRB[:, 64:128], op=ALU.add),
                 reads=['psRB', 'o2s'], writes=['oout'])
        steps += [r2, r3, r4, r5, r6]
        return steps

    ngroups = NCH // G
    for gi in range(ngroups + 1):
        tl = [tstages(gi * G + j) for j in range(G)] if gi < ngroups else []
        rl = []
        if gi >= 1:
            for j in range(G):
                rl += rsteps((gi - 1) * G + j)
        nst = len(tl[0]) if tl else 0
        for s in range(max(nst, len(rl))):
            if s < nst:
                for j in range(G):
                    tl[j][s]()
            if s < len(rl):
                rl[s]()
    P.dma('sync', lambda e: e.dma_start(out=yC.rearrange("(n p) d -> p n d", p=128), in_=oout[:]), 'oout', reads=['oout'], final=True)
    P.emit()
    return nc


def l2d_consts():
    i = np.arange(128)
    triT = (i[:, None] <= i[None, :]).astype(np.float32)
    ones = np.ones((128, 128), np.float32)
    ident = np.eye(128, dtype=np.float32)
    BIG = 1.0e5
    m1 = np.where(i[None, :] < i[:, None], 0.0, BIG).astype(np.float32)
    m2 = np.where(i[None, :] >= i[:, None], 0.0, -BIG).astype(np.float32)
    return np.ascontiguousarray(np.stack([triT, ones, ident, m1, m2], 1))


def l2d_host_inputs(oC, oG):
    cst = l2d_consts()
    maps = []
    for u in range(NCORES):
        d, h = u // 4, u % 4
        fl = (lambda a: a[..., ::-1]) if d == 1 else (lambda a: a)
        qT = np.ascontiguousarray(fl(oC[h * 64:(h + 1) * 64]))
        kT = np.ascontiguousarray(fl(oC[256 + h * 64:256 + (h + 1) * 64]))
        vT = fl(oC[512 + h * 64:512 + (h + 1) * 64])
        bt = fl(oG[64 + d * 4 + h])
        gg = fl(oG[96 + d * 4 + h])
        maps.append(dict(qT=qT, kT=kT, ktok=np.ascontiguousarray(kT.T), vtok=np.ascontiguousarray(vT.T),
                         beta=np.ascontiguousarray(bt.reshape(NCH, 128).T), g=np.ascontiguousarray(gg.reshape(NCH, 128).T), cst=cst))
    return maps


def l2d_host_outputs(res):
    out = np.zeros((2, 256, SEQ), np.float32)
    for u in range(NCORES):
        d, h = u // 4, u % 4
        y = res[u]['yC']
        if d == 1:
            y = y[::-1]
        out[d, h * 64:(h + 1) * 64] = y.T
    return out


CAST_TF = 4096
CAST_NT = 16


def build_l0():
    nc = _new_nc()
    src = nc.dram_tensor("src", [128, CAST_NT * CAST_TF], F32, kind="ExternalInput").ap()
    dst = nc.dram_tensor("dst", [128, CAST_NT * CAST_TF], BF16, kind="ExternalOutput").ap()
    P = Prog(nc)
    a = [P.sb("ca%d" % i, [128, CAST_TF], F32) for i in range(3)]
    b = [P.sb("cb%d" % i, [128, CAST_TF], BF16) for i in range(3)]
    for t in range(CAST_NT):
        s = t % 3
        sl = slice(t * CAST_TF, (t + 1) * CAST_TF)
        P.dma('sync', lambda e, s=s, sl=sl: e.dma_start(out=a[s][:], in_=src[:, sl]), 'ca%d' % s, writes=['ca%d' % s])
        eng = ('vector', 'gpsimd', 'scalar')[t % 3]
        if eng == 'scalar':
            P.op(eng, lambda e, s=s: e.copy(out=b[s][:], in_=a[s][:]), reads=['ca%d' % s], writes=['cb%d' % s])
        else:
            P.op(eng, lambda e, s=s: e.tensor_copy(out=b[s][:], in_=a[s][:]), reads=['ca%d' % s], writes=['cb%d' % s])
        P.dma('gpsimd', lambda e, s=s, sl=sl: e.dma_start(out=dst[:, sl], in_=b[s][:]), 'cb%d' % s, reads=['cb%d' % s], final=True)
    P.emit()
    return nc


def cast_weights_on_device(arrs, nc0, launch):
    names = list(arrs.keys())
    flat = np.concatenate([np.ascontiguousarray(arrs[n]).reshape(-1) for n in names])
    per_launch = NCORES * 128 * CAST_NT * CAST_TF
    nl = (flat.size + per_launch - 1) // per_launch
    buf = np.zeros(nl * per_launch, np.float32)
    buf[:flat.size] = flat
    buf = buf.reshape(nl, NCORES, 128, CAST_NT * CAST_TF)
    out = np.zeros(buf.shape, NPBF)
    for i in range(nl):
        res = launch(nc0, [dict(src=buf[i, c]) for c in range(NCORES)])
        for c in range(NCORES):
            out[i, c] = res[c]['dst']
    out = out.reshape(-1)
    ret = {}
    off = 0
    for n in names:
        sz = arrs[n].size
        ret[n] = out[off:off + sz].reshape(arrs[n].shape)
        off += sz
    return ret


NPASS = TC // 512
PC_ANW, PC_CNW, PC_SINK, PC_LN1W, PC_LN1B, PC_LN2W, PC_LN2B = 0, 2, 3, 4, 12, 20, 28
CS_BONES, CS_ONES, CS_IDENT, CS_E4, CS_E8, CS_EE = 0, 128, 256, 384, 640, 1152
CS_N = 1152 + 1024


def build_l3(moe, dbg=False):
    nc = _new_nc()
    T = TC
    E = 8 if moe else 1
    J = 11 if moe else 22
    xT = nc.dram_tensor("xT", [D, T], F32, kind="ExternalInput").ap()
    hA = nc.dram_tensor("hA", [2, 256, T], F32, kind="ExternalInput").ap()
    oCd = nc.dram_tensor("oCd", [2, 256, T], F32, kind="ExternalInput").ap()
    accB = nc.dram_tensor("accB", [3, 256, T], F32, kind="ExternalInput").ap()
    lB = nc.dram_tensor("lB", [3, 4, T], F32, kind="ExternalInput").ap()
    accD = nc.dram_tensor("accD", [512, T], F32, kind="ExternalInput").ap()
    lD = nc.dram_tensor("lD", [8, T], F32, kind="ExternalInput").ap()
    wg = nc.dram_tensor("wg", [9, 128, 8 * 512], BF16, kind="ExternalInput").ap()
    wbr = nc.dram_tensor("wbr", [128, 10 * 1024], BF16, kind="ExternalInput").ap()
    wout = nc.dram_tensor("wout", [128, 8 * 1024], BF16, kind="ExternalInput").ap()
    pcol = nc.dram_tensor("pcol", [128, 40], F32, kind="ExternalInput").ap()
    cst3 = nc.dram_tensor("cst3", [128, CS_N], F32, kind="ExternalInput").ap()
    w1d = nc.dram_tensor("w1p", [E, J, 128, 8 * 128], BF16, kind="ExternalInput").ap()
    w3d = nc.dram_tensor("w3p", [E, J, 128, 8 * 128], BF16, kind="ExternalInput").ap()
    w2d = nc.dram_tensor("w2p", [E, 8, 128, J * 128], BF16, kind="ExternalInput").ap()
    if moe:
        rtd = nc.dram_tensor("router", [128, 8 * 8], F32, kind="ExternalInput").ap()
    xo = nc.dram_tensor("xo", [D, T], F32, kind="ExternalOutput").ap()
    if dbg:
        dbg_g = nc.dram_tensor("dbg_g", [8, T], F32, kind="ExternalOutput").ap()
        dbg_x = nc.dram_tensor("dbg_x", [D, T], F32, kind="ExternalOutput").ap()
        dbg_l = nc.dram_tensor("dbg_l", [NPASS, 128, 64], F32, kind="ExternalOutput").ap()

    P = Prog(nc)
    xr = P.sb("xr", [128, 8, 512], F32)
    xb = P.sb("xb", [128, 8, 512], BF16)
    yb = P.sb("yb", [128, 10, 512], BF16)
    mg = P.sb("mg", [128, 8, 512], BF16)
    hT = P.sb("hT", [128, 22, 512], BF16)
    wbr_t = P.sb("wbr_t", [128, 10, 1024], BF16)
    wout_t = P.sb("wout_t", [128, 8, 1024], BF16)
    wgr = [P.sb("wgr%d" % i, [128, 8, 512], BF16) for i in range(2)]
    w1r = [P.sb("w1r%d" % i, [128, 8, 128], BF16) for i in range(4)]
    w3r = [P.sb("w3r%d" % i, [128, 8, 128], BF16) for i in range(4)]
    w2r = [P.sb("w2r%d" % i, [128, J, 128], BF16) for i in range(3)]
    pc = P.sb("pc", [128, 40], F32)
    cs = P.sb("cs", [128, CS_N], F32)
    esink = P.sb("esink", [8, 1], F32)
    eps5 = P.sb("eps5", [128, 1], F32)
    eps6 = P.sb("eps6", [128, 1], F32)
    class _TT(dict):
        ALIAS = {'ts0': 'td0', 'ts1': 'td1', 'tsq0': 'tm0', 'tsq1': 'tm1', 'tn0': 'tacc0', 'tn1': 'tacc1'}
    tt = _TT({n: P.sb(n, [128, 512], F32) for n in ('ta', 'tb', 'tc', 'td0', 'td1', 'tm0', 'tm1', 'tacc0', 'tacc1', 'tmean', 'tmsq', 'trstd')})
    l4 = [P.sb("l4_%d" % i, [4, 512], F32) for i in range(3)]
    l8 = P.sb("l8", [8, 512], F32)
    if moe:
        rt = P.sb("rt", [128, 8, 8], F32)
        lg = P.sb("lg", [128, 4, 8], F32)
        lg2 = P.sb("lg2", [128, 8], F32)
        gsel = P.sb("gsel", [128, 8], F32)
        m1 = P.sb("m1", [128, 4], F32)
        gts = P.sb("gts", [128, 4, 8], F32)
        gT = P.sb("gT", [8, 512], F32)
        gb = P.sb("gb", [128, 8, 512], F32)
    banks = [P.ps("bk%d" % i) for i in range(8)]
    st = dict(b=0, w1=0, w3=0, w2=0, wg=0, td=0, tm=0, tacc=0, tsq=0, tn=0, ts=0)

    def nb():
        i = st['b']
        st['b'] = (i + 1) % 8
        return banks[i], 'bk%d' % i

    def rot(name, n):
        i = st[name]
        st[name] = (i + 1) % n
        return i

    BONES = cs[:, CS_BONES:CS_BONES + 128]
    ONES = cs[:, CS_ONES:CS_ONES + 128]
    IDN = cs[:, CS_IDENT:CS_IDENT + 128]

    P.dma('sync', lambda e: e.dma_start(out=pc[:], in_=pcol), 'pc', writes=['pc'])
    P.dma('sync', lambda e: e.dma_start(out=cs[:], in_=cst3), 'cs', writes=['cs'])
    P.dma('sync', lambda e: e.dma_start(out=wbr_t[:].rearrange("p k c -> p (k c)"), in_=wbr), 'wbr', writes=['wbr'])
    P.dma('sync', lambda e: e.dma_start(out=wout_t[:].rearrange("p k c -> p (k c)"), in_=wout), 'wout', writes=['wout'])
    if moe:
        P.dma('sync', lambda e: e.dma_start(out=rt[:].rearrange("p k c -> p (k c)"), in_=rtd), 'rt', writes=['rt'])
    P.op('vector', lambda e: e.memset(eps5[:], 1e-5), writes=['eps5'])
    P.op('vector', lambda e: e.memset(eps6[:], 1e-6), writes=['eps6'])
    P.op('scalar', lambda e: e.activation(out=esink[:], in_=pc[0:8, PC_SINK:PC_SINK + 1], func=AF.Exp), reads=['pc'], writes=['esink'])

    def proj(wt, wkey, col, rhs_t, rhs_key):
        ps, pk = nb()
        for k in range(8):
            P.mm(ps[:, 0:512], wt[:, k, col:col + 128], rhs_t[:, k, :], k == 0, k == 7, reads=[wkey, rhs_key], writes=[pk])
        return ps, pk

    def load_wg(gi):
        s = rot('wg', 2)
        P.dma('sync', lambda e: e.dma_start(out=wgr[s][:].rearrange("p k c -> p (k c)"), in_=wg[gi]), 'wgr%d' % s, writes=['wgr%d' % s])
        return wgr[s], 'wgr%d' % s

    def rsqrt_from_psum(ps, pk, scale, epst, ekey, out_t, okey):
        P.op('scalar', lambda e: e.activation(out=out_t[:], in_=ps[:, 0:512], func=AF.Sqrt, bias=epst[:, 0:1], scale=scale),
             reads=[pk, ekey], writes=[okey])
        P.op('vector', lambda e: e.reciprocal(out=out_t[:], in_=out_t[:]), reads=[okey], writes=[okey])

    def layer_norm(wc, bc, make_bf16, out_dram_t0=None):
        s1, k1 = nb()
        for k in range(8):
            P.mm(s1[:, 0:512], ONES, xr[:, k, :], k == 0, k == 7, reads=['cs', 'xr%d' % k], writes=[k1])
        s2, k2 = nb()
        for k in range(8):
            i = rot('tsq', 2)
            tq = tt['tm%d' % i]
            P.op('scalar', lambda e, tq=tq, k=k: e.activation(out=tq[:], in_=xr[:, k, :], func=AF.Square), reads=['xr%d' % k], writes=['tm%d' % i])
            P.mm(s2[:, 0:512], ONES, tq[:], k == 0, k == 7, reads=['cs', 'tm%d' % i], writes=[k2])
        tmean, tmsq, trstd = tt['tmean'], tt['tmsq'], tt['trstd']
        P.op('vector', lambda e: e.tensor_scalar(out=tmean[:], in0=s1[:, 0:512], scalar1=1.0 / D, scalar2=None, op0=ALU.mult), reads=[k1], writes=['tmean'])
        P.op('gpsimd', lambda e: e.tensor_tensor(out=tmsq[:], in0=tmean[:], in1=tmean[:], op=ALU.mult), reads=['tmean'], writes=['tmsq'])
        P.op('vector', lambda e: e.scalar_tensor_tensor(out=trstd[:], in0=s2[:, 0:512], scalar=1.0 / D, in1=tmsq[:], op0=ALU.mult, op1=ALU.subtract),
             reads=[k2, 'tmsq'], writes=['trstd'])
        P.op('scalar', lambda e: e.activation(out=trstd[:], in_=trstd[:], func=AF.Sqrt, bias=eps5[:, 0:1], scale=1.0), reads=['trstd', 'eps5'], writes=['trstd'])
        P.op('vector', lambda e: e.reciprocal(out=trstd[:], in_=trstd[:]), reads=['trstd'], writes=['trstd'])
        for k in range(8):
            i = rot('tn', 2)
            tn = tt['tacc%d' % i]
            P.op('vector', lambda e, tn=tn, k=k: e.tensor_tensor(out=tn[:], in0=xr[:, k, :], in1=tmean[:], op=ALU.subtract),
                 reads=['xr%d' % k, 'tmean'], writes=['tacc%d' % i])
            P.op('vector', lambda e, tn=tn: e.tensor_tensor(out=tn[:], in0=tn[:], in1=trstd[:], op=ALU.mult), reads=['tacc%d' % i, 'trstd'], writes=['tacc%d' % i])
            P.op('scalar', lambda e, tn=tn, k=k: e.activation(out=xr[:, k, :], in_=tn[:], func=AF.Identity, bias=pc[:, bc + k:bc + k + 1], scale=pc[:, wc + k:wc + k + 1]),
                 reads=['tacc%d' % i, 'pc'], writes=['xr%d' % k])
            if make_bf16:
                P.op('gpsimd', lambda e, k=k: e.tensor_copy(out=xb[:, k, :], in_=xr[:, k, :]), reads=['xr%d' % k], writes=['xb'])
            if out_dram_t0 is not None:
                t0 = out_dram_t0
                P.dma('gpsimd', lambda e, k=k, t0=t0: e.dma_start(out=xo[k * 128:(k + 1) * 128, t0:t0 + 512], in_=xr[:, k, :]),
                      'xr%d' % k, reads=['xr%d' % k], final=True)

    xrk = ['xr%d' % k for k in range(8)]
    for ps_i in range(NPASS):
        t0 = ps_i * 512
        tsl = slice(t0, t0 + 512)
        for k in range(8):
            P.dma('sync', lambda e, k=k, tsl=tsl: e.dma_start(out=xr[:, k, :], in_=xT[k * 128:(k + 1) * 128, tsl]), 'xr%d' % k, writes=['xr%d' % k])
            P.op(('vector', 'gpsimd')[k % 2], lambda e, k=k: e.tensor_copy(out=xb[:, k, :], in_=xr[:, k, :]), reads=['xr%d' % k], writes=['xb'])
        ta, tb, tc = tt['ta'], tt['tb'], tt['tc']
        wg8, wg8k = load_wg(8)
        for c in range(2):
            rs = slice(c * 128, (c + 1) * 128)
            P.dma('sync', lambda e, rs=rs, tsl=tsl: e.dma_start(out=ta[:], in_=hA[0, rs, tsl]), 'ta', writes=['ta'])
            P.dma('sync', lambda e, rs=rs, tsl=tsl: e.dma_start(out=tb[:], in_=hA[1, rs, tsl]), 'tb', writes=['tb'])
            P.op('vector', lambda e: e.tensor_tensor(out=ta[:], in0=ta[:], in1=tb[:], op=ALU.add), reads=['ta', 'tb'], writes=['ta'])
            ps, pk = nb()
            P.mm(ps[:, 0:512], BONES, ta[:], True, True, reads=['cs', 'ta'], writes=[pk])
            P.op('vector', lambda e, ps=ps: e.scalar_tensor_tensor(out=tb[:], in0=ps[:, 0:512], scalar=-1.0 / 64, in1=ta[:], op0=ALU.mult, op1=ALU.add),
                 reads=[pk, 'ta'], writes=['tb'])
            P.op('gpsimd', lambda e: e.tensor_tensor(out=tc[:], in0=tb[:], in1=tb[:], op=ALU.mult), reads=['tb'], writes=['tc'])
            ps2, pk2 = nb()
            P.mm(ps2[:, 0:512], BONES, tc[:], True, True, reads=['cs', 'tc'], writes=[pk2])
            rsqrt_from_psum(ps2, pk2, 1.0 / 64, eps5, 'eps5', tc, 'tc')
            P.op('gpsimd', lambda e: e.tensor_tensor(out=tb[:], in0=tb[:], in1=tc[:], op=ALU.mult), reads=['tb', 'tc'], writes=['tb'])
            psg, pkg = proj(wg8, wg8k, c * 128, xb, 'xb')
            i = rot('td', 2)
            td = tt['td%d' % i]
            P.op('scalar', lambda e, td=td, psg=psg: e.activation(out=td[:], in_=psg[:, 0:512], func=AF.Sigmoid), reads=[pkg], writes=['td%d' % i])
            P.op('vector', lambda e, td=td, c=c: e.scalar_tensor_tensor(out=yb[:, c, :], in0=tb[:], scalar=pc[:, PC_ANW + c:PC_ANW + c + 1], in1=td[:],
                                                                        op0=ALU.mult, op1=ALU.mult), reads=['tb', 'pc', 'td%d' % i], writes=['yb'])
        for c in range(2):
            rs = slice(c * 128, (c + 1) * 128)
            P.dma('sync', lambda e, rs=rs, tsl=tsl: e.dma_start(out=ta[:], in_=oCd[0, rs, tsl]), 'ta', writes=['ta'])
            P.dma('sync', lambda e, rs=rs, tsl=tsl: e.dma_start(out=tb[:], in_=oCd[1, rs, tsl]), 'tb', writes=['tb'])
            P.op('vector', lambda e: e.tensor_tensor(out=ta[:], in0=ta[:], in1=tb[:], op=ALU.add), reads=['ta', 'tb'], writes=['ta'])
            P.op('gpsimd', lambda e: e.tensor_tensor(out=tc[:], in0=ta[:], in1=ta[:], op=ALU.mult), reads=['ta'], writes=['tc'])
            ps2, pk2 = nb()
            P.mm(ps2[:, 0:512], BONES, tc[:], True, True, reads=['cs', 'tc'], writes=[pk2])
            rsqrt_from_psum(ps2, pk2, 1.0 / 64, eps6, 'eps6', tc, 'tc')
            P.op('gpsimd', lambda e: e.tensor_tensor(out=tb[:], in0=ta[:], in1=tc[:], op=ALU.mult), reads=['ta', 'tc', 'tb'], writes=['tb'])
            psg, pkg = proj(wg8, wg8k, 256 + c * 128, xb, 'xb')
            i = rot('td', 2)
            td = tt['td%d' % i]
            P.op('scalar', lambda e, td=td, psg=psg: e.activation(out=td[:], in_=psg[:, 0:512], func=AF.Silu), reads=[pkg], writes=['td%d' % i])
            P.op('vector', lambda e, td=td, c=c: e.scalar_tensor_tensor(out=yb[:, 4 + c, :], in0=tb[:], scalar=pc[:, PC_CNW:PC_CNW + 1], in1=td[:],
                                                                        op0=ALU.mult, op1=ALU.mult), reads=['tb', 'pc', 'td%d' % i], writes=['yb'])
        for gi in range(3):
            P.dma('sync', lambda e, gi=gi, tsl=tsl: e.dma_start(out=l4[gi][:], in_=lB[gi, :, tsl]), 'l4_%d' % gi, writes=['l4_%d' % gi])
        P.op('vector', lambda e: e.tensor_tensor(out=l4[0][:], in0=l4[0][:], in1=l4[1][:], op=ALU.add), reads=['l4_0', 'l4_1'], writes=['l4_0'])
        P.op('vector', lambda e: e.tensor_tensor(out=l4[0][:], in0=l4[0][:], in1=l4[2][:], op=ALU.add), reads=['l4_0', 'l4_2'], writes=['l4_0'])
        P.op('vector', lambda e: e.reciprocal(out=l4[0][:], in_=l4[0][:]), reads=['l4_0'], writes=['l4_0'])
        for c in range(2):
            rs = slice(c * 128, (c + 1) * 128)
            P.dma('sync', lambda e, rs=rs, tsl=tsl: e.dma_start(out=ta[:], in_=accB[0, rs, tsl]), 'ta', writes=['ta'])
            P.dma('sync', lambda e, rs=rs, tsl=tsl: e.dma_start(out=tb[:], in_=accB[1, rs, tsl]), 'tb', writes=['tb'])
            P.dma('sync', lambda e, rs=rs, tsl=tsl: e.dma_start(out=tc[:], in_=accB[2, rs, tsl]), 'tc', writes=['tc'])
            P.op('vector', lambda e: e.tensor_tensor(out=ta[:], in0=ta[:], in1=tb[:], op=ALU.add), reads=['ta', 'tb'], writes=['ta'])
            P.op('gpsimd', lambda e: e.tensor_tensor(out=ta[:], in0=ta[:], in1=tc[:], op=ALU.add), reads=['ta', 'tc'], writes=['ta'])
            ps, pk = nb()
            P.mm(ps[:, 0:512], cs[0:4, CS_E4 + c * 128:CS_E4 + (c + 1) * 128], l4[0][:], True, True, reads=['cs', 'l4_0'], writes=[pk])
            P.op('vector', lambda e, ps=ps, c=c: e.tensor_tensor(out=yb[:, 2 + c, :], in0=ta[:], in1=ps[:, 0:512], op=ALU.mult), reads=['ta', pk], writes=['yb'])
        P.dma('sync', lambda e, tsl=tsl: e.dma_start(out=l8[:], in_=lD[:, tsl]), 'l8', writes=['l8'])
        P.op('vector', lambda e: e.tensor_scalar(out=l8[:], in0=l8[:], scalar1=esink[:, 0:1], scalar2=None, op0=ALU.add), reads=['l8', 'esink'], writes=['l8'])
        P.op('vector', lambda e: e.reciprocal(out=l8[:], in_=l8[:]), reads=['l8'], writes=['l8'])
        for c in range(4):
            rs = slice(c * 128, (c + 1) * 128)
            P.dma('sync', lambda e, rs=rs, tsl=tsl: e.dma_start(out=ta[:], in_=accD[rs, tsl]), 'ta', writes=['ta'])
            ps, pk = nb()
            P.mm(ps[:, 0:512], cs[0:8, CS_E8 + c * 128:CS_E8 + (c + 1) * 128], l8[:], True, True, reads=['cs', 'l8'], writes=[pk])
            P.op('vector', lambda e, ps=ps, c=c: e.tensor_tensor(out=yb[:, 6 + c, :], in0=ta[:], in1=ps[:, 0:512], op=ALU.mult), reads=['ta', pk], writes=['yb'])
        BR = ((0, 1), (2, 3), (4, 5), (6, 7, 8, 9))
        for m in range(8):
            wgm, wgk = load_wg(m)
            ai = rot('tacc', 2)
            tacc = tt['tacc%d' % ai]
            for b in range(4):
                psg, pkg = proj(wgm, wgk, b * 128, xb, 'xb')
                i = rot('td', 2)
                td = tt['td%d' % i]
                P.op('scalar', lambda e, td=td, psg=psg: e.activation(out=td[:], in_=psg[:, 0:512], func=AF.Sigmoid), reads=[pkg], writes=['td%d' % i])
                psb, pkb = nb()
                for kk, kc in enumerate(BR[b]):
                    P.mm(psb[:, 0:512], wbr_t[:, kc, m * 128:(m + 1) * 128], yb[:, kc, :], kk == 0, kk == len(BR[b]) - 1, reads=['wbr', 'yb'], writes=[pkb])
                if b == 0:
                    P.op('vector', lambda e, td=td, psb=psb, tacc=tacc: e.tensor_tensor(out=tacc[:], in0=td[:], in1=psb[:, 0:512], op=ALU.mult),
                         reads=['td%d' % i, pkb], writes=['tacc%d' % ai])
                else:
                    j = rot('tm', 2)
                    tm = tt['tm%d' % j]
                    P.op('vector', lambda e, td=td, psb=psb, tm=tm: e.tensor_tensor(out=tm[:], in0=td[:], in1=psb[:, 0:512], op=ALU.mult),
                         reads=['td%d' % i, pkb], writes=['tm%d' % j])
                    if b < 3:
                        P.op('vector', lambda e, tm=tm, tacc=tacc: e.tensor_tensor(out=tacc[:], in0=tacc[:], in1=tm[:], op=ALU.add),
                             reads=['tacc%d' % ai, 'tm%d' % j], writes=['tacc%d' % ai])
                    else:
                        P.op('vector', lambda e, tm=tm, tacc=tacc, m=m: e.tensor_tensor(out=mg[:, m, :], in0=tacc[:], in1=tm[:], op=ALU.add),
                             reads=['tacc%d' % ai, 'tm%d' % j], writes=['mg'])
        for m in range(8):
            ps, pk = nb()
            for k in range(8):
                P.mm(ps[:, 0:512], wout_t[:, k, m * 128:(m + 1) * 128], mg[:, k, :], k == 0, k == 7, reads=['wout', 'mg'], writes=[pk])
            P.op('vector', lambda e, ps=ps, m=m: e.scalar_tensor_tensor(out=xr[:, m, :], in0=xr[:, m, :], scalar=ALPHA, in1=ps[:, 0:512], op0=ALU.mult, op1=ALU.add),
                 reads=['xr%d' % m, pk], writes=['xr%d' % m])
        layer_norm(PC_LN1W, PC_LN1B, True)
        if moe:
            P.strict = True
            for sbt in range(4):
                ps, pk = nb()
                for k in range(8):
                    P.mm(ps[:, 0:8], xr[:, k, sbt * 128:(sbt + 1) * 128], rt[:, k, :], k == 0, k == 7, reads=['xr%d' % k, 'rt'], writes=[pk])
                P.op('vector', lambda e, ps=ps, sbt=sbt: e.tensor_copy(out=lg[:, sbt, :], in_=ps[:, 0:8]), reads=[pk], writes=['lg'])
                P.op('vector', lambda e, sbt=sbt: e.tensor_reduce(out=m1[:, 0:1], in_=lg[:, sbt, :], axis=AX.X, op=ALU.max), reads=['lg'], writes=['m1'])
                P.op('vector', lambda e, sbt=sbt: e.tensor_scalar(out=gsel[:], in0=lg[:, sbt, :], scalar1=m1[:, 0:1], scalar2=None, op0=ALU.is_equal),
                     reads=['lg', 'm1'], writes=['gsel'])
                P.op('vector', lambda e, sbt=sbt: e.scalar_tensor_tensor(out=lg2[:], in0=gsel[:], scalar=-1.0e30, in1=lg[:, sbt, :], op0=ALU.mult, op1=ALU.add),
                     reads=['gsel', 'lg'], writes=['lg2'])
                P.op('vector', lambda e: e.tensor_reduce(out=m1[:, 1:2], in_=lg2[:], axis=AX.X, op=ALU.max), reads=['lg2'], writes=['m1'])
                P.op('vector', lambda e, sbt=sbt: e.tensor_scalar(out=gsel[:], in0=lg[:, sbt, :], scalar1=m1[:, 1:2], scalar2=None, op0=ALU.is_ge),
                     reads=['lg', 'm1'], writes=['gsel'])
                P.op('vector', lambda e: e.tensor_scalar(out=m1[:, 2:3], in0=m1[:, 0:1], scalar1=-1.0, scalar2=None, op0=ALU.mult), reads=['m1'], writes=['m1'])
                P.op('scalar', lambda e, sbt=sbt: e.activation(out=lg2[:], in_=lg[:, sbt, :], func=AF.Exp, bias=m1[:, 2:3], scale=1.0), reads=['lg', 'm1'], writes=['lg2'])
                P.op('vector', lambda e: e.tensor_tensor(out=lg2[:], in0=lg2[:], in1=gsel[:], op=ALU.mult), reads=['lg2', 'gsel'], writes=['lg2'])
                P.op('vector', lambda e: e.tensor_reduce(out=m1[:, 3:4], in_=lg2[:], axis=AX.X, op=ALU.add), reads=['lg2'], writes=['m1'])
                P.op('vector', lambda e: e.reciprocal(out=m1[:, 3:4], in_=m1[:, 3:4]), reads=['m1'], writes=['m1'])
                P.op('vector', lambda e, sbt=sbt: e.tensor_scalar(out=gts[:, sbt, :], in0=lg2[:], scalar1=m1[:, 3:4], scalar2=None, op0=ALU.mult),
                     reads=['lg2', 'm1'], writes=['gts'])
                ps2, pk2 = nb()
                P.mm(ps2[0:8, 0:128], gts[:, sbt, :], IDN, True, True, reads=['gts', 'cs'], writes=[pk2])
                P.op('vector', lambda e, ps2=ps2, sbt=sbt: e.tensor_copy(out=gT[:, sbt * 128:(sbt + 1) * 128], in_=ps2[0:8, 0:128]), reads=[pk2], writes=['gT'])
            P.strict = False
            if dbg:
                P.dma('gpsimd', lambda e, t0=t0: e.dma_start(out=dbg_g[:, t0:t0 + 512], in_=gT[:]), 'gT', reads=['gT'], final=True)
                P.dma('gpsimd', lambda e, ps_i=ps_i: e.dma_start(out=dbg_l[ps_i, :, 0:32], in_=lg[:].rearrange("p a b -> p (a b)")), 'lg', reads=['lg'], final=True)
                P.dma('gpsimd', lambda e, ps_i=ps_i: e.dma_start(out=dbg_l[ps_i, :, 32:64], in_=gts[:].rearrange("p a b -> p (a b)")), 'gts', reads=['gts'], final=True)
                for k in range(8):
                    P.dma('gpsimd', lambda e, t0=t0, k=k: e.dma_start(out=dbg_x[k * 128:(k + 1) * 128, t0:t0 + 512], in_=xr[:, k, :]), 'xr%d' % k, reads=['xr%d' % k], final=True)
            for ex in range(8):
                ps, pk = nb()
                P.mm(ps[:, 0:512], cs[0:8, CS_EE + ex * 128:CS_EE + (ex + 1) * 128], gT[:], True, True, reads=['cs', 'gT'], writes=[pk])
                P.op('scalar', lambda e, ps=ps, ex=ex: e.copy(out=gb[:, ex, :], in_=ps[:, 0:512]), reads=[pk], writes=['gb%d' % ex])
        for ex in range(E):
            hoff = (ex % 2) * 11 if moe else 0
            hkey = 'hT%d' % (ex % 2)
            for jc in range(J):
                i1 = rot('w1', 4)
                P.dma('sync', lambda e, i1=i1, ex=ex, jc=jc: e.dma_start(out=w1r[i1][:].rearrange("p k c -> p (k c)"), in_=w1d[ex, jc]), 'w1r%d' % i1, writes=['w1r%d' % i1])
                i3 = rot('w3', 4)
                P.dma('sync', lambda e, i3=i3, ex=ex, jc=jc: e.dma_start(out=w3r[i3][:].rearrange("p k c -> p (k c)"), in_=w3d[ex, jc]), 'w3r%d' % i3, writes=['w3r%d' % i3])
                p1, k1 = proj(w1r[i1], 'w1r%d' % i1, 0, xb, 'xb')
                p3, k3 = proj(w3r[i3], 'w3r%d' % i3, 0, xb, 'xb')
                si = rot('ts', 2)
                ts_ = tt['td%d' % si]
                P.op('scalar', lambda e, ts_=ts_, p1=p1: e.activation(out=ts_[:], in_=p1[:, 0:512], func=AF.Silu), reads=[k1], writes=['td%d' % si])
                if moe:
                    P.op('vector', lambda e, ts_=ts_, p3=p3: e.tensor_tensor(out=ts_[:], in0=ts_[:], in1=p3[:, 0:512], op=ALU.mult), reads=['td%d' % si, k3], writes=['td%d' % si])
                    P.op('vector', lambda e, ts_=ts_, ex=ex, jc=jc, hoff=hoff: e.tensor_tensor(out=hT[:, hoff + jc, :], in0=ts_[:], in1=gb[:, ex, :], op=ALU.mult),
                         reads=['td%d' % si, 'gb%d' % ex], writes=[hkey])
                else:
                    P.op('vector', lambda e, ts_=ts_, p3=p3, jc=jc: e.tensor_tensor(out=hT[:, jc, :], in0=ts_[:], in1=p3[:, 0:512], op=ALU.mult),
                         reads=['td%d' % si, k3], writes=[hkey])
            for m in range(8):
                i2 = rot('w2', 3)
                P.dma('sync', lambda e, i2=i2, ex=ex, m=m: e.dma_start(out=w2r[i2][:].rearrange("p k c -> p (k c)"), in_=w2d[ex, m]), 'w2r%d' % i2, writes=['w2r%d' % i2])
                ps, pk = nb()
                for jc in range(J):
                    P.mm(ps[:, 0:512], w2r[i2][:, jc, :], hT[:, hoff + jc, :], jc == 0, jc == J - 1, reads=['w2r%d' % i2, hkey], writes=[pk])
                if not moe:
                    P.op('vector', lambda e, ps=ps, m=m: e.scalar_tensor_tensor(out=xr[:, m, :], in0=xr[:, m, :], scalar=ALPHA, in1=ps[:, 0:512], op0=ALU.mult, op1=ALU.add),
                         reads=['xr%d' % m, pk], writes=['xr%d' % m])
                elif ex == 0:
                    P.op('vector', lambda e, ps=ps, m=m: e.scalar_tensor_tensor(out=xr[:, m, :], in0=xr[:, m, :], scalar=ALPHA, in1=ps[:, 0:512], op0=ALU.mult, op1=ALU.add),
                         reads=['xr%d' % m, pk], writes=['xr%d' % m])
                else:
                    P.op('vector', lambda e, ps=ps, m=m: e.tensor_tensor(out=xr[:, m, :], in0=xr[:, m, :], in1=ps[:, 0:512], op=ALU.add),
                         reads=[pk, 'xr%d' % m], writes=['xr%d' % m])
        layer_norm(PC_LN2W, PC_LN2B, False, out_dram_t0=t0)
    P.emit()
    return nc


def l3_consts():
    c = np.zeros((128, CS_N), np.float32)
    i = np.arange(128)
    c[:, CS_BONES:CS_BONES + 128] = (i[:, None] // 64 == i[None, :] // 64)
    c[:, CS_ONES:CS_ONES + 128] = 1.0
    c[:, CS_IDENT:CS_IDENT + 128] = np.eye(128)
    for ch in range(2):
        for h in range(4):
            c[h, CS_E4 + ch * 128:CS_E4 + (ch + 1) * 128] = ((ch * 2 + i // 64) == h)
    for ch in range(4):
        for h in range(8):
            c[h, CS_E8 + ch * 128:CS_E8 + (ch + 1) * 128] = ((ch * 2 + i // 64) == h)
    for ex in range(8):
        c[ex, CS_EE + ex * 128:CS_EE + (ex + 1) * 128] = 1.0
    return c


def l3_gate_columns():
    cols = []
    for m in range(8):
        for b in range(4):
            cols.append(np.arange(b * 1024 + m * 128, b * 1024 + (m + 1) * 128))
    cols.append(_cols('a_o'))
    cols.append(_cols('c_gate'))
    return np.concatenate(cols)


def _pack_kxc(w, cw):
    K_, C_ = w.shape
    g = C_ // cw
    return np.ascontiguousarray(w.reshape(K_ // 128, 128, g, cw).transpose(2, 1, 0, 3).reshape(g, 128, (K_ // 128) * cw))


def _pack_w2(w):
    J_ = w.shape[0] // 128
    return np.ascontiguousarray(w.reshape(J_, 128, 8, 128).transpose(2, 1, 0, 3).reshape(8, 128, J_ * 128))


def l3_pcol(inp, l):
    p = np.zeros((128, 40), np.float32)
    p[:, PC_ANW:PC_ANW + 2] = inp['a_norm_w'][l].reshape(2, 128).T
    p[:, PC_CNW] = np.tile(inp['c_norm_w'][l], 2)
    p[0:8, PC_SINK] = inp['d_sink'][l]
    p[:, PC_LN1W:PC_LN1W + 8] = inp['ln1_w'][l].reshape(8, 128).T
    p[:, PC_LN1B:PC_LN1B + 8] = inp['ln1_b'][l].reshape(8, 128).T
    p[:, PC_LN2W:PC_LN2W + 8] = inp['ln2_w'][l].reshape(8, 128).T
    p[:, PC_LN2B:PC_LN2B + 8] = inp['ln2_b'][l].reshape(8, 128).T
    return p


def l3_weight_maps(wb, inp, l):
    d = {}
    d['wg'] = _pack_kxc(wb['wgsel'][l], 512)
    wbr = np.concatenate([wb['w_branch_a'][l], wb['w_branch_b'][l], wb['w_branch_c'][l], wb['w_branch_d'][l]], 0)
    d['wbr'] = np.ascontiguousarray(wbr.reshape(10, 128, 1024).transpose(1, 0, 2).reshape(128, 10 * 1024))
    d['wout'] = np.ascontiguousarray(wb['w_out'][l].reshape(8, 128, 1024).transpose(1, 0, 2).reshape(128, 8 * 1024))
    d['pcol'] = l3_pcol(inp, l)
    j = l // 2
    if l % 2 == 0:
        d['w1p'] = _pack_kxc(wb['ffn_w1'][j], 128)[None]
        d['w3p'] = _pack_kxc(wb['ffn_w3'][j], 128)[None]
        d['w2p'] = _pack_w2(wb['ffn_w2'][j])[None]
    else:
        d['w1p'] = np.stack([_pack_kxc(wb['moe_w1'][j, e], 128) for e in range(8)])
        d['w3p'] = np.stack([_pack_kxc(wb['moe_w3'][j, e], 128) for e in range(8)])
        d['w2p'] = np.stack([_pack_w2(wb['moe_w2'][j, e]) for e in range(8)])
        d['router'] = np.ascontiguousarray(inp['moe_router'][j].reshape(8, 128, 8).transpose(1, 0, 2).reshape(128, 64))
    return d


def l3_host_inputs(xT_full, hM, oD, accB, accD, wmap, cst3):
    accBT = np.ascontiguousarray(accB[:, :, :, :64].reshape(3, SEQ, 256).transpose(0, 2, 1))
    lBT = np.ascontiguousarray(accB[:, :, :, 64].transpose(0, 2, 1))
    accDT = np.ascontiguousarray(accD[:, :, :64].reshape(SEQ, 512).T)
    lDT = np.ascontiguousarray(accD[:, :, 64].T)
    maps = []
    for c in range(NCORES):
        sl = slice(c * TC, (c + 1) * TC)
        m = dict(xT=np.ascontiguousarray(xT_full[:, sl]), hA=np.ascontiguousarray(hM[:, :, sl]), oCd=np.ascontiguousarray(oD[:, :, sl]),
                 accB=np.ascontiguousarray(accBT[:, :, sl]), lB=np.ascontiguousarray(lBT[:, :, sl]),
                 accD=np.ascontiguousarray(accDT[:, sl]), lD=np.ascontiguousarray(lDT[:, sl]), cst3=cst3)
        m.update(wmap)
        maps.append(m)
    return maps


_NC_CACHE = {}


def _get_nc(name):
    if name not in _NC_CACHE:
        _NC_CACHE[name] = {'l0': build_l0, 'l1': build_l1, 'l2a': build_l2a, 'l2m': build_l2m, 'l2d': build_l2d,
                           'l3d': lambda: build_l3(False), 'l3m': lambda: build_l3(True)}[name]()
    return _NC_CACHE[name]


_PROFILE = []


def _launch(nc, maps):
    if _PROFILE:
        r = run_bass_kernel_spmd(nc, maps, core_ids=list(range(NCORES)), trace=True)
        _PROFILE.append(r.exec_time_ns)
        return r.results
    return run_bass_kernel_spmd(nc, maps, core_ids=list(range(NCORES))).results


def run_layer(xT_full, l, inp, consts, wb, cst3):
    r1 = _launch(_get_nc('l1'), l1_host_inputs(xT_full, l, inp, consts))
    oA = np.concatenate([r['oA'] for r in r1], axis=1)
    oB = np.concatenate([r['oB'] for r in r1], axis=1)
    oC = np.concatenate([r['oC'] for r in r1], axis=1)
    oG = np.concatenate([r['oG'] for r in r1], axis=1)
    ra = _launch(_get_nc('l2a'), l2a_host_inputs(oA, oB, inp, l))
    accB, accD = l2a_host_outputs(ra)
    hM = l2m_host_outputs(_launch(_get_nc('l2m'), l2m_host_inputs(oA, oG)))
    oD = l2d_host_outputs(_launch(_get_nc('l2d'), l2d_host_inputs(oC, oG)))
    wmap = l3_weight_maps(wb, inp, l)
    r3 = _launch(_get_nc('l3m' if l % 2 else 'l3d'), l3_host_inputs(xT_full, hM, oD, accB, accD, wmap, cst3))
    return np.concatenate([r['xo'] for r in r3], axis=1)


def cast_all_weights(inp):
    gc = l3_gate_columns()
    arrs = {'wgsel': np.ascontiguousarray(inp['w_in'][:, :, gc])}
    for n in ('w_branch_a', 'w_branch_b', 'w_branch_c', 'w_branch_d', 'w_out', 'ffn_w1', 'ffn_w3', 'ffn_w2', 'moe_w1', 'moe_w3', 'moe_w2'):
        arrs[n] = inp[n]
    return cast_weights_on_device(arrs, _get_nc('l0'), _launch)


def kernel(**inputs):
    inp = {k: np.asarray(v) for k, v in inputs.items()}
    consts = make_consts(inp)
    cst3 = l3_consts()
    wb = cast_all_weights(inp)
    xT = np.ascontiguousarray(inp['x'][0].T)
    for l in range(DEPTH):
        xT = run_layer(xT, l, inp, consts, wb, cst3)
    return np.ascontiguousarray(xT.T)[None].astype(np.float32)
```
